# Optimizing a Trainium2 kernel written in Bass

```python
import math
import jax
import jax.numpy as jnp
from jax import lax
import numpy as np

D_MODEL = 1024
BATCH = 8
SEQ = 2048
DEPTH = 4

N_MIXERS = 3
D_PLE = 256
NORM_EPS = 1e-6
Q_BLOCK = 128
NEG_INF = -1e30

MLA_HEADS = 16
MLA_Q_RANK = 384
MLA_KV_RANK = 256
MLA_NOPE = 64
MLA_ROPE = 32
MLA_V = 64
ROPE_THETA = 10000.0

DIL_PATTERNS = ((128, 1), (512, 4), (2048, 16))
DIL_GROUPS = len(DIL_PATTERNS)
DIL_HEADS = 16
DIL_HEAD_DIM = 64

REL_BUCKETS = 32
REL_MAX_DIST = 2048

FOX_HEADS = 16
FOX_HEAD_DIM = 64

D_FF = -(-8 * D_MODEL // (3 * 256)) * 256

kernel_name = 'hybrid_mla_dilated_fox_trunk'


def rms_norm(x, g):
    xf = x.astype(jnp.float32)
    y = xf * lax.rsqrt(jnp.mean(xf * xf, axis=-1, keepdims=True) + NORM_EPS)
    return (y * g.astype(jnp.float32)).astype(x.dtype)


def apply_rope(x, positions):
    half = x.shape[-1] // 2
    inv = ROPE_THETA ** (-jnp.arange(half, dtype=jnp.float32) / half)
    ang = positions.astype(jnp.float32)[:, :, None, None] * inv
    cos, sin = jnp.cos(ang), jnp.sin(ang)
    xf = x.astype(jnp.float32)
    x1, x2 = xf[..., :half], xf[..., half:]
    return jnp.concatenate([x1 * cos - x2 * sin, x2 * cos + x1 * sin], axis=-1).astype(x.dtype)


def causal_block_attention(q, k, v, scale, log_forget_cumsum=None):
    s_len = q.shape[1]
    outs = []
    for b0 in range(0, s_len, Q_BLOCK):
        b1 = b0 + Q_BLOCK
        qb, kb, vb = q[:, b0:b1], k[:, :b1], v[:, :b1]
        logits = jnp.einsum('bqhd,bkhd->bhqk', qb, kb).astype(jnp.float32) * scale
        if log_forget_cumsum is not None:
            c_q = jnp.transpose(log_forget_cumsum[:, b0:b1], (0, 2, 1))
            c_k = jnp.transpose(log_forget_cumsum[:, :b1], (0, 2, 1))
            logits = logits + (c_q[..., :, None] - c_k[..., None, :])
        q_idx = b0 + jnp.arange(Q_BLOCK)
        k_idx = jnp.arange(b1)
        causal = k_idx[None, :] <= q_idx[:, None]
        logits = jnp.where(causal, logits, NEG_INF)
        probs = jax.nn.softmax(logits, axis=-1).astype(v.dtype)
        outs.append(jnp.einsum('bhqk,bkhd->bqhd', probs, vb))
    return jnp.concatenate(outs, axis=1)


def t5_bucket(dist):
    max_exact = REL_BUCKETS // 2
    n = jnp.maximum(dist.astype(jnp.float32), 1.0)
    large = max_exact + (jnp.log(n / max_exact) / math.log(REL_MAX_DIST / max_exact)
                         * (REL_BUCKETS - max_exact)).astype(jnp.int32)
    large = jnp.minimum(large, REL_BUCKETS - 1)
    return jnp.where(dist < max_exact, dist, large)


def dilated_branch(q, k, v, dilation, span, bias_table):
    b, s_len, h, dh = q.shape
    sub_len = s_len // dilation
    n_blk = -(-sub_len // Q_BLOCK)
    pad_len = n_blk * Q_BLOCK

    def to_blocks(t):
        t = t.reshape(b, sub_len, dilation, h, dh).transpose(0, 2, 1, 3, 4)
        t = t.reshape(b * dilation, sub_len, h, dh)
        t = jnp.pad(t, ((0, 0), (0, pad_len - sub_len), (0, 0), (0, 0)))
        return t.reshape(b * dilation, n_blk, Q_BLOCK, h, dh)

    def with_previous(t):
        prev = jnp.concatenate([jnp.zeros_like(t[:, :1]), t[:, :-1]], axis=1)
        return jnp.concatenate([prev, t], axis=2)

    qs = to_blocks(q)
    kb = with_previous(to_blocks(k))
    vb = with_previous(to_blocks(v))

    logits = jnp.einsum('bnqhd,bnkhd->bnhqk', qs, kb).astype(jnp.float32) * (dh ** -0.5)
    i = jnp.arange(Q_BLOCK)
    j = jnp.arange(2 * Q_BLOCK)
    rel = Q_BLOCK + i[:, None] - j[None, :]
    bucket = t5_bucket(jnp.clip(rel, 0) * dilation)
    bias = jnp.transpose(bias_table[bucket].astype(jnp.float32), (2, 0, 1))
    key_pos = jnp.arange(n_blk)[:, None] * Q_BLOCK - Q_BLOCK + j[None, :]
    valid = ((rel >= 0) & (rel <= span))[None] & (key_pos >= 0)[:, None, :]
    logits = jnp.where(valid[None, :, None], logits + bias[None, None], NEG_INF)
    lse = jax.nn.logsumexp(logits, axis=-1, keepdims=True)
    probs = jnp.exp(logits - lse).astype(v.dtype)
    o = jnp.einsum('bnhqk,bnkhd->bnqhd', probs, vb)

    o = o.reshape(b * dilation, pad_len, h, dh)[:, :sub_len]
    o = o.reshape(b, dilation, sub_len, h, dh).transpose(0, 2, 1, 3, 4).reshape(b, s_len, h, dh)
    lse = jnp.transpose(lse[..., 0], (0, 1, 3, 2)).reshape(b * dilation, pad_len, h)[:, :sub_len]
    lse = lse.reshape(b, dilation, sub_len, h).transpose(0, 2, 1, 3).reshape(b, s_len, h)
    return o, lse


def mla_mixer(h, positions, w_a, q_norm, kv_norm, w_uq, w_ukv, w_o):
    b, s_len, _ = h.shape
    a = h @ w_a
    c_q = rms_norm(a[..., :MLA_Q_RANK], q_norm)
    c_kv = rms_norm(a[..., MLA_Q_RANK:MLA_Q_RANK + MLA_KV_RANK], kv_norm)
    k_rot = apply_rope(a[..., MLA_Q_RANK + MLA_KV_RANK:][:, :, None, :], positions)
    q = (c_q @ w_uq).reshape(b, s_len, MLA_HEADS, MLA_NOPE + MLA_ROPE)
    q = jnp.concatenate([q[..., :MLA_NOPE], apply_rope(q[..., MLA_NOPE:], positions)], axis=-1)
    kv = (c_kv @ w_ukv).reshape(b, s_len, MLA_HEADS, MLA_NOPE + MLA_V)
    k = jnp.concatenate([kv[..., :MLA_NOPE],
                         jnp.broadcast_to(k_rot, (b, s_len, MLA_HEADS, MLA_ROPE))], axis=-1)
    v = kv[..., MLA_NOPE:]
    o = causal_block_attention(q, k, v, (MLA_NOPE + MLA_ROPE) ** -0.5)
    return o.reshape(b, s_len, MLA_HEADS * MLA_V) @ w_o


def dilated_mixer(h, w_qkv, w_o, rel_bias):
    b, s_len, _ = h.shape
    qkv = (h @ w_qkv).reshape(b, s_len, DIL_GROUPS, 3, DIL_HEADS, DIL_HEAD_DIM)
    table = rel_bias.reshape(REL_BUCKETS, DIL_GROUPS, DIL_HEADS)
    outs, lses = [], []
    for g, (window, dilation) in enumerate(DIL_PATTERNS):
        o, lse = dilated_branch(qkv[:, :, g, 0], qkv[:, :, g, 1], qkv[:, :, g, 2],
                                dilation, window // dilation, table[:, g])
        outs.append(o)
        lses.append(lse)
    alpha = jax.nn.softmax(jnp.stack(lses, axis=0), axis=0)
    o = jnp.sum(alpha[..., None] * jnp.stack(outs, axis=0).astype(jnp.float32), axis=0).astype(h.dtype)
    return o.reshape(b, s_len, DIL_HEADS * DIL_HEAD_DIM) @ w_o


def fox_mixer(h, w_qkvf, b_f, w_o):
    b, s_len, _ = h.shape
    inner = FOX_HEADS * FOX_HEAD_DIM
    a = h @ w_qkvf
    qkv = a[..., :3 * inner].reshape(b, s_len, 3, FOX_HEADS, FOX_HEAD_DIM)
    log_f = jax.nn.log_sigmoid((a[..., 3 * inner:] + b_f).astype(jnp.float32))
    cum = jnp.cumsum(log_f, axis=1)
    o = causal_block_attention(qkv[:, :, 0], qkv[:, :, 1], qkv[:, :, 2], FOX_HEAD_DIM ** -0.5, cum)
    return o.reshape(b, s_len, inner) @ w_o


def swiglu(h, w_in, w_out):
    gu = h @ w_in
    return (jax.nn.silu(gu[..., :D_FF]) * gu[..., D_FF:]) @ w_out


def setup_inputs(seed: int = 0) -> dict:
    key = jax.random.key(seed)
    ks = jax.random.split(key, 20)
    f32 = jnp.float32
    n_a, n_b, n_c = (len(range(m, DEPTH, N_MIXERS)) for m in range(N_MIXERS))

    def dense(k, shape):
        return jax.random.normal(k, shape, f32) * shape[-2] ** -0.5

    def gain(k, shape):
        return 1.0 + 0.1 * jax.random.normal(k, shape, f32)

    x = jax.random.normal(ks[0], (BATCH, SEQ, D_MODEL), f32)
    p = jax.random.normal(ks[1], (DEPTH, BATCH, SEQ, D_PLE), f32)
    offsets = jax.random.randint(ks[2], (BATCH, 1), 0, 4096, jnp.int32)
    positions = (offsets + jnp.arange(SEQ, dtype=jnp.int32)[None, :]).astype(jnp.int32)
    norm_g = gain(ks[3], (DEPTH, 4, D_MODEL))
    ffn_w_in = dense(ks[4], (DEPTH, D_MODEL, 2 * D_FF))
    ffn_w_out = dense(ks[5], (DEPTH, D_FF, D_MODEL))
    ple_w_proj = dense(ks[6], (DEPTH, D_PLE, D_MODEL))
    ple_w_gate = dense(ks[7], (DEPTH, D_MODEL, D_MODEL))
    rel_bias = 0.5 * jax.random.normal(ks[8], (REL_BUCKETS, DIL_GROUPS * DIL_HEADS), f32)
    mla_w_a = dense(ks[9], (n_a, D_MODEL, MLA_Q_RANK + MLA_KV_RANK + MLA_ROPE))
    mla_q_norm = gain(ks[10], (n_a, MLA_Q_RANK))
    mla_kv_norm = gain(ks[11], (n_a, MLA_KV_RANK))
    mla_w_uq = dense(ks[12], (n_a, MLA_Q_RANK, MLA_HEADS * (MLA_NOPE + MLA_ROPE)))
    mla_w_ukv = dense(ks[13], (n_a, MLA_KV_RANK, MLA_HEADS * (MLA_NOPE + MLA_V)))
    mla_w_o = dense(ks[14], (n_a, MLA_HEADS * MLA_V, D_MODEL))
    dil_w_qkv = dense(ks[15], (n_b, D_MODEL, DIL_GROUPS * 3 * DIL_HEADS * DIL_HEAD_DIM))
    dil_w_o = dense(ks[16], (n_b, DIL_HEADS * DIL_HEAD_DIM, D_MODEL))
    fox_w_qkvf = dense(ks[17], (n_c, D_MODEL, 3 * FOX_HEADS * FOX_HEAD_DIM + FOX_HEADS))
    fox_b_f = jax.random.uniform(ks[18], (n_c, FOX_HEADS), f32, 1.0, 5.0)
    fox_w_o = dense(ks[19], (n_c, FOX_HEADS * FOX_HEAD_DIM, D_MODEL))
    return {'x': x, 'p': p, 'positions': positions, 'norm_g': norm_g,
            'ffn_w_in': ffn_w_in, 'ffn_w_out': ffn_w_out,
            'ple_w_proj': ple_w_proj, 'ple_w_gate': ple_w_gate, 'rel_bias': rel_bias,
            'mla_w_a': mla_w_a, 'mla_q_norm': mla_q_norm, 'mla_kv_norm': mla_kv_norm,
            'mla_w_uq': mla_w_uq, 'mla_w_ukv': mla_w_ukv, 'mla_w_o': mla_w_o,
            'dil_w_qkv': dil_w_qkv, 'dil_w_o': dil_w_o,
            'fox_w_qkvf': fox_w_qkvf, 'fox_b_f': fox_b_f, 'fox_w_o': fox_w_o}


def reference(x, p, positions, norm_g, ffn_w_in, ffn_w_out, ple_w_proj, ple_w_gate, rel_bias,
              mla_w_a, mla_q_norm, mla_kv_norm, mla_w_uq, mla_w_ukv, mla_w_o,
              dil_w_qkv, dil_w_o, fox_w_qkvf, fox_b_f, fox_w_o):
    h = x
    for i in range(DEPTH):
        mixer, j = i % N_MIXERS, i // N_MIXERS
        g = norm_g[i]
        hn = rms_norm(h, g[0])
        if mixer == 0:
            y = mla_mixer(hn, positions, mla_w_a[j], mla_q_norm[j], mla_kv_norm[j],
                          mla_w_uq[j], mla_w_ukv[j], mla_w_o[j])
        elif mixer == 1:
            y = dilated_mixer(hn, dil_w_qkv[j], dil_w_o[j], rel_bias)
        else:
            y = fox_mixer(hn, fox_w_qkvf[j], fox_b_f[j], fox_w_o[j])
        h = h + rms_norm(y, g[1])
        h = h + rms_norm(swiglu(rms_norm(h, g[2]), ffn_w_in[i], ffn_w_out[i]), g[3])
        h = h + (p[i] @ ple_w_proj[i]) * jax.nn.sigmoid(h @ ple_w_gate[i])
    return h
```

```python
import contextlib
import math
import numpy as np
import ml_dtypes
import concourse.bass as bass
import concourse.mybir as mybir
from concourse.bass_utils import run_bass_kernel_spmd

F32 = mybir.dt.float32
BF16 = mybir.dt.bfloat16
I32 = mybir.dt.int32
AF = mybir.ActivationFunctionType
ALU = mybir.AluOpType

D = 1024
T = 2048
NT = 16
DFF = 2816
NF = 22
DEPTH = 4
EPS = 1e-6
TWO_PI = 2.0 * math.pi


def _size(dt):
    return mybir.dt.size(dt)


class Op:
    __slots__ = ("eng", "fn", "deps", "dma", "signal", "count", "dsem", "dval", "idx")


class Prog:
    def __init__(self):
        self.q = {"pe": [], "act": [], "dve": [], "pool": [], "sp": []}
        self.state = {}
        self.ndma = {"pool": 0, "sp": 0}
        self.NDSEM = 8

    def _st(self, name):
        st = self.state.get(name)
        if st is None:
            st = self.state[name] = [None, {}, []]
        return st

    def inherit(self, new, olds):
        ops = {}
        for n in olds:
            st = self.state.get(n)
            if st is None:
                continue
            for o in [st[0]] + list(st[1].values()) + list(st[2]):
                if o is not None:
                    ops[id(o)] = o
        st = self._st(new)
        for o in ops.values():
            st[2].append(o)

    def add(self, eng, fn, r=(), w=(), dma=False):
        op = Op()
        op.eng, op.fn, op.dma, op.signal = eng, fn, dma, False
        deps = {}

        def dep(o):
            if o is None or o is op:
                return
            deps[id(o)] = o

        for name in r:
            st = self._st(name)
            dep(st[0])
            if name.startswith("ps"):
                for e2, o2 in st[1].items():
                    if e2 != eng:
                        dep(o2)
        for name in w:
            st = self._st(name)
            dep(st[0])
            for o2 in st[1].values():
                dep(o2)
            for o2 in st[2]:
                dep(o2)
        for name in r:
            st = self.state[name]
            if dma:
                st[2].append(op)
            else:
                st[1][eng] = op
        for name in w:
            self.state[name] = [op, {}, []]
        dl = []
        best = {}
        for o in deps.values():
            if o.dma:
                dl.append(o)
                continue
            if o.eng == eng and not dma and eng == "pe":
                continue
            b = best.get(o.eng)
            if b is None or o.idx > b.idx:
                best[o.eng] = o
        dl.extend(best.values())
        op.deps = dl
        op.idx = len(self.q[eng])
        if dma:
            k = self.ndma[eng]
            self.ndma[eng] += 1
            op.dsem = k % self.NDSEM
            op.dval = 16 * (k // self.NDSEM + 1)
        self.q[eng].append(op)
        return op

    def finalize(self):
        for e in self.q:
            for op in self.q[e]:
                for d in op.deps:
                    if not d.dma:
                        d.signal = True
        for e in self.q:
            c = 0
            for op in self.q[e]:
                if op.dma:
                    continue
                if op.signal:
                    c += 1
                    op.count = c

    def emit(self, e, engine, sems, dsems):
        known = {}

        def wait(key, sem, val):
            if known.get(key, 0) >= val:
                return
            known[key] = val
            engine.wait_ge(sem, val)

        for op in self.q[e]:
            for d in op.deps:
                if d.dma:
                    wait(("d", d.eng, d.dsem), dsems[d.eng][d.dsem], d.dval)
                else:
                    wait(("c", d.eng), sems[d.eng], d.count)
            if op.dma:
                if op.dval > 16:
                    wait(("d", e, op.dsem), dsems[e][op.dsem], op.dval - 16)
                ins = op.fn(engine)
                ins.then_inc(dsems[e][op.dsem], 16)
            else:
                ins = op.fn(engine)
                if op.signal:
                    ins.then_inc(sems[e], 1)
        if e in self.ndma:
            for k in range(self.NDSEM):
                n = (self.ndma[e] - k + self.NDSEM - 1) // self.NDSEM
                if n > 0:
                    engine.wait_ge(dsems[e][k], 16 * n)


def sub(handle, dt, off_bytes, shape):
    es = _size(dt)
    h = handle.bitcast(dt)
    n = int(np.prod(shape))
    assert off_bytes % es == 0
    o = off_bytes // es
    ap = h[:, o:o + n]
    if len(shape) == 2:
        ap = ap.rearrange("p (a b) -> p a b", a=shape[0], b=shape[1])
    elif len(shape) == 3:
        ap = ap.rearrange("p (a b c) -> p a b c", a=shape[0], b=shape[1], c=shape[2])
    return ap


class Region:
    def __init__(self, P, handle, nbytes):
        self.P, self.h, self.size = P, handle, nbytes
        self.cur, self.old, self.off = [], [], 0

    def reset(self):
        self.old += self.cur
        self.cur = []
        self.off = 0

    def take(self, names, dt, shape):
        if isinstance(names, str):
            names = [names]
        n = int(np.prod(shape)) * _size(dt)
        n = (n + 3) // 4 * 4
        off = self.off
        self.off += n
        assert self.off <= self.size, (names, self.off, self.size)
        for nm in names:
            self.cur.append(nm)
            self.P.inherit(nm, self.old)
        return sub(self.h, dt, off, shape)


class Builder:
    def __init__(self, n_layers=DEPTH, mixers=(0, 1, 2), types=(0, 1, 2, 0)):
        self.n_layers = n_layers
        self.mixers = mixers
        self.types = list(types)
        self.cctr = 0
        self.sctr = 0
        self.stop = 0

    def build(self):
        nc = bass.Bass("TRN2", target_bir_lowering=False)
        self.nc = nc
        P = self.P = Prog()
        dr = {}

        def din(name, shape, dt=F32):
            dr[name] = nc.dram_tensor(name, list(shape), dt, kind="ExternalInput").ap()

        din("x", [T, D])
        din("pT", [DEPTH, 256, T])
        din("pos", [128, NT], I32)
        din("gpre", [128, DEPTH * 2 * 8])
        din("gpost", [DEPTH * 2, D])
        din("ffn_w_in", [DEPTH, D, 2 * DFF])
        din("ffn_w_out", [DEPTH, DFF, D])
        din("ple_w_proj", [DEPTH, 256, D])
        din("ple_w_gate", [DEPTH, D, D])
        din("mla_w_a", [2, D, 672])
        din("mla_qn", [128, 6])
        din("mla_kvn", [128, 4])
        din("mla_w_uq", [2, 384, 1536])
        din("mla_w_ukv", [2, 256, 2048])
        din("mla_w_o", [2, D, D])
        din("dil_w_qkv", [1, D, 9216])
        din("dil_w_o", [1, D, D])
        din("dil_strip", [48, 128, 256])
        din("fox_w_qkvf", [1, D, 3088])
        din("fox_b_f", [1, 16])
        din("fox_w_o", [1, D, D])
        din("ident", [128, 128], BF16)
        din("trif", [128, 128])
        din("invf", [128, 16])
        dr["out"] = nc.dram_tensor("out", [T, D], F32, kind="ExternalOutput").ap()
        self.dr = dr

        with (
            nc.sbuf_tensor("H", [128, NT, D], F32) as H,
            nc.sbuf_tensor("X", [128, 16384], BF16) as X,
            nc.sbuf_tensor("Y", [128, 22528], BF16) as Y,
            nc.sbuf_tensor("W", [128, 22528], BF16) as W,
            nc.sbuf_tensor("C", [128, 6100], F32) as C,
            nc.psum_tensor("PA", [128, 1024], F32) as PA,
            nc.psum_tensor("PB", [128, 1024], F32) as PB,
            nc.psum_tensor("PC", [128, 1024], F32) as PC,
            nc.psum_tensor("PD", [128, 1024], F32) as PD,
        ):
            self.H = H
            self.RX = Region(P, X, 32768)
            self.RY = Region(P, Y, 45056)
            self.RW = Region(P, W, 45056)
            self.RC = Region(P, C, 24400)
            self.PS = [PA, PB, PC, PD]
            self.setup_consts()
            self.load_x()
            for l in range(self.n_layers):
                self.layer(l)
            self.store_out()
            P.finalize()

            with contextlib.ExitStack() as es:
                sems = {e: es.enter_context(nc.semaphore("s_" + e)) for e in ("pe", "act", "dve", "pool", "sp")}
                dsems = {e: [es.enter_context(nc.semaphore("d_%s%d" % (e, k))) for k in range(P.NDSEM)]
                         for e in ("pool", "sp")}
                block = es.enter_context(nc.Block())

                @block.tensor
                def _(e):
                    P.emit("pe", e, sems, dsems)

                @block.scalar
                def _(e):
                    P.emit("act", e, sems, dsems)

                @block.vector
                def _(e):
                    P.emit("dve", e, sems, dsems)

                @block.gpsimd
                def _(e):
                    P.emit("pool", e, sems, dsems)

                @block.sync
                def _(e):
                    P.emit("sp", e, sems, dsems)
        return nc

    def bank(self, b):
        return self.PS[b // 2][:, (b % 2) * 512:(b % 2) * 512 + 512]

    def bank_bf(self, b):
        h = self.PS[b // 2].bitcast(BF16)
        return h[:, (b % 2) * 1024:(b % 2) * 1024 + 1024]

    def pair(self, q):
        return self.PS[q][:, :]

    def setup_consts(self):
        P, dr, RC = self.P, self.dr, self.RC
        self.ident = RC.take("ident", BF16, [128])
        self.negm = RC.take("negm", F32, [128])
        self.trif = RC.take("trif", F32, [128])
        self.onesf = RC.take("onesf", F32, [128])
        self.gpre = RC.take("gpre", F32, [DEPTH * 2 * 8])
        self.invf = RC.take("invf", F32, [16])
        self.posi = RC.take("posi", I32, [NT])
        self.qn = RC.take("qn", F32, [6])
        self.kvn = RC.take("kvn", F32, [4])
        self.bf = RC.take("bf", F32, [16])
        self.ss = RC.take(["ss0", "ss1", "pss0", "pss1"], F32, [4])
        self.rs = RC.take(["rs0", "rs1", "prs0", "prs1"], F32, [4])
        self.st = RC.take("st", F32, [16])
        self.cos2 = RC.take("cos2", F32, [NT, 32])
        self.sin2 = RC.take("sin2", F32, [NT, 32])
        self.gpost = RC.take("gpost", F32, [D])
        self.hb = [RC.take("hb%d" % k, BF16, [D]) for k in range(2)]
        self.tmpf = [RC.take("tmpf%d" % k, F32, [D]) for k in range(2)]
        self.krope = RC.take("krope", BF16, [NT, 32])
        self.ropes = [RC.take(nm, F32, [2, 32]) for nm in ("ropea", "ropeb")]
        for nm, ap in (("ident", self.ident), ("trif", self.trif), ("gpre", self.gpre),
                       ("invf", self.invf), ("qn", self.qn), ("kvn", self.kvn)):
            src = {"qn": "mla_qn", "kvn": "mla_kvn"}.get(nm, nm)
            P.add("sp", lambda e, ap=ap, src=src: e.dma_start(out=ap, in_=dr[src]), w=[nm], dma=True)
        P.add("sp", lambda e: e.dma_start(out=self.posi, in_=dr["pos"]), w=["posi"], dma=True)
        P.add("sp", lambda e: e.dma_start(out=self.bf.unsqueeze(1), in_=dr["fox_b_f"].partition_broadcast(128)),
              w=["bf"], dma=True)
        P.add("dve", lambda e: e.memset(self.onesf, 1.0), w=["onesf"])
        P.add("dve", lambda e: e.tensor_scalar(out=self.negm, in0=self.trif, scalar1=240000.0, scalar2=-240000.0, op0=ALU.mult, op1=ALU.add),
              r=["trif"], w=["negm"])
        self.rope_tables()

    def rope_tables(self):
        P = self.P
        ang = self.tmpf[0][:, 0:256].rearrange("p (i f) -> p i f", i=NT)
        kf = self.tmpf[0][:, 256:512].rearrange("p (i f) -> p i f", i=NT)
        ki = self.tmpf[0][:, 512:768].bitcast(I32).rearrange("p (i f) -> p i f", i=NT)
        m = self.tmpf[0][:, 768:1024].rearrange("p (i f) -> p i f", i=NT)
        posf = self.tmpf[1][:, 0:NT]
        ab = self.tmpf[1][:, 256:512].rearrange("p (i f) -> p i f", i=NT)
        r = ["tmpf0"]
        P.add("dve", lambda e: e.tensor_copy(out=posf, in_=self.posi), r=["posi"], w=["tmpf1"])
        P.add("dve", lambda e: e.tensor_tensor(out=ang, in0=posf.unsqueeze(2).broadcast_to([128, NT, 16]),
                                               in1=self.invf.unsqueeze(1).broadcast_to([128, NT, 16]), op=ALU.mult),
              r=["tmpf1", "invf"], w=r)
        P.add("dve", lambda e: e.tensor_scalar(out=kf, in0=ang, scalar1=1.0 / TWO_PI, scalar2=None, op0=ALU.mult), r=r, w=r)
        P.add("dve", lambda e: e.tensor_copy(out=ki, in_=kf), r=r, w=r)
        P.add("dve", lambda e: e.tensor_copy(out=kf, in_=ki), r=r, w=r)
        C1 = 6.28125
        C2 = TWO_PI - C1
        P.add("dve", lambda e: e.scalar_tensor_tensor(out=ang, in0=kf, scalar=-C1, in1=ang, op0=ALU.mult, op1=ALU.add), r=r, w=r)
        P.add("dve", lambda e: e.scalar_tensor_tensor(out=ang, in0=kf, scalar=-C2, in1=ang, op0=ALU.mult, op1=ALU.add), r=r, w=r)
        P.add("dve", lambda e: e.tensor_scalar(out=m, in0=ang, scalar1=math.pi, scalar2=-TWO_PI, op0=ALU.is_gt, op1=ALU.mult), r=r, w=r)
        P.add("dve", lambda e: e.tensor_tensor(out=ang, in0=ang, in1=m, op=ALU.add), r=r, w=r)
        P.add("dve", lambda e: e.tensor_scalar(out=m, in0=ang, scalar1=-math.pi, scalar2=TWO_PI, op0=ALU.is_lt, op1=ALU.mult), r=r, w=r)
        P.add("dve", lambda e: e.tensor_tensor(out=ang, in0=ang, in1=m, op=ALU.add), r=r, w=r)
        P.add("dve", lambda e: e.tensor_scalar(out=ang, in0=ang, scalar1=math.pi, scalar2=-math.pi, op0=ALU.min, op1=ALU.max), r=r, w=r)
        P.add("dve", lambda e: e.tensor_scalar(out=ab, in0=ang, scalar1=-1.0, scalar2=None, op0=ALU.mult), r=r, w=["tmpf1"])
        P.add("dve", lambda e: e.tensor_tensor(out=ab, in0=ab, in1=ang, op=ALU.max), r=r + ["tmpf1"], w=["tmpf1"])
        P.add("act", lambda e: e.activation(out=self.sin2[:, :, 16:32], in_=ang, func=AF.Sin), r=r, w=["sin2"])
        hp = self.st[:, 0:1]
        P.add("dve", lambda e: e.memset(hp, math.pi / 2), w=["st"])
        P.add("act", lambda e: e.activation(out=self.cos2[:, :, 0:16], in_=ab, func=AF.Sin, scale=-1.0, bias=hp),
              r=["tmpf1", "st"], w=["cos2"])
        P.add("dve", lambda e: e.tensor_copy(out=self.cos2[:, :, 16:32], in_=self.cos2[:, :, 0:16]), r=["cos2"], w=["cos2"])
        P.add("dve", lambda e: e.tensor_scalar(out=self.sin2[:, :, 0:16], in0=self.sin2[:, :, 16:32], scalar1=-1.0, scalar2=None, op0=ALU.mult),
              r=["sin2"], w=["sin2"])

    def load_x(self):
        P, dr = self.P, self.dr
        xv = dr["x"].rearrange("(i p) d -> p i d", p=128)
        for i0 in range(0, NT, 4):
            P.add("sp", lambda e, i0=i0: e.dma_start(out=self.H[:, i0:i0 + 4, :], in_=xv[:, i0:i0 + 4, :]),
                  w=["H%d" % i for i in range(i0, i0 + 4)], dma=True)

    def store_out(self):
        P, dr = self.P, self.dr
        ov = dr["out"].rearrange("(i p) d -> p i d", p=128)
        for i0 in range(0, NT, 4):
            P.add("sp", lambda e, i0=i0: e.dma_start(out=ov[:, i0:i0 + 4, :], in_=self.H[:, i0:i0 + 4, :]),
                  r=["H%d" % i for i in range(i0, i0 + 4)], w=["out%d" % i0], dma=True)

    def wload(self, dst, src, wname):
        self.P.add("pool", lambda e: e.dma_start(out=dst, in_=src), w=[wname], dma=True)

    def rstd(self, ss_ap, out_ap, inv_n, rn, wn):
        P = self.P
        P.add("act", lambda e: e.activation(out=out_ap, in_=ss_ap, func=AF.Sqrt, scale=inv_n, bias=self.epsc), r=[rn, "epsc"], w=[wn])
        P.add("dve", lambda e: e.reciprocal(out=out_ap, in_=out_ap), r=[wn], w=[wn])

    def prenorm_T(self, l, which, tiles, dst, dst_names, normalize=True):
        P, H = self.P, self.H
        for k, i in enumerate(tiles):
            kk = k % 2
            hb, hbn = self.hb[kk], "hb%d" % kk
            ssn, rsn = "ss%d" % kk, "rs%d" % kk
            ss = self.ss[:, kk:kk + 1]
            rs = self.rs[:, kk:kk + 1]
            hn = "H%d" % i
            if normalize:
                P.add("act", lambda e, i=i, ss=ss, hb=hb: e.activation(out=hb, in_=H[:, i, :], func=AF.Square, accum_out=ss),
                      r=[hn], w=[hbn, ssn])
                self.rstd(ss, rs, 1.0 / D, ssn, rsn)
                P.add("dve", lambda e, i=i, rs=rs, hb=hb: e.tensor_scalar(out=hb, in0=H[:, i, :], scalar1=rs, scalar2=None, op0=ALU.mult),
                      r=[hn, rsn], w=[hbn])
            else:
                P.add("act", lambda e, i=i, hb=hb: e.activation(out=hb, in_=H[:, i, :], func=AF.Copy), r=[hn], w=[hbn])
            b = 6 + kk
            pb = self.bank_bf(b)

            def tr(e, hb=hb, pb=pb):
                for c in range(8):
                    ins = e.transpose(pb[:, c * 128:(c + 1) * 128], hb[:, c * 128:(c + 1) * 128], self.ident)
                return ins
            P.add("pe", tr, r=[hbn, "ident"], w=["ps%d" % b])
            d = dst[:, :, k * 128:(k + 1) * 128]
            src = pb.rearrange("p (c t) -> p c t", c=8)
            dn = [dst_names[k]] if isinstance(dst_names, list) else [dst_names]
            if normalize:
                g = self.gpre[:, (l * 2 + which) * 8:(l * 2 + which) * 8 + 8]
                gb = g.unsqueeze(2).broadcast_to([128, 8, 128])
                P.add("dve", lambda e, d=d, src=src, gb=gb: e.tensor_tensor(out=d, in0=src, in1=gb, op=ALU.mult),
                      r=["ps%d" % b, "gpre"], w=dn)
            else:
                P.add("dve", lambda e, d=d, src=src: e.tensor_copy(out=d, in_=src), r=["ps%d" % b], w=dn)

    def load_gpost(self, l, which):
        P, dr = self.P, self.dr
        row = 2 * l + which
        P.add("sp", lambda e: e.dma_start(out=self.gpost.unsqueeze(1), in_=dr["gpost"][row:row + 1, :].partition_broadcast(128)),
              w=["gpost"], dma=True)

    def postnorm_res(self, q, i):
        P = self.P
        y = self.pair(q)
        pn = ["ps%d" % (2 * q), "ps%d" % (2 * q + 1)]
        k = i % 2
        ss = self.ss[:, 2 + k:3 + k]
        rs = self.rs[:, 2 + k:3 + k]
        ssn, rsn = "pss%d" % k, "prs%d" % k
        tmp, tn = self.tmpf[k], "tmpf%d" % k
        hn = "H%d" % i
        P.add("act", lambda e: e.activation(out=tmp, in_=y, func=AF.Square, accum_out=ss), r=pn, w=[tn, ssn])
        self.rstd(ss, rs, 1.0 / D, ssn, rsn)
        P.add("dve", lambda e: e.scalar_tensor_tensor(out=tmp, in0=y, scalar=rs, in1=self.gpost, op0=ALU.mult, op1=ALU.mult),
              r=pn + [rsn, "gpost"], w=[tn])
        P.add("dve", lambda e: e.tensor_tensor(out=self.H[:, i, :], in0=self.H[:, i, :], in1=tmp, op=ALU.add),
              r=[tn, hn], w=[hn])

    def layer(self, l):
        P = self.P
        if l == 0:
            self.epsc = self.st[:, 1:2]
            P.add("dve", lambda e: e.memset(self.epsc, EPS), w=["epsc"])
            self.onec = self.st[:, 2:3]
            P.add("dve", lambda e: e.memset(self.onec, 1.0), w=["onec"])
        m = self.types[l]
        j = sum(1 for t in self.types[:l] if t == m)
        if m in self.mixers:
            self.load_gpost(l, 0)
            if m == 0:
                self.mla(l, j)
            elif m == 1:
                self.dil(l, j)
            else:
                self.fox(l, j)
        self.load_gpost(l, 1)
        for blk in range(2):
            self.ffn_ple(l, blk)

    def ffn_ple(self, l, blk):
        P, dr = self.P, self.dr
        RX, RY, RW = self.RX, self.RY, self.RW
        TB = 1024
        tiles = list(range(blk * 8, blk * 8 + 8))
        RX.reset()
        RY.reset()
        RW.reset()
        hnT = RX.take("hnT", BF16, [8, TB])
        win = [RX.take("win%d" % k, BF16, [8, 2, 256]) for k in range(2)]
        actT = RY.take(["act%d" % f for f in range(NF)], BF16, [NF, TB])
        wout = RW.take("wout", BF16, [NF, D])
        w_in = dr["ffn_w_in"][l]
        NG = NF // 2

        def load_win(g):
            k = g % 2
            for s in range(2):
                src = w_in[:, s * DFF + g * 256: s * DFF + g * 256 + 256].rearrange("(c p) n -> p c n", p=128)
                self.wload(win[k][:, :, s, :], src, "win%d" % k)

        load_win(0)
        load_win(1)
        wo = dr["ffn_w_out"][l].rearrange("(f p) n -> p f n", p=128)
        for f0 in range(0, NF, 11):
            self.wload(wout[:, f0:f0 + 11, :], wo[:, f0:f0 + 11, :], "wout")
        self.prenorm_T(l, 1, tiles, hnT, "hnT")
        step = 0
        for g in range(NG):
            k = g % 2
            for ff in range(2):
                f = 2 * g + ff
                for tc in range(2):
                    bg, bu = (0, 1) if step % 2 == 0 else (2, 3)
                    step += 1
                    pg, pu = self.bank(bg), self.bank(bu)

                    def mm(e, k=k, ff=ff, tc=tc, pg=pg, pu=pu):
                        for s, po in ((0, pg), (1, pu)):
                            for c in range(8):
                                ins = e.matmul(po, win[k][:, c, s, ff * 128:(ff + 1) * 128], hnT[:, c, tc * 512:(tc + 1) * 512],
                                               start=(c == 0), stop=(c == 7))
                        return ins
                    P.add("pe", mm, r=["win%d" % k, "hnT"], w=["ps%d" % bg, "ps%d" % bu])
                    sg = self.hb[step % 2][:, 0:512]
                    sgn = "hb%d" % (step % 2)
                    P.add("act", lambda e, sg=sg, pg=pg: e.activation(out=sg, in_=pg, func=AF.Silu), r=["ps%d" % bg], w=[sgn])
                    P.add("dve", lambda e, sg=sg, pu=pu, f=f, tc=tc: e.tensor_tensor(out=actT[:, f, tc * 512:(tc + 1) * 512], in0=sg, in1=pu, op=ALU.mult),
                          r=[sgn, "ps%d" % bu], w=["act%d" % f])
            if g + 2 < NG:
                load_win(g + 2)
        for k, i in enumerate(tiles):
            q = k % 2
            y0, y1 = self.bank(2 * q), self.bank(2 * q + 1)

            def mm2(e, k=k, y0=y0, y1=y1):
                for hf, po in ((0, y0), (1, y1)):
                    for f in range(NF):
                        ins = e.matmul(po, actT[:, f, k * 128:(k + 1) * 128], wout[:, f, hf * 512:(hf + 1) * 512],
                                       start=(f == 0), stop=(f == NF - 1))
                return ins
            P.add("pe", mm2, r=["act%d" % f for f in range(NF)] + ["wout"], w=["ps%d" % (2 * q), "ps%d" % (2 * q + 1)])
            self.postnorm_res(q, i)
        RX.reset()
        RY.reset()
        hT = RX.take("hT", BF16, [8, TB])
        pT = RX.take("pTb", BF16, [2, TB])
        wp = RX.take("wproj", BF16, [2, D])
        wg = RY.take("wgate", BF16, [8, D])
        self.wload(pT, dr["pT"][l][:, blk * TB:(blk + 1) * TB].rearrange("(c p) t -> p c t", p=128), "pTb")
        self.wload(wp, dr["ple_w_proj"][l].rearrange("(c p) n -> p c n", p=128), "wproj")
        self.wload(wg, dr["ple_w_gate"][l].rearrange("(c p) n -> p c n", p=128), "wgate")
        self.prenorm_T(l, 0, tiles, hT, "hT", normalize=False)
        for k, i in enumerate(tiles):
            qg, qp = (0, 1) if k % 2 == 0 else (2, 1)
            qg = 0 if k % 2 == 0 else 2
            qp = 1
            gps, pps = self.pair(qg), self.pair(qp)

            def mmg(e, k=k, qg=qg):
                for hf in range(2):
                    po = self.bank(2 * qg + hf)
                    for c in range(8):
                        ins = e.matmul(po, hT[:, c, k * 128:(k + 1) * 128], wg[:, c, hf * 512:(hf + 1) * 512], start=(c == 0), stop=(c == 7))
                return ins

            def mmp(e, k=k, qp=qp):
                for hf in range(2):
                    po = self.bank(2 * qp + hf)
                    for c in range(2):
                        ins = e.matmul(po, pT[:, c, k * 128:(k + 1) * 128], wp[:, c, hf * 512:(hf + 1) * 512], start=(c == 0), stop=(c == 1))
                return ins
            P.add("pe", mmg, r=["hT", "wgate"], w=["ps%d" % (2 * qg), "ps%d" % (2 * qg + 1)])
            P.add("pe", mmp, r=["pTb", "wproj"], w=["ps%d" % (2 * qp), "ps%d" % (2 * qp + 1)])
            tmp, tn = self.tmpf[k % 2], "tmpf%d" % (k % 2)
            hn = "H%d" % i
            P.add("act", lambda e, tmp=tmp, gps=gps: e.activation(out=tmp, in_=gps, func=AF.Sigmoid),
                  r=["ps%d" % (2 * qg), "ps%d" % (2 * qg + 1)], w=[tn])
            P.add("dve", lambda e, tmp=tmp, pps=pps: e.tensor_tensor(out=tmp, in0=tmp, in1=pps, op=ALU.mult),
                  r=[tn, "ps%d" % (2 * qp), "ps%d" % (2 * qp + 1)], w=[tn])
            P.add("dve", lambda e, tmp=tmp, i=i: e.tensor_tensor(out=self.H[:, i, :], in0=self.H[:, i, :], in1=tmp, op=ALU.add),
                  r=[tn, hn], w=[hn])

    def qk_scratch(self, R, dkp):
        self.Qtok = [R.take("qtok%d" % k, BF16, [2, dkp]) for k in range(2)]
        self.Ktok = [R.take("ktok%d" % k, BF16, [2, dkp]) for k in range(2)]

    def qk_transpose(self, i, dk, qT, kT, last):
        P = self.P
        k2 = i % 2
        b = 6 + ((i // 2) % 2)
        pb = self.bank_bf(b).rearrange("p (s t) -> p s t", s=4)
        Q, K = self.Qtok[k2], self.Ktok[k2]

        def tr(e):
            for s, src in enumerate((Q[:, 0, 0:dk], Q[:, 1, 0:dk], K[:, 0, 0:dk], K[:, 1, 0:dk])):
                ins = e.transpose(pb[0:dk, s, (i % 2) * 128:(i % 2) * 128 + 128], src, self.ident)
            return ins
        P.add("pe", tr, r=["qtok%d" % k2, "ktok%d" % k2, "ident"], w=["ps%d" % b])
        if i % 2 == 1:
            i0 = i - 1
            for s, dst, nm in ((0, qT[0:dk, 0, i0 * 128:i0 * 128 + 256], "qT0"), (1, qT[0:dk, 1, i0 * 128:i0 * 128 + 256], "qT1"),
                               (2, kT[0:dk, 0, i0 * 128:i0 * 128 + 256], "kT0"), (3, kT[0:dk, 1, i0 * 128:i0 * 128 + 256], "kT1")):
                P.add("dve", lambda e, dst=dst, s=s, pb=pb: e.tensor_copy(out=dst, in_=pb[0:dk, s, :]), r=["ps%d" % b], w=[nm])

    def attention(self, hg, qT, kT, V, dk, scale, bias_fn, bias_names, Otok):
        for hh in range(2):
            for qc in range(4):
                self.attn_chunk(hg * 2 + hh, hh, qc, qT, kT, V, dk, scale, bias_fn, bias_names, Otok)

    def attn_chunk(self, h, hh, qc, qT, kT, V, dk, scale, bias_fn, bias_names, Otok):
        P = self.P
        accb = 2 + (self.cctr % 2)
        rcs = self.cctr % 2
        self.cctr += 1
        acc = self.bank(accb)
        first = [True]

        def do_S(j, off, ncols):
            sb = self.sctr % 2
            pt = self.sctr % len(self.PT)
            self.sctr += 1
            S = self.bank(sb)
            PTb = self.PT[pt]
            P.add("pe", lambda e: e.matmul(S[:, 0:ncols], kT[0:dk, hh, j * 128:(j + 1) * 128],
                                           qT[0:dk, hh, qc * 512 + off:(qc + 1) * 512], start=True, stop=True),
                  r=["kT%d" % hh, "qT%d" % hh], w=["ps%d" % sb])
            if j >= 4 * qc:
                P.add("dve", lambda e: e.tensor_tensor(out=S[:, 0:128], in0=S[:, 0:128], in1=self.negm, op=ALU.add),
                      r=["negm"], w=["ps%d" % sb])
            bias = bias_fn(j, h) if bias_fn is not None else None
            if bias is None:
                P.add("act", lambda e: e.activation(out=PTb[:, 0:ncols], in_=S[:, 0:ncols], func=AF.Exp, scale=scale),
                      r=["ps%d" % sb], w=["pt%d" % pt])
            else:
                P.add("act", lambda e: e.activation(out=PTb[:, 0:ncols], in_=S[:, 0:ncols], func=AF.Exp, scale=scale, bias=bias),
                      r=["ps%d" % sb] + bias_names, w=["pt%d" % pt])
            return pt

        def do_PV(j, off, ncols, pt):
            PTb = self.PT[pt]

            def mm(e):
                for ql in range(off // 128, 4):
                    st = first[0]
                    first[0] = False
                    ins = e.matmul(acc[:, ql * 128:ql * 128 + 65], PTb[:, ql * 128 - off:ql * 128 - off + 128],
                                   V[:, j, hh, 0:65], start=st, stop=(j == 4 * qc + ql), skip_group_check=True)
                return ins
            P.add("pe", mm, r=["pt%d" % pt, "V"], w=["ps%d" % accb])

        prev = None
        for j in range(4 * qc + 4):
            off = max(0, j - 4 * qc) * 128
            pt = do_S(j, off, 512 - off)
            if prev is not None:
                do_PV(*prev)
            prev = (j, off, 512 - off, pt)
        do_PV(*prev)
        rc = self.st[:, 4 + 4 * rcs:8 + 4 * rcs]
        rcn = "rc%d" % rcs
        accv = acc.rearrange("p (q c) -> p q c", q=4)
        P.add("dve", lambda e: e.reciprocal(out=rc, in_=accv[:, :, 64]), r=["ps%d" % accb], w=[rcn])
        P.add("dve", lambda e: e.tensor_tensor(out=Otok[:, 4 * qc:4 * qc + 4, h * 64:(h + 1) * 64], in0=accv[:, :, 0:64],
                                               in1=rc.unsqueeze(2).broadcast_to([128, 4, 64]), op=ALU.mult),
              r=["ps%d" % accb, rcn], w=["Otok"])

    def out_proj(self, Otok, wo):
        P = self.P
        for i in range(NT):
            kk = i % 2
            b = 6 + kk
            pb = self.bank_bf(b)
            oT, oTn = self.hb[kk], "hb%d" % kk

            def tr(e, i=i, pb=pb):
                for c in range(8):
                    ins = e.transpose(pb[:, c * 128:(c + 1) * 128], Otok[:, i, c * 128:(c + 1) * 128], self.ident)
                return ins
            P.add("pe", tr, r=["Otok", "ident"], w=["ps%d" % b])
            P.add("dve", lambda e, oT=oT, pb=pb: e.tensor_copy(out=oT, in_=pb), r=["ps%d" % b], w=[oTn])
            q = kk

            def mm(e, oT=oT, q=q):
                for hf in range(2):
                    po = self.bank(2 * q + hf)
                    for c in range(8):
                        ins = e.matmul(po, oT[:, c * 128:(c + 1) * 128], wo[:, c, hf * 512:(hf + 1) * 512], start=(c == 0), stop=(c == 7))
                return ins
            P.add("pe", mm, r=[oTn, "wo"], w=["ps%d" % (2 * q), "ps%d" % (2 * q + 1)])
            self.postnorm_res(q, i)

    def mla(self, l, j):
        P, dr = self.P, self.dr
        RX, RY, RW = self.RX, self.RY, self.RW
        RX.reset(); RY.reset(); RW.reset()
        hnT = RX.take(["hnT%d" % i for i in range(NT)], BF16, [8, T])
        wa = RW.take("wa", BF16, [8, 672])
        wuq = RW.take("wuq", BF16, [3, 1536])
        wukv = RW.take("wukv", BF16, [2, 2048])
        wo = RW.take("wo", BF16, [8, D])
        cqT = RY.take("cqT", BF16, [3, T])
        ckvT = RY.take("ckvT", BF16, [2, T])
        qT = RY.take(["qT0", "qT1"], BF16, [2, T])
        kT = RY.take(["kT0", "kT1"], BF16, [2, T])
        V = RY.take("V", BF16, [NT, 2, 66])
        self.qk_scratch(RY, 96)
        self.PT = [RY.take("pt%d" % k, BF16, [512]) for k in range(2)]
        self.wload(wa, dr["mla_w_a"][j].rearrange("(c p) n -> p c n", p=128), "wa")
        self.wload(wuq, dr["mla_w_uq"][j].rearrange("(c p) n -> p c n", p=128), "wuq")
        self.wload(wukv, dr["mla_w_ukv"][j].rearrange("(c p) n -> p c n", p=128), "wukv")
        self.wload(wo, dr["mla_w_o"][j].rearrange("(c p) n -> p c n", p=128), "wo")
        self.prenorm_T(l, 0, list(range(NT)), hnT, ["hnT%d" % i for i in range(NT)])
        P.add("dve", lambda e: e.memset(V[:, :, :, 64:66], 1.0), w=["V"])
        for i in range(NT):
            q = i % 2
            pa, pbk = self.bank(2 * q), self.bank(2 * q + 1)
            pn = ["ps%d" % (2 * q), "ps%d" % (2 * q + 1)]

            def mm(e, i=i, pa=pa, pbk=pbk):
                for c in range(8):
                    e.matmul(pa[:, 0:384], hnT[:, c, i * 128:(i + 1) * 128], wa[:, c, 0:384], start=(c == 0), stop=(c == 7))
                for c in range(8):
                    ins = e.matmul(pbk[:, 0:288], hnT[:, c, i * 128:(i + 1) * 128], wa[:, c, 384:672], start=(c == 0), stop=(c == 7))
                return ins
            P.add("pe", mm, r=["hnT%d" % i, "wa"], w=pn)
            kk = i % 2
            hb, hbn = self.hb[kk], "hb%d" % kk
            ssq, sskv = self.ss[:, kk:kk + 1], self.rs[:, kk:kk + 1]
            rq, rkv = self.st[:, 12 + 2 * kk:13 + 2 * kk], self.st[:, 13 + 2 * kk:14 + 2 * kk]
            n_ssq, n_sskv, n_rq, n_rkv = "ss%d" % kk, "rs%d" % kk, "mrq%d" % kk, "mrkv%d" % kk
            P.add("act", lambda e, hb=hb, pa=pa, ssq=ssq: e.activation(out=hb[:, 0:384], in_=pa[:, 0:384], func=AF.Square, accum_out=ssq),
                  r=pn, w=[hbn, n_ssq])
            P.add("act", lambda e, hb=hb, pbk=pbk, sskv=sskv: e.activation(out=hb[:, 384:640], in_=pbk[:, 0:256], func=AF.Square, accum_out=sskv),
                  r=pn, w=[hbn, n_sskv])
            self.rstd(ssq, rq, 1.0 / 384, n_ssq, n_rq)
            self.rstd(sskv, rkv, 1.0 / 256, n_sskv, n_rkv)
            P.add("dve", lambda e, hb=hb, pa=pa, rq=rq: e.tensor_scalar(out=hb[:, 0:384], in0=pa[:, 0:384], scalar1=rq, scalar2=None, op0=ALU.mult),
                  r=pn + [n_rq], w=[hbn])
            P.add("dve", lambda e, hb=hb, pbk=pbk, rkv=rkv: e.tensor_scalar(out=hb[:, 384:640], in0=pbk[:, 0:256], scalar1=rkv, scalar2=None, op0=ALU.mult),
                  r=pn + [n_rkv], w=[hbn])
            ta = self.ropes[0][:, 0, :]
            tb = self.ropes[1][:, 0, :]
            P.add("dve", lambda e, ta=ta, pbk=pbk, i=i: e.tensor_tensor(out=ta, in0=pbk[:, 256:288], in1=self.cos2[:, i, :], op=ALU.mult),
                  r=pn + ["cos2"], w=["ropea"])
            P.add("dve", lambda e, tb=tb, pbk=pbk, i=i: e.tensor_tensor(out=tb[:, 0:16], in0=pbk[:, 272:288], in1=self.sin2[:, i, 0:16], op=ALU.mult),
                  r=pn + ["sin2"], w=["ropeb"])
            P.add("dve", lambda e, tb=tb, pbk=pbk, i=i: e.tensor_tensor(out=tb[:, 16:32], in0=pbk[:, 256:272], in1=self.sin2[:, i, 16:32], op=ALU.mult),
                  r=pn + ["sin2"], w=["ropeb"])
            P.add("dve", lambda e, ta=ta, tb=tb, i=i: e.tensor_tensor(out=self.krope[:, i, :], in0=ta, in1=tb, op=ALU.add),
                  r=["ropea", "ropeb"], w=["krope"])
            b = 6 + kk
            pt = self.bank_bf(b)

            def tr(e, hb=hb, pt=pt):
                for c in range(5):
                    ins = e.transpose(pt[:, c * 128:(c + 1) * 128], hb[:, c * 128:(c + 1) * 128], self.ident)
                return ins
            P.add("pe", tr, r=[hbn, "ident"], w=["ps%d" % b])
            ptv = pt.rearrange("p (c t) -> p c t", c=8)
            gq = self.qn[:, 3 * j:3 * j + 3].unsqueeze(2).broadcast_to([128, 3, 128])
            gkv = self.kvn[:, 2 * j:2 * j + 2].unsqueeze(2).broadcast_to([128, 2, 128])
            P.add("dve", lambda e, ptv=ptv, gq=gq, i=i: e.tensor_tensor(out=cqT[:, :, i * 128:(i + 1) * 128], in0=ptv[:, 0:3, :], in1=gq, op=ALU.mult),
                  r=["ps%d" % b, "qn"], w=["cqT"])
            P.add("dve", lambda e, ptv=ptv, gkv=gkv, i=i: e.tensor_tensor(out=ckvT[:, :, i * 128:(i + 1) * 128], in0=ptv[:, 3:5, :], in1=gkv, op=ALU.mult),
                  r=["ps%d" % b, "kvn"], w=["ckvT"])
        if self.stop == 1:
            return
        RX.reset()
        Otok = RX.take("Otok", BF16, [NT, D])
        scale = 96.0 ** -0.5
        for hg in range(8):
            for i in range(NT):
                b = 4 + (i % 2)
                ps = self.bank(b)
                psn = ["ps%d" % b]

                def mm(e, i=i, ps=ps, hg=hg):
                    for c in range(3):
                        e.matmul(ps[:, 0:192], cqT[:, c, i * 128:(i + 1) * 128], wuq[:, c, hg * 192:(hg + 1) * 192], start=(c == 0), stop=(c == 2))
                    for c in range(2):
                        ins = e.matmul(ps[:, 256:512], ckvT[:, c, i * 128:(i + 1) * 128], wukv[:, c, hg * 256:(hg + 1) * 256], start=(c == 0), stop=(c == 1))
                    return ins
                P.add("pe", mm, r=["cqT", "ckvT", "wuq", "wukv"], w=psn)
                k2 = i % 2
                Q, K = self.Qtok[k2], self.Ktok[k2]
                qn_, kn_ = "qtok%d" % k2, "ktok%d" % k2
                qv = ps[:, 0:192].rearrange("p (h c) -> p h c", h=2)
                kvv = ps[:, 256:512].rearrange("p (h c) -> p h c", h=2)
                ta, tb = self.ropes[0], self.ropes[1]
                cosb = self.cos2[:, i, :].unsqueeze(1).broadcast_to([128, 2, 32])
                sina = self.sin2[:, i, 0:16].unsqueeze(1).broadcast_to([128, 2, 16])
                sinb = self.sin2[:, i, 16:32].unsqueeze(1).broadcast_to([128, 2, 16])
                P.add("dve", lambda e, Q=Q, qv=qv: e.tensor_copy(out=Q[:, :, 0:64], in_=qv[:, :, 0:64]), r=psn, w=[qn_])
                P.add("dve", lambda e, ta=ta, qv=qv, cosb=cosb: e.tensor_tensor(out=ta, in0=qv[:, :, 64:96], in1=cosb, op=ALU.mult),
                      r=psn + ["cos2"], w=["ropea"])
                P.add("dve", lambda e, tb=tb, qv=qv, sina=sina: e.tensor_tensor(out=tb[:, :, 0:16], in0=qv[:, :, 80:96], in1=sina, op=ALU.mult),
                      r=psn + ["sin2"], w=["ropeb"])
                P.add("dve", lambda e, tb=tb, qv=qv, sinb=sinb: e.tensor_tensor(out=tb[:, :, 16:32], in0=qv[:, :, 64:80], in1=sinb, op=ALU.mult),
                      r=psn + ["sin2"], w=["ropeb"])
                P.add("dve", lambda e, Q=Q, ta=ta, tb=tb: e.tensor_tensor(out=Q[:, :, 64:96], in0=ta, in1=tb, op=ALU.add),
                      r=["ropea", "ropeb"], w=[qn_])
                P.add("dve", lambda e, K=K, kvv=kvv: e.tensor_copy(out=K[:, :, 0:64], in_=kvv[:, :, 0:64]), r=psn, w=[kn_])
                P.add("dve", lambda e, K=K, i=i: e.tensor_copy(out=K[:, :, 64:96], in_=self.krope[:, i, :].unsqueeze(1).broadcast_to([128, 2, 32])),
                      r=["krope"], w=[kn_])
                P.add("dve", lambda e, kvv=kvv, i=i: e.tensor_copy(out=V[:, i, :, 0:64], in_=kvv[:, :, 64:128]), r=psn, w=["V"])
                self.qk_transpose(i, 96, qT, kT, i == NT - 1)
            if self.stop not in (2, 4):
                self.attention(hg, qT, kT, V, 96, scale, None, [], Otok)
            if self.stop == 4 and hg == 0:
                P.add("dve", lambda e: e.memset(Otok, 0.5), w=["Otok"])
        if self.stop == 3:
            dbg = self.nc.dram_tensor("dbg", [128, NT * D], BF16, kind="ExternalOutput").ap()
            P.add("sp", lambda e: e.dma_start(out=dbg, in_=Otok.rearrange("p a b -> p (a b)")), r=["Otok"], w=["dbgout"], dma=True)
        if self.stop in (2, 3):
            return
        self.out_proj(Otok, wo)

    def cls_ap(self, ap, d, c4):
        if d == 1:
            return ap[:, c4 * 512:(c4 + 1) * 512]
        if d == 4:
            return ap.rearrange("p (u r) -> p r u", r=4)[:, c4, :]
        return ap.rearrange("p (u r) -> p r u", r=16)[:, 4 * c4:4 * c4 + 4, :]

    def dil(self, l, j):
        P, dr = self.P, self.dr
        RX, RY, RW = self.RX, self.RY, self.RW
        RX.reset(); RY.reset(); RW.reset()
        hnT = RX.take(["hnT%d" % i for i in range(NT)], BF16, [8, T])
        oT = RY.take("oTall", BF16, [8, T])
        UT = RY.take("UTacc", F32, [T])
        UZ = [RY.take("uz%d" % k, BF16, [2, 4, 128]) for k in range(2)]
        wq = [RW.take("wq%d" % k, BF16, [8, 3, 128]) for k in range(2)]
        qkv = [RW.take(nm, BF16, [T]) for nm in ("dqT", "dkT", "dvT")]
        V = RW.take("V", BF16, [NT, 2, 66])
        strip = [RW.take("strip%d" % k, F32, [2, 256]) for k in range(2)]
        PT = [RW.take("pt%d" % k, BF16, [256]) for k in range(3)]
        tmpS = [RW.take("tmpS%d" % k, F32, [256]) for k in range(2)]
        ZT = RW.take("ZTacc", F32, [T])
        w = dr["dil_w_qkv"][j]
        hn_names = ["hnT%d" % i for i in range(NT)]

        combos = [(hp, g) for hp in range(8) for g in range(3)]

        def load_w(ci):
            hp, g = combos[ci]
            k = ci % 2
            for s_ in range(3):
                c0 = g * 3072 + s_ * 1024 + hp * 128
                self.wload(wq[k][:, :, s_, :], w[:, c0:c0 + 128].rearrange("(c p) n -> p c n", p=128), "wq%d" % k)

        def load_strip(ci):
            hp, g = combos[ci]
            k = ci % 2
            self.P.add("sp", lambda e, k=k, g=g, hp=hp: e.dma_start(out=strip[k], in_=dr["dil_strip"][g * 16 + 2 * hp:g * 16 + 2 * hp + 2].rearrange("h s t -> s h t")),
                       w=["strip%d" % k], dma=True)

        load_w(0)
        load_w(1)
        load_strip(0)
        load_strip(1)
        self.prenorm_T(l, 0, list(range(NT)), hnT, hn_names)
        P.add("dve", lambda e: e.memset(V[:, :, :, 64:66], 1.0), w=["V"])
        dils = (1, 4, 16)
        pctr = 0
        only = getattr(self, "only_g", None)
        for ci, (hp, g) in enumerate(combos):
            if only is not None and g != only:
                continue
            d = dils[g]
            n_blk = NT // d
            k = ci % 2
            for s_ in range(3):
                for c4 in range(4):
                    b = pctr % 4
                    pctr += 1
                    ps = self.bank(b)

                    def mm(e, s_=s_, c4=c4, ps=ps, k=k, d=d):
                        for c in range(8):
                            ins = e.matmul(ps, wq[k][:, c, s_, :], self.cls_ap(hnT[:, c, :], d, c4), start=(c == 0), stop=(c == 7))
                        return ins
                    P.add("pe", mm, r=hn_names + ["wq%d" % k], w=["ps%d" % b])
                    dst = qkv[s_][:, c4 * 512:(c4 + 1) * 512]
                    nm = ("dqT", "dkT", "dvT")[s_]
                    if s_ == 1:
                        P.add("dve", lambda e, dst=dst, ps=ps: e.tensor_copy(out=dst, in_=ps), r=["ps%d" % b], w=[nm])
                    else:
                        P.add("act", lambda e, dst=dst, ps=ps: e.activation(out=dst, in_=ps, func=AF.Copy), r=["ps%d" % b], w=[nm])
            for t4 in range(4):
                b = 6 + (t4 % 2)
                pb = self.bank_bf(b)

                def tr(e, t4=t4, pb=pb):
                    for tt in range(4):
                        kt = 4 * t4 + tt
                        ins = e.transpose(pb[:, tt * 128:(tt + 1) * 128], qkv[2][:, kt * 128:(kt + 1) * 128], self.ident)
                    return ins
                P.add("pe", tr, r=["dvT", "ident"], w=["ps%d" % b])
                P.add("dve", lambda e, t4=t4, pb=pb: e.tensor_copy(out=V[:, 4 * t4:4 * t4 + 4, :, 0:64],
                                                                  in_=pb[:, 0:512].rearrange("p (t h c) -> p t h c", t=4, h=2)),
                      r=["ps%d" % b], w=["V"])
            if ci + 2 < len(combos):
                load_w(ci + 2)
            if self.stop == 6:
                for nm_, ap_, dt_, shp in (("dbgV", V.rearrange("p a b c -> p (a b c)"), BF16, [128, NT * 2 * 66]), ("dbgq", qkv[0], BF16, [128, T]),
                                           ("dbgk", qkv[1], BF16, [128, T]), ("dbgh", hnT.rearrange("p a b -> p (a b)"), BF16, [128, 8 * T])):
                    dd = self.nc.dram_tensor(nm_, shp, dt_, kind="ExternalOutput").ap()
                    P.add("sp", lambda e, dd=dd, ap_=ap_: e.dma_start(out=dd, in_=ap_), r=["V", "dqT", "dkT"] + hn_names, w=[nm_], dma=True)
                return
            self.dil_attention(g, d, n_blk, k, hp, qkv, V, strip, PT, tmpS, UZ, UT, ZT)
            if ci + 2 < len(combos):
                load_strip(ci + 2)
            if (g == 2 or only is not None) and self.stop == 5:
                d1 = self.nc.dram_tensor("dbgU", [128, T], F32, kind="ExternalOutput").ap()
                d2 = self.nc.dram_tensor("dbgZ", [128, T], F32, kind="ExternalOutput").ap()
                P.add("sp", lambda e: e.dma_start(out=d1, in_=UT), r=["UTacc"], w=["dbg1"], dma=True)
                P.add("sp", lambda e: e.dma_start(out=d2, in_=ZT), r=["ZTacc"], w=["dbg2"], dma=True)
                return
            if g == 2:
                P.add("dve", lambda e: e.reciprocal(out=ZT, in_=ZT), r=["ZTacc"], w=["ZTacc"])
                P.add("dve", lambda e, hp=hp: e.tensor_tensor(out=oT[:, hp, :], in0=UT, in1=ZT, op=ALU.mult),
                      r=["UTacc", "ZTacc"], w=["oTall"])
        RW.reset()
        wo = RW.take("wo", BF16, [8, D])
        self.wload(wo, dr["dil_w_o"][j].rearrange("(c p) n -> p c n", p=128), "wo")
        for i in range(NT):
            q = i % 2

            def mm(e, i=i, q=q):
                for hf in range(2):
                    po = self.bank(2 * q + hf)
                    for c in range(8):
                        ins = e.matmul(po, oT[:, c, i * 128:(i + 1) * 128], wo[:, c, hf * 512:(hf + 1) * 512], start=(c == 0), stop=(c == 7))
                return ins
            P.add("pe", mm, r=["oTall", "wo"], w=["ps%d" % (2 * q), "ps%d" % (2 * q + 1)])
            self.postnorm_res(q, i)

    def dil_attention(self, g, d, n_blk, k, hp, qkv, V, strip, PT, tmpS, UZ, UT, ZT):
        P = self.P
        qT, kT = qkv[0], qkv[1]
        for ch in range(4):
            for tt in range(4):
                kt = 4 * ch + tt
                jb = kt % n_blk
                has_next = (jb + 1 < n_blk)
                ncols = 256 if has_next else 128
                for hh in range(2):
                    self.dil_step(kt, jb, has_next, ncols, hh, k, qT, kT, V, strip, PT, tmpS)
            self.dil_chunk_out(0 if getattr(self, "only_g", None) is not None else g, d, ch, UZ, UT, ZT)

    def dil_step(self, kt, jb, has_next, ncols, hh, k, qT, kT, V, strip, PT, tmpS):
        P = self.P
        sb = self.sctr % 2
        pt = self.sctr % 3
        ts = self.sctr % 2
        self.sctr += 1
        S = self.bank(sb)
        rows = slice(hh * 64, hh * 64 + 64)
        P.add("pe", lambda e: e.matmul(S[:, 0:ncols], kT[rows, kt * 128:(kt + 1) * 128], qT[rows, kt * 128:kt * 128 + ncols], start=True, stop=True),
              r=["dkT", "dqT"], w=["ps%d" % sb])
        tm = tmpS[ts]
        P.add("dve", lambda e: e.scalar_tensor_tensor(out=tm[:, 0:ncols], in0=S[:, 0:ncols], scalar=0.125, in1=strip[k][:, hh, 0:ncols],
                                                      op0=ALU.mult, op1=ALU.add),
              r=["ps%d" % sb, "strip%d" % k], w=["tmpS%d" % ts])
        PTb = PT[pt]
        P.add("act", lambda e: e.activation(out=PTb[:, 0:ncols], in_=tm[:, 0:ncols], func=AF.Exp), r=["tmpS%d" % ts], w=["pt%d" % pt])

        def acc_ap(qt):
            bank = (2 if (qt // 4) % 2 == 0 else 4) + hh
            return self.bank(bank)[:, (qt % 4) * 128:(qt % 4) * 128 + 65], "ps%d" % bank

        a0, n0 = acc_ap(kt)
        P.add("pe", lambda e: e.matmul(a0, PTb[:, 0:128], V[:, kt, hh, 0:65], start=(jb == 0), stop=True), r=["pt%d" % pt, "V"], w=[n0])
        if has_next:
            a1, n1 = acc_ap(kt + 1)
            P.add("pe", lambda e: e.matmul(a1, PTb[:, 128:256], V[:, kt, hh, 0:65], start=True, stop=False), r=["pt%d" % pt, "V"], w=[n1])

    def dil_chunk_out(self, g, d, ch, UZ, UT, ZT):
        P = self.P
        uz = UZ[ch % 2]
        uzn = "uz%d" % (ch % 2)
        for hh in range(2):
            bank = (2 if ch % 2 == 0 else 4) + hh
            accv = self.bank(bank).rearrange("p (q c) -> p q c", q=4)
            P.add("dve", lambda e, accv=accv, hh=hh: e.tensor_copy(out=uz[:, 0, :, hh * 64:(hh + 1) * 64], in_=accv[:, :, 0:64]),
                  r=["ps%d" % bank], w=[uzn])
            P.add("dve", lambda e, accv=accv, hh=hh: e.tensor_copy(out=uz[:, 1, :, hh * 64:(hh + 1) * 64], in_=accv[:, :, 64:65].broadcast_to([128, 4, 64])),
                  r=["ps%d" % bank], w=[uzn])
        b = 6 + (ch % 2)
        pb = self.bank_bf(b)

        def tr(e):
            for a in range(2):
                for tt in range(4):
                    ins = e.transpose(pb[:, (a * 4 + tt) * 128:(a * 4 + tt + 1) * 128], uz[:, a, tt, :], self.ident)
            return ins
        P.add("pe", tr, r=[uzn, "ident"], w=["ps%d" % b])
        for a, acc_t, nm in ((0, UT, "UTacc"), (1, ZT, "ZTacc")):
            dst = self.cls_ap(acc_t, d, ch)
            src = pb[:, a * 512:(a + 1) * 512]
            if d == 16:
                src = src.rearrange("p (r u) -> p r u", r=4)
            if g == 0:
                P.add("dve", lambda e, dst=dst, src=src: e.tensor_copy(out=dst, in_=src), r=["ps%d" % b], w=[nm])
            else:
                P.add("dve", lambda e, dst=dst, src=src: e.tensor_tensor(out=dst, in0=dst, in1=src, op=ALU.add), r=["ps%d" % b, nm], w=[nm])

    def fox(self, l, j):
        P, dr = self.P, self.dr
        RX, RY, RW = self.RX, self.RY, self.RW
        RX.reset(); RY.reset(); RW.reset()
        hnT = RX.take(["hnT%d" % i for i in range(NT)], BF16, [8, T])
        Otok = RY.take("Otok", BF16, [NT, D])
        qT = RY.take(["qT0", "qT1"], BF16, [2, T])
        self.qk_scratch(RY, 68)
        wf = RW.take("wf", BF16, [8, 16])
        wq = [RW.take("wq%d" % k, BF16, [8, 3, 128]) for k in range(2)]
        wo = RW.take("wo", BF16, [8, D])
        kT = RW.take(["kT0", "kT1"], BF16, [2, T])
        V = RW.take("V", BF16, [NT, 2, 66])
        self.PT = [RW.take("pt%d" % k, BF16, [512]) for k in range(3)]
        w = dr["fox_w_qkvf"][j]

        def load_wq(hg):
            k = hg % 2
            for s in range(3):
                src = w[:, s * 1024 + hg * 128: s * 1024 + hg * 128 + 128].rearrange("(c p) n -> p c n", p=128)
                self.wload(wq[k][:, :, s, :], src, "wq%d" % k)

        self.wload(wf, w[:, 3072:3088].rearrange("(c p) n -> p c n", p=128), "wf")
        load_wq(0)
        load_wq(1)
        self.wload(wo, dr["fox_w_o"][j].rearrange("(c p) n -> p c n", p=128), "wo")
        self.prenorm_T(l, 0, list(range(NT)), hnT, ["hnT%d" % i for i in range(NT)])
        P.add("dve", lambda e: e.memset(V[:, :, :, 64:66], 1.0), w=["V"])
        for k in range(2):
            Kt = self.Ktok[k]
            P.add("dve", lambda e, Kt=Kt: e.memset(Kt[:, :, 64:68], 1.0), w=["ktok%d" % k])
        t0, t1 = self.tmpf[0], self.tmpf[1]
        LF = t0[:, 0:256]
        CS = t0[:, 256:512]
        C8 = t0[:, 512:768]
        R1 = t0[:, 768:1024]
        AUG = t1[:, 0:384].bitcast(BF16)[:, 0:768].rearrange("p (i h a) -> p i h a", i=NT, h=16)
        HB = t1[:, 384:512].bitcast(BF16)
        H32 = t1[:, 512:768]
        PRE = t1[:, 768:1024]
        LFv = LF.rearrange("p (i h) -> p i h", i=NT)
        for i in range(NT):
            b = 4 + (i % 2)
            ps = self.bank(b)

            def mm(e, i=i, ps=ps):
                for c in range(8):
                    ins = e.matmul(ps[:, 0:16], hnT[:, c, i * 128:(i + 1) * 128], wf[:, c, :], start=(c == 0), stop=(c == 7))
                return ins
            P.add("pe", mm, r=["hnT%d" % i, "wf"], w=["ps%d" % b])
            P.add("dve", lambda e, i=i, ps=ps: e.tensor_tensor(out=LFv[:, i, :], in0=ps[:, 0:16], in1=self.bf, op=ALU.add),
                  r=["ps%d" % b, "bf"], w=["tmpf0"])
        P.add("act", lambda e: e.activation(out=LF, in_=LF, func=AF.Exp, scale=-1.0), r=["tmpf0"], w=["tmpf0"])
        P.add("act", lambda e: e.activation(out=LF, in_=LF, func=AF.Ln, bias=self.onec), r=["tmpf0", "onec"], w=["tmpf0"])
        pw, pt_ = self.bank(4), self.bank(5)
        P.add("pe", lambda e: e.matmul(pw[:, 0:256], self.trif, LF, start=True, stop=True), r=["trif", "tmpf0"], w=["ps4"])
        P.add("pe", lambda e: e.matmul(pt_[:, 0:256], self.onesf, LF, start=True, stop=True), r=["onesf", "tmpf0"], w=["ps5"])
        PREv = PRE.rearrange("p (i h) -> p i h", i=NT)
        ptv = pt_[:, 0:256].rearrange("p (i h) -> p i h", i=NT)
        P.add("dve", lambda e: e.memset(PREv[:, 0, :], 0.0), w=["tmpf1"])
        for i in range(1, NT):
            P.add("dve", lambda e, i=i: e.tensor_tensor(out=PREv[:, i, :], in0=PREv[:, i - 1, :], in1=ptv[:, i - 1, :], op=ALU.add),
                  r=["tmpf1", "ps5"], w=["tmpf1"])
        P.add("dve", lambda e: e.tensor_tensor(out=CS, in0=pw[:, 0:256], in1=PRE, op=ALU.add), r=["ps4", "tmpf1"], w=["tmpf0"])
        P.add("dve", lambda e: e.tensor_scalar(out=C8, in0=CS, scalar1=-8.0, scalar2=None, op0=ALU.mult), r=["tmpf0"], w=["tmpf0"])
        AUGf = AUG.rearrange("p i h a -> p (i h) a")
        for a in range(3):
            src = C8 if a == 0 else R1
            P.add("dve", lambda e, a=a, src=src: e.tensor_copy(out=AUGf[:, :, a], in_=src), r=["tmpf0"], w=["tmpf1"])
            if a < 2:
                P.add("dve", lambda e, a=a: e.tensor_copy(out=H32, in_=AUGf[:, :, a]), r=["tmpf1"], w=["tmpf1"])
                P.add("dve", lambda e, src=src: e.tensor_tensor(out=R1, in0=src, in1=H32, op=ALU.subtract), r=["tmpf0", "tmpf1"], w=["tmpf0"])
        CSv = CS.rearrange("p (i h) -> p i h", i=NT)
        for hg in range(8):
            k = hg % 2
            for i in range(NT):
                b = 4 + (i % 2)
                ps = self.bank(b)
                psn = ["ps%d" % b]

                def mm(e, i=i, ps=ps, k=k):
                    for c in range(8):
                        ins = e.matmul(ps[:, 0:384], hnT[:, c, i * 128:(i + 1) * 128], wq[k][:, c, :, :], start=(c == 0), stop=(c == 7))
                    return ins
                P.add("pe", mm, r=["hnT%d" % i, "wq%d" % k], w=psn)
                k2 = i % 2
                Q, K = self.Qtok[k2], self.Ktok[k2]
                qn_, kn_ = "qtok%d" % k2, "ktok%d" % k2
                pv = ps[:, 0:384].rearrange("p (s h c) -> p s h c", s=3, h=2)
                P.add("dve", lambda e, Q=Q, pv=pv: e.tensor_copy(out=Q[:, :, 0:64], in_=pv[:, 0, :, :]), r=psn, w=[qn_])
                P.add("dve", lambda e, Q=Q, i=i, hg=hg: e.tensor_copy(out=Q[:, :, 64:67], in_=AUG[:, i, 2 * hg:2 * hg + 2, :]), r=["tmpf1"], w=[qn_])
                P.add("dve", lambda e, K=K, pv=pv: e.tensor_copy(out=K[:, :, 0:64], in_=pv[:, 1, :, :]), r=psn, w=[kn_])
                P.add("dve", lambda e, pv=pv, i=i: e.tensor_copy(out=V[:, i, :, 0:64], in_=pv[:, 2, :, :]), r=psn, w=["V"])
                self.qk_transpose(i, 67, qT, kT, i == NT - 1)
            if hg + 2 < 8:
                load_wq(hg + 2)
            self.attention(hg, qT, kT, V, 67, 0.125, lambda jj, h: CSv[:, jj, h:h + 1], ["tmpf0"], Otok)
        self.out_proj(Otok, wo)

def _t5_bucket_const(dist):
    max_exact = 16
    n = np.maximum(dist.astype(np.float32), np.float32(1.0))
    large = max_exact + (np.log(n / np.float32(max_exact)) / np.float32(math.log(2048 / max_exact))
                         * np.float32(32 - max_exact)).astype(np.int32)
    large = np.minimum(large, 31)
    return np.where(dist < max_exact, dist, large)


def host_inputs(inputs, b):
    f32 = np.float32
    m = {}
    m["x"] = np.ascontiguousarray(inputs["x"][b])
    m["pT"] = np.ascontiguousarray(np.transpose(inputs["p"][:, b], (0, 2, 1)))
    m["pos"] = np.ascontiguousarray(inputs["positions"][b].reshape(NT, 128).T.astype(np.int32))
    g = inputs["norm_g"]
    gpre = np.stack([g[:, 0], g[:, 2]], axis=1)
    m["gpre"] = np.ascontiguousarray(gpre.reshape(DEPTH * 2, 8, 128).transpose(2, 0, 1).reshape(128, DEPTH * 2 * 8))
    m["gpost"] = np.ascontiguousarray(np.stack([g[:, 1], g[:, 3]], axis=1).reshape(DEPTH * 2, D))
    for k in ("ffn_w_in", "ffn_w_out", "ple_w_proj", "ple_w_gate", "mla_w_a", "mla_w_uq", "mla_w_ukv", "mla_w_o",
              "dil_w_qkv", "dil_w_o", "fox_w_qkvf", "fox_b_f", "fox_w_o"):
        m[k] = inputs[k]
    m["mla_qn"] = np.ascontiguousarray(inputs["mla_q_norm"].reshape(2, 3, 128).transpose(2, 0, 1).reshape(128, 6))
    m["mla_kvn"] = np.ascontiguousarray(inputs["mla_kv_norm"].reshape(2, 2, 128).transpose(2, 0, 1).reshape(128, 4))
    sidx = np.arange(128)[:, None]
    tidx = np.arange(256)[None, :]
    rel = tidx - sidx
    valid = (rel >= 0) & (rel <= 128)
    strips = np.full((48, 128, 256), -30000.0, f32)
    rb = inputs["rel_bias"]
    for g_, d_ in enumerate((1, 4, 16)):
        bucket = _t5_bucket_const(np.clip(rel, 0, None) * d_)
        for h_ in range(16):
            col = g_ * 16 + h_
            strips[col] = np.where(valid, rb[bucket, col], f32(-30000.0))
    m["dil_strip"] = strips
    m["ident"] = np.eye(128).astype(ml_dtypes.bfloat16)
    tri = (np.arange(128)[:, None] <= np.arange(128)[None, :])
    m["trif"] = tri.astype(f32)
    inv = (np.float32(10000.0) ** (-np.arange(16, dtype=f32) / np.float32(16))).astype(f32)
    m["invf"] = np.ascontiguousarray(np.broadcast_to(inv[None, :], (128, 16))).astype(f32)
    return m


_NC_CACHE = {}


def get_nc(n_layers=DEPTH, mixers=(0, 1, 2)):
    key = (n_layers, tuple(mixers))
    if key not in _NC_CACHE:
        _NC_CACHE[key] = Builder(n_layers, mixers).build()
    return _NC_CACHE[key]


def kernel(**inputs):
    inputs = {k: np.asarray(v) for k, v in inputs.items()}
    nc = get_nc()
    in_maps = [host_inputs(inputs, b) for b in range(8)]
    res = run_bass_kernel_spmd(nc, in_maps, core_ids=list(range(8)))
    out = np.stack([np.asarray(r["out"]) for r in res.results], axis=0)
    return out.astype(np.float32)
```

```python
import contextlib
import math
import numpy as np
import ml_dtypes
import concourse.bass as bass
import concourse.mybir as mybir
from concourse.bass_utils import run_bass_kernel_spmd

F32 = mybir.dt.float32
BF16 = mybir.dt.bfloat16
I32 = mybir.dt.int32
AF = mybir.ActivationFunctionType
ALU = mybir.AluOpType

D = 1024
T = 2048
NT = 16
DFF = 2816
NF = 22
DEPTH = 4
EPS = 1e-6
TWO_PI = 2.0 * math.pi


def _size(dt):
    return mybir.dt.size(dt)


class Op:
    __slots__ = ("eng", "fn", "deps", "dma", "signal", "count", "dsem", "dval", "idx")


class Prog:
    def __init__(self):
        self.q = {"pe": [], "act": [], "dve": [], "pool": [], "sp": []}
        self.state = {}
        self.ndma = {"pool": 0, "sp": 0}
        self.NDSEM = 8

    def _st(self, name):
        st = self.state.get(name)
        if st is None:
            st = self.state[name] = [None, {}, []]
        return st

    def inherit(self, new, olds):
        ops = {}
        for n in olds:
            st = self.state.get(n)
            if st is None:
                continue
            for o in [st[0]] + list(st[1].values()) + list(st[2]):
                if o is not None:
                    ops[id(o)] = o
        st = self._st(new)
        for o in ops.values():
            st[2].append(o)

    def add(self, eng, fn, r=(), w=(), dma=False):
        op = Op()
        op.eng, op.fn, op.dma, op.signal = eng, fn, dma, False
        deps = {}

        def dep(o):
            if o is None or o is op:
                return
            deps[id(o)] = o

        for name in r:
            st = self._st(name)
            dep(st[0])
            if name.startswith("ps"):
                for e2, o2 in st[1].items():
                    if e2 != eng:
                        dep(o2)
        for name in w:
            st = self._st(name)
            dep(st[0])
            for o2 in st[1].values():
                dep(o2)
            for o2 in st[2]:
                dep(o2)
        for name in r:
            st = self.state[name]
            if dma:
                st[2].append(op)
            else:
                st[1][eng] = op
        for name in w:
            self.state[name] = [op, {}, []]
        dl = []
        best = {}
        for o in deps.values():
            if o.dma:
                dl.append(o)
                continue
            if o.eng == eng and not dma and eng == "pe":
                continue
            b = best.get(o.eng)
            if b is None or o.idx > b.idx:
                best[o.eng] = o
        dl.extend(best.values())
        op.deps = dl
        op.idx = len(self.q[eng])
        if dma:
            k = self.ndma[eng]
            self.ndma[eng] += 1
            op.dsem = k % self.NDSEM
            op.dval = 16 * (k // self.NDSEM + 1)
        self.q[eng].append(op)
        return op

    def finalize(self):
        for e in self.q:
            for op in self.q[e]:
                for d in op.deps:
                    if not d.dma:
                        d.signal = True
        for e in self.q:
            c = 0
            for op in self.q[e]:
                if op.dma:
                    continue
                if op.signal:
                    c += 1
                    op.count = c

    def emit(self, e, engine, sems, dsems):
        known = {}

        def wait(key, sem, val):
            if known.get(key, 0) >= val:
                return
            known[key] = val
            engine.wait_ge(sem, val)

        for op in self.q[e]:
            for d in op.deps:
                if d.dma:
                    wait(("d", d.eng, d.dsem), dsems[d.eng][d.dsem], d.dval)
                else:
                    wait(("c", d.eng), sems[d.eng], d.count)
            if op.dma:
                if op.dval > 16:
                    wait(("d", e, op.dsem), dsems[e][op.dsem], op.dval - 16)
                ins = op.fn(engine)
                ins.then_inc(dsems[e][op.dsem], 16)
            else:
                ins = op.fn(engine)
                if op.signal:
                    ins.then_inc(sems[e], 1)
        if e in self.ndma:
            for k in range(self.NDSEM):
                n = (self.ndma[e] - k + self.NDSEM - 1) // self.NDSEM
                if n > 0:
                    engine.wait_ge(dsems[e][k], 16 * n)


def sub(handle, dt, off_bytes, shape):
    es = _size(dt)
    h = handle.bitcast(dt)
    n = int(np.prod(shape))
    assert off_bytes % es == 0
    o = off_bytes // es
    ap = h[:, o:o + n]
    if len(shape) == 2:
        ap = ap.rearrange("p (a b) -> p a b", a=shape[0], b=shape[1])
    elif len(shape) == 3:
        ap = ap.rearrange("p (a b c) -> p a b c", a=shape[0], b=shape[1], c=shape[2])
    return ap


class Region:
    def __init__(self, P, handle, nbytes):
        self.P, self.h, self.size = P, handle, nbytes
        self.cur, self.old, self.off = [], [], 0

    def reset(self):
        self.old += self.cur
        self.cur = []
        self.off = 0

    def take(self, names, dt, shape):
        if isinstance(names, str):
            names = [names]
        n = int(np.prod(shape)) * _size(dt)
        n = (n + 3) // 4 * 4
        off = self.off
        self.off += n
        assert self.off <= self.size, (names, self.off, self.size)
        for nm in names:
            self.cur.append(nm)
            self.P.inherit(nm, self.old)
        return sub(self.h, dt, off, shape)


class Builder:
    def __init__(self, n_layers=DEPTH, mixers=(0, 1, 2), types=(0, 1, 2, 0)):
        self.n_layers = n_layers
        self.mixers = mixers
        self.types = list(types)
        self.cctr = 0
        self.sctr = 0
        self.stop = 0

    def build(self):
        nc = bass.Bass("TRN2", target_bir_lowering=False)
        self.nc = nc
        P = self.P = Prog()
        dr = {}

        def din(name, shape, dt=F32):
            dr[name] = nc.dram_tensor(name, list(shape), dt, kind="ExternalInput").ap()

        din("x", [T, D])
        din("pT", [DEPTH, 256, T])
        din("pos", [128, NT], I32)
        din("gpre", [128, DEPTH * 2 * 8])
        din("gpost", [DEPTH * 2, D])
        din("ffn_w_in", [DEPTH, D, 2 * DFF])
        din("ffn_w_out", [DEPTH, DFF, D])
        din("ple_w_proj", [DEPTH, 256, D])
        din("ple_w_gate", [DEPTH, D, D])
        din("mla_w_a", [2, D, 672])
        din("mla_qn", [128, 6])
        din("mla_kvn", [128, 4])
        din("mla_w_uq", [2, 384, 1536])
        din("mla_w_ukv", [2, 256, 2048])
        din("mla_w_o", [2, D, D])
        din("dil_w_qkv", [1, D, 9216])
        din("dil_w_o", [1, D, D])
        din("dil_strip", [48, 128, 256])
        din("fox_w_qkvf", [1, D, 3088])
        din("fox_b_f", [1, 16])
        din("fox_w_o", [1, D, D])
        din("ident", [128, 128], BF16)
        din("trif", [128, 128])
        din("invf", [128, 16])
        dr["out"] = nc.dram_tensor("out", [T, D], F32, kind="ExternalOutput").ap()
        self.dr = dr

        with (
            nc.sbuf_tensor("H", [128, NT, D], F32) as H,
            nc.sbuf_tensor("X", [128, 16384], BF16) as X,
            nc.sbuf_tensor("Y", [128, 22528], BF16) as Y,
            nc.sbuf_tensor("W", [128, 22528], BF16) as W,
            nc.sbuf_tensor("C", [128, 6100], F32) as C,
            nc.psum_tensor("PA", [128, 1024], F32) as PA,
            nc.psum_tensor("PB", [128, 1024], F32) as PB,
            nc.psum_tensor("PC", [128, 1024], F32) as PC,
            nc.psum_tensor("PD", [128, 1024], F32) as PD,
        ):
            self.H = H
            self.RX = Region(P, X, 32768)
            self.RY = Region(P, Y, 45056)
            self.RW = Region(P, W, 45056)
            self.RC = Region(P, C, 24400)
            self.PS = [PA, PB, PC, PD]
            self.setup_consts()
            self.load_x()
            for l in range(self.n_layers):
                self.layer(l)
            self.store_out()
            P.finalize()

            with contextlib.ExitStack() as es:
                sems = {e: es.enter_context(nc.semaphore("s_" + e)) for e in ("pe", "act", "dve", "pool", "sp")}
                dsems = {e: [es.enter_context(nc.semaphore("d_%s%d" % (e, k))) for k in range(P.NDSEM)]
                         for e in ("pool", "sp")}
                block = es.enter_context(nc.Block())

                @block.tensor
                def _(e):
                    P.emit("pe", e, sems, dsems)

                @block.scalar
                def _(e):
                    P.emit("act", e, sems, dsems)

                @block.vector
                def _(e):
                    P.emit("dve", e, sems, dsems)

                @block.gpsimd
                def _(e):
                    P.emit("pool", e, sems, dsems)

                @block.sync
                def _(e):
                    P.emit("sp", e, sems, dsems)
        return nc

    def bank(self, b):
        return self.PS[b // 2][:, (b % 2) * 512:(b % 2) * 512 + 512]

    def bank_bf(self, b):
        h = self.PS[b // 2].bitcast(BF16)
        return h[:, (b % 2) * 1024:(b % 2) * 1024 + 1024]

    def pair(self, q):
        return self.PS[q][:, :]

    def setup_consts(self):
        P, dr, RC = self.P, self.dr, self.RC
        self.ident = RC.take("ident", BF16, [128])
        self.negm = RC.take("negm", F32, [128])
        self.trif = RC.take("trif", F32, [128])
        self.onesf = RC.take("onesf", F32, [128])
        self.gpre = RC.take("gpre", F32, [DEPTH * 2 * 8])
        self.invf = RC.take("invf", F32, [16])
        self.posi = RC.take("posi", I32, [NT])
        self.qn = RC.take("qn", F32, [6])
        self.kvn = RC.take("kvn", F32, [4])
        self.bf = RC.take("bf", F32, [16])
        self.ss = RC.take(["ss0", "ss1", "pss0", "pss1"], F32, [4])
        self.rs = RC.take(["rs0", "rs1", "prs0", "prs1"], F32, [4])
        self.st = RC.take("st", F32, [16])
        self.cos2 = RC.take("cos2", F32, [NT, 32])
        self.sin2 = RC.take("sin2", F32, [NT, 32])
        self.gpost = RC.take("gpost", F32, [D])
        self.hb = [RC.take("hb%d" % k, BF16, [D]) for k in range(2)]
        self.tmpf = [RC.take("tmpf%d" % k, F32, [D]) for k in range(2)]
        self.krope = RC.take("krope", BF16, [NT, 32])
        self.ropes = [RC.take(nm, F32, [2, 32]) for nm in ("ropea", "ropeb")]
        for nm, ap in (("ident", self.ident), ("trif", self.trif), ("gpre", self.gpre),
                       ("invf", self.invf), ("qn", self.qn), ("kvn", self.kvn)):
            src = {"qn": "mla_qn", "kvn": "mla_kvn"}.get(nm, nm)
            P.add("sp", lambda e, ap=ap, src=src: e.dma_start(out=ap, in_=dr[src]), w=[nm], dma=True)
        P.add("sp", lambda e: e.dma_start(out=self.posi, in_=dr["pos"]), w=["posi"], dma=True)
        P.add("sp", lambda e: e.dma_start(out=self.bf.unsqueeze(1), in_=dr["fox_b_f"].partition_broadcast(128)),
              w=["bf"], dma=True)
        P.add("dve", lambda e: e.memset(self.onesf, 1.0), w=["onesf"])
        P.add("dve", lambda e: e.tensor_scalar(out=self.negm, in0=self.trif, scalar1=240000.0, scalar2=-240000.0, op0=ALU.mult, op1=ALU.add),
              r=["trif"], w=["negm"])
        self.rope_tables()

    def rope_tables(self):
        P = self.P
        ang = self.tmpf[0][:, 0:256].rearrange("p (i f) -> p i f", i=NT)
        kf = self.tmpf[0][:, 256:512].rearrange("p (i f) -> p i f", i=NT)
        ki = self.tmpf[0][:, 512:768].bitcast(I32).rearrange("p (i f) -> p i f", i=NT)
        m = self.tmpf[0][:, 768:1024].rearrange("p (i f) -> p i f", i=NT)
        posf = self.tmpf[1][:, 0:NT]
        ab = self.tmpf[1][:, 256:512].rearrange("p (i f) -> p i f", i=NT)
        r = ["tmpf0"]
        P.add("dve", lambda e: e.tensor_copy(out=posf, in_=self.posi), r=["posi"], w=["tmpf1"])
        P.add("dve", lambda e: e.tensor_tensor(out=ang, in0=posf.unsqueeze(2).broadcast_to([128, NT, 16]),
                                               in1=self.invf.unsqueeze(1).broadcast_to([128, NT, 16]), op=ALU.mult),
              r=["tmpf1", "invf"], w=r)
        P.add("dve", lambda e: e.tensor_scalar(out=kf, in0=ang, scalar1=1.0 / TWO_PI, scalar2=None, op0=ALU.mult), r=r, w=r)
        P.add("dve", lambda e: e.tensor_copy(out=ki, in_=kf), r=r, w=r)
        P.add("dve", lambda e: e.tensor_copy(out=kf, in_=ki), r=r, w=r)
        C1 = 6.28125
        C2 = TWO_PI - C1
        P.add("dve", lambda e: e.scalar_tensor_tensor(out=ang, in0=kf, scalar=-C1, in1=ang, op0=ALU.mult, op1=ALU.add), r=r, w=r)
        P.add("dve", lambda e: e.scalar_tensor_tensor(out=ang, in0=kf, scalar=-C2, in1=ang, op0=ALU.mult, op1=ALU.add), r=r, w=r)
        P.add("dve", lambda e: e.tensor_scalar(out=m, in0=ang, scalar1=math.pi, scalar2=-TWO_PI, op0=ALU.is_gt, op1=ALU.mult), r=r, w=r)
        P.add("dve", lambda e: e.tensor_tensor(out=ang, in0=ang, in1=m, op=ALU.add), r=r, w=r)
        P.add("dve", lambda e: e.tensor_scalar(out=m, in0=ang, scalar1=-math.pi, scalar2=TWO_PI, op0=ALU.is_lt, op1=ALU.mult), r=r, w=r)
        P.add("dve", lambda e: e.tensor_tensor(out=ang, in0=ang, in1=m, op=ALU.add), r=r, w=r)
        P.add("dve", lambda e: e.tensor_scalar(out=ang, in0=ang, scalar1=math.pi, scalar2=-math.pi, op0=ALU.min, op1=ALU.max), r=r, w=r)
        P.add("dve", lambda e: e.tensor_scalar(out=ab, in0=ang, scalar1=-1.0, scalar2=None, op0=ALU.mult), r=r, w=["tmpf1"])
        P.add("dve", lambda e: e.tensor_tensor(out=ab, in0=ab, in1=ang, op=ALU.max), r=r + ["tmpf1"], w=["tmpf1"])
        P.add("act", lambda e: e.activation(out=self.sin2[:, :, 16:32], in_=ang, func=AF.Sin), r=r, w=["sin2"])
        hp = self.st[:, 0:1]
        P.add("dve", lambda e: e.memset(hp, math.pi / 2), w=["st"])
        P.add("act", lambda e: e.activation(out=self.cos2[:, :, 0:16], in_=ab, func=AF.Sin, scale=-1.0, bias=hp),
              r=["tmpf1", "st"], w=["cos2"])
        P.add("dve", lambda e: e.tensor_copy(out=self.cos2[:, :, 16:32], in_=self.cos2[:, :, 0:16]), r=["cos2"], w=["cos2"])
        P.add("dve", lambda e: e.tensor_scalar(out=self.sin2[:, :, 0:16], in0=self.sin2[:, :, 16:32], scalar1=-1.0, scalar2=None, op0=ALU.mult),
              r=["sin2"], w=["sin2"])

    def load_x(self):
        P, dr = self.P, self.dr
        xv = dr["x"].rearrange("(i p) d -> p i d", p=128)
        for i0 in range(0, NT, 4):
            P.add("sp", lambda e, i0=i0: e.dma_start(out=self.H[:, i0:i0 + 4, :], in_=xv[:, i0:i0 + 4, :]),
                  w=["H%d" % i for i in range(i0, i0 + 4)], dma=True)

    def store_out(self):
        P, dr = self.P, self.dr
        ov = dr["out"].rearrange("(i p) d -> p i d", p=128)
        for i0 in range(0, NT, 4):
            P.add("sp", lambda e, i0=i0: e.dma_start(out=ov[:, i0:i0 + 4, :], in_=self.H[:, i0:i0 + 4, :]),
                  r=["H%d" % i for i in range(i0, i0 + 4)], w=["out%d" % i0], dma=True)

    def wload(self, dst, src, wname):
        self.P.add("pool", lambda e: e.dma_start(out=dst, in_=src), w=[wname], dma=True)

    def rstd(self, ss_ap, out_ap, inv_n, rn, wn):
        P = self.P
        P.add("act", lambda e: e.activation(out=out_ap, in_=ss_ap, func=AF.Sqrt, scale=inv_n, bias=self.epsc), r=[rn, "epsc"], w=[wn])
        P.add("dve", lambda e: e.reciprocal(out=out_ap, in_=out_ap), r=[wn], w=[wn])

    def prenorm_T(self, l, which, tiles, dst, dst_names, normalize=True):
        P, H = self.P, self.H
        for k, i in enumerate(tiles):
            kk = k % 2
            hb, hbn = self.hb[kk], "hb%d" % kk
            ssn, rsn = "ss%d" % kk, "rs%d" % kk
            ss = self.ss[:, kk:kk + 1]
            rs = self.rs[:, kk:kk + 1]
            hn = "H%d" % i
            if normalize:
                P.add("act", lambda e, i=i, ss=ss, hb=hb: e.activation(out=hb, in_=H[:, i, :], func=AF.Square, accum_out=ss),
                      r=[hn], w=[hbn, ssn])
                self.rstd(ss, rs, 1.0 / D, ssn, rsn)
                P.add("dve", lambda e, i=i, rs=rs, hb=hb: e.tensor_scalar(out=hb, in0=H[:, i, :], scalar1=rs, scalar2=None, op0=ALU.mult),
                      r=[hn, rsn], w=[hbn])
            else:
                P.add("act", lambda e, i=i, hb=hb: e.activation(out=hb, in_=H[:, i, :], func=AF.Copy), r=[hn], w=[hbn])
            b = 6 + kk
            pb = self.bank_bf(b)

            def tr(e, hb=hb, pb=pb):
                for c in range(8):
                    ins = e.transpose(pb[:, c * 128:(c + 1) * 128], hb[:, c * 128:(c + 1) * 128], self.ident)
                return ins
            P.add("pe", tr, r=[hbn, "ident"], w=["ps%d" % b])
            d = dst[:, :, k * 128:(k + 1) * 128]
            src = pb.rearrange("p (c t) -> p c t", c=8)
            dn = [dst_names[k]] if isinstance(dst_names, list) else [dst_names]
            if normalize:
                g = self.gpre[:, (l * 2 + which) * 8:(l * 2 + which) * 8 + 8]
                gb = g.unsqueeze(2).broadcast_to([128, 8, 128])
                P.add("dve", lambda e, d=d, src=src, gb=gb: e.tensor_tensor(out=d, in0=src, in1=gb, op=ALU.mult),
                      r=["ps%d" % b, "gpre"], w=dn)
            else:
                P.add("dve", lambda e, d=d, src=src: e.tensor_copy(out=d, in_=src), r=["ps%d" % b], w=dn)

    def load_gpost(self, l, which):
        P, dr = self.P, self.dr
        row = 2 * l + which
        P.add("sp", lambda e: e.dma_start(out=self.gpost.unsqueeze(1), in_=dr["gpost"][row:row + 1, :].partition_broadcast(128)),
              w=["gpost"], dma=True)

    def postnorm_res(self, q, i):
        P = self.P
        y = self.pair(q)
        pn = ["ps%d" % (2 * q), "ps%d" % (2 * q + 1)]
        k = i % 2
        ss = self.ss[:, 2 + k:3 + k]
        rs = self.rs[:, 2 + k:3 + k]
        ssn, rsn = "pss%d" % k, "prs%d" % k
        tmp, tn = self.tmpf[k], "tmpf%d" % k
        hn = "H%d" % i
        P.add("act", lambda e: e.activation(out=tmp, in_=y, func=AF.Square, accum_out=ss), r=pn, w=[tn, ssn])
        self.rstd(ss, rs, 1.0 / D, ssn, rsn)
        P.add("dve", lambda e: e.scalar_tensor_tensor(out=tmp, in0=y, scalar=rs, in1=self.gpost, op0=ALU.mult, op1=ALU.mult),
              r=pn + [rsn, "gpost"], w=[tn])
        P.add("pool", lambda e: e.tensor_tensor(out=self.H[:, i, :], in0=self.H[:, i, :], in1=tmp, op=ALU.add),
              r=[tn, hn], w=[hn])

    def layer(self, l):
        P = self.P
        if l == 0:
            self.epsc = self.st[:, 1:2]
            P.add("dve", lambda e: e.memset(self.epsc, EPS), w=["epsc"])
            self.onec = self.st[:, 2:3]
            P.add("dve", lambda e: e.memset(self.onec, 1.0), w=["onec"])
        m = self.types[l]
        j = sum(1 for t in self.types[:l] if t == m)
        if m in self.mixers:
            self.load_gpost(l, 0)
            if m == 0:
                self.mla(l, j)
            elif m == 1:
                self.dil(l, j)
            else:
                self.fox(l, j)
        self.load_gpost(l, 1)
        for blk in range(2):
            self.ffn_ple(l, blk)

    def ffn_ple(self, l, blk):
        P, dr = self.P, self.dr
        RX, RY, RW = self.RX, self.RY, self.RW
        TB = 1024
        tiles = list(range(blk * 8, blk * 8 + 8))
        RX.reset()
        RY.reset()
        RW.reset()
        hnT = RX.take("hnT", BF16, [8, TB])
        win = [RX.take("win%d" % k, BF16, [8, 2, 256]) for k in range(2)]
        actT = RY.take(["act%d" % f for f in range(NF)], BF16, [NF, TB])
        wout = RW.take("wout", BF16, [NF, D])
        w_in = dr["ffn_w_in"][l]
        NG = NF // 2

        def load_win(g):
            k = g % 2
            for s in range(2):
                src = w_in[:, s * DFF + g * 256: s * DFF + g * 256 + 256].rearrange("(c p) n -> p c n", p=128)
                self.wload(win[k][:, :, s, :], src, "win%d" % k)

        load_win(0)
        load_win(1)
        wo = dr["ffn_w_out"][l].rearrange("(f p) n -> p f n", p=128)
        for f0 in range(0, NF, 11):
            self.wload(wout[:, f0:f0 + 11, :], wo[:, f0:f0 + 11, :], "wout")
        self.prenorm_T(l, 1, tiles, hnT, "hnT")
        step = 0
        for g in range(NG):
            k = g % 2
            for ff in range(2):
                f = 2 * g + ff
                for tc in range(2):
                    bg, bu = (0, 1) if step % 2 == 0 else (2, 3)
                    step += 1
                    pg, pu = self.bank(bg), self.bank(bu)

                    def mm(e, k=k, ff=ff, tc=tc, pg=pg, pu=pu):
                        for s, po in ((0, pg), (1, pu)):
                            for c in range(8):
                                ins = e.matmul(po, win[k][:, c, s, ff * 128:(ff + 1) * 128], hnT[:, c, tc * 512:(tc + 1) * 512],
                                               start=(c == 0), stop=(c == 7))
                        return ins
                    P.add("pe", mm, r=["win%d" % k, "hnT"], w=["ps%d" % bg, "ps%d" % bu])
                    sg = self.hb[step % 2][:, 0:512]
                    sgn = "hb%d" % (step % 2)
                    P.add("act", lambda e, sg=sg, pg=pg: e.activation(out=sg, in_=pg, func=AF.Silu), r=["ps%d" % bg], w=[sgn])
                    P.add("dve", lambda e, sg=sg, pu=pu, f=f, tc=tc: e.tensor_tensor(out=actT[:, f, tc * 512:(tc + 1) * 512], in0=sg, in1=pu, op=ALU.mult),
                          r=[sgn, "ps%d" % bu], w=["act%d" % f])
            if g + 2 < NG:
                load_win(g + 2)
        for k, i in enumerate(tiles):
            q = k % 2
            y0, y1 = self.bank(2 * q), self.bank(2 * q + 1)

            def mm2(e, k=k, y0=y0, y1=y1):
                for hf, po in ((0, y0), (1, y1)):
                    for f in range(NF):
                        ins = e.matmul(po, actT[:, f, k * 128:(k + 1) * 128], wout[:, f, hf * 512:(hf + 1) * 512],
                                       start=(f == 0), stop=(f == NF - 1))
                return ins
            P.add("pe", mm2, r=["act%d" % f for f in range(NF)] + ["wout"], w=["ps%d" % (2 * q), "ps%d" % (2 * q + 1)])
            self.postnorm_res(q, i)
        RX.reset()
        RY.reset()
        hT = RX.take("hT", BF16, [8, TB])
        pT = RX.take("pTb", BF16, [2, TB])
        wp = RX.take("wproj", BF16, [2, D])
        wg = RY.take("wgate", BF16, [8, D])
        self.wload(pT, dr["pT"][l][:, blk * TB:(blk + 1) * TB].rearrange("(c p) t -> p c t", p=128), "pTb")
        self.wload(wp, dr["ple_w_proj"][l].rearrange("(c p) n -> p c n", p=128), "wproj")
        self.wload(wg, dr["ple_w_gate"][l].rearrange("(c p) n -> p c n", p=128), "wgate")
        self.prenorm_T(l, 0, tiles, hT, "hT", normalize=False)
        for k, i in enumerate(tiles):
            qg, qp = (0, 1) if k % 2 == 0 else (2, 1)
            qg = 0 if k % 2 == 0 else 2
            qp = 1
            gps, pps = self.pair(qg), self.pair(qp)

            def mmg(e, k=k, qg=qg):
                for hf in range(2):
                    po = self.bank(2 * qg + hf)
                    for c in range(8):
                        ins = e.matmul(po, hT[:, c, k * 128:(k + 1) * 128], wg[:, c, hf * 512:(hf + 1) * 512], start=(c == 0), stop=(c == 7))
                return ins

            def mmp(e, k=k, qp=qp):
                for hf in range(2):
                    po = self.bank(2 * qp + hf)
                    for c in range(2):
                        ins = e.matmul(po, pT[:, c, k * 128:(k + 1) * 128], wp[:, c, hf * 512:(hf + 1) * 512], start=(c == 0), stop=(c == 1))
                return ins
            P.add("pe", mmg, r=["hT", "wgate"], w=["ps%d" % (2 * qg), "ps%d" % (2 * qg + 1)])
            P.add("pe", mmp, r=["pTb", "wproj"], w=["ps%d" % (2 * qp), "ps%d" % (2 * qp + 1)])
            tmp, tn = self.tmpf[k % 2], "tmpf%d" % (k % 2)
            hn = "H%d" % i
            P.add("act", lambda e, tmp=tmp, gps=gps: e.activation(out=tmp, in_=gps, func=AF.Sigmoid),
                  r=["ps%d" % (2 * qg), "ps%d" % (2 * qg + 1)], w=[tn])
            P.add("dve", lambda e, tmp=tmp, pps=pps: e.tensor_tensor(out=tmp, in0=tmp, in1=pps, op=ALU.mult),
                  r=[tn, "ps%d" % (2 * qp), "ps%d" % (2 * qp + 1)], w=[tn])
            P.add("pool", lambda e, tmp=tmp, i=i: e.tensor_tensor(out=self.H[:, i, :], in0=self.H[:, i, :], in1=tmp, op=ALU.add),
                  r=[tn, hn], w=[hn])

    def qk_scratch(self, R, dkp):
        self.Qtok = [R.take("qtok%d" % k, BF16, [2, dkp]) for k in range(2)]
        self.Ktok = [R.take("ktok%d" % k, BF16, [2, dkp]) for k in range(2)]

    def qk_transpose(self, i, dk, qT, kT, last):
        P = self.P
        k2 = i % 2
        b = 6 + ((i // 2) % 2)
        pb = self.bank_bf(b).rearrange("p (s t) -> p s t", s=4)
        Q, K = self.Qtok[k2], self.Ktok[k2]

        def tr(e):
            for s, src in enumerate((Q[:, 0, 0:dk], Q[:, 1, 0:dk], K[:, 0, 0:dk], K[:, 1, 0:dk])):
                ins = e.transpose(pb[0:dk, s, (i % 2) * 128:(i % 2) * 128 + 128], src, self.ident)
            return ins
        P.add("pe", tr, r=["qtok%d" % k2, "ktok%d" % k2, "ident"], w=["ps%d" % b])
        if i % 2 == 1:
            i0 = i - 1
            for s, dst, nm in ((0, qT[0:dk, 0, i0 * 128:i0 * 128 + 256], "qT0"), (1, qT[0:dk, 1, i0 * 128:i0 * 128 + 256], "qT1"),
                               (2, kT[0:dk, 0, i0 * 128:i0 * 128 + 256], "kT0"), (3, kT[0:dk, 1, i0 * 128:i0 * 128 + 256], "kT1")):
                P.add("dve", lambda e, dst=dst, s=s, pb=pb: e.tensor_copy(out=dst, in_=pb[0:dk, s, :]), r=["ps%d" % b], w=[nm])

    def attention(self, hg, qT, kT, V, dk, scale, bias_fn, bias_names, Otok):
        for hh in range(2):
            for qc in range(4):
                self.attn_chunk(hg * 2 + hh, hh, qc, qT, kT, V, dk, scale, bias_fn, bias_names, Otok)

    def attn_chunk(self, h, hh, qc, qT, kT, V, dk, scale, bias_fn, bias_names, Otok):
        P = self.P
        accb = 2 + (self.cctr % 2)
        rcs = self.cctr % 2
        self.cctr += 1
        acc = self.bank(accb)
        first = [True]

        def do_S(j, off, ncols):
            sb = self.sctr % 2
            pt = self.sctr % len(self.PT)
            self.sctr += 1
            S = self.bank(sb)
            PTb = self.PT[pt]
            P.add("pe", lambda e: e.matmul(S[:, 0:ncols], kT[0:dk, hh, j * 128:(j + 1) * 128],
                                           qT[0:dk, hh, qc * 512 + off:(qc + 1) * 512], start=True, stop=True),
                  r=["kT%d" % hh, "qT%d" % hh], w=["ps%d" % sb])
            if j >= 4 * qc:
                P.add("dve", lambda e: e.tensor_tensor(out=S[:, 0:128], in0=S[:, 0:128], in1=self.negm, op=ALU.add),
                      r=["negm"], w=["ps%d" % sb])
            bias = bias_fn(j, h) if bias_fn is not None else None
            if bias is None:
                P.add("act", lambda e: e.activation(out=PTb[:, 0:ncols], in_=S[:, 0:ncols], func=AF.Exp, scale=scale),
                      r=["ps%d" % sb], w=["pt%d" % pt])
            else:
                P.add("act", lambda e: e.activation(out=PTb[:, 0:ncols], in_=S[:, 0:ncols], func=AF.Exp, scale=scale, bias=bias),
                      r=["ps%d" % sb] + bias_names, w=["pt%d" % pt])
            return pt

        def do_PV(j, off, ncols, pt):
            PTb = self.PT[pt]

            def mm(e):
                for ql in range(off // 128, 4):
                    st = first[0]
                    first[0] = False
                    ins = e.matmul(acc[:, ql * 128:ql * 128 + 65], PTb[:, ql * 128 - off:ql * 128 - off + 128],
                                   V[:, j, hh, 0:65], start=st, stop=(j == 4 * qc + ql), skip_group_check=True)
                return ins
            P.add("pe", mm, r=["pt%d" % pt, "V"], w=["ps%d" % accb])

        prev = None
        for j in range(4 * qc + 4):
            off = max(0, j - 4 * qc) * 128
            pt = do_S(j, off, 512 - off)
            if prev is not None:
                do_PV(*prev)
            prev = (j, off, 512 - off, pt)
        do_PV(*prev)
        rc = self.st[:, 4 + 4 * rcs:8 + 4 * rcs]
        rcn = "rc%d" % rcs
        accv = acc.rearrange("p (q c) -> p q c", q=4)
        P.add("dve", lambda e: e.reciprocal(out=rc, in_=accv[:, :, 64]), r=["ps%d" % accb], w=[rcn])
        P.add("dve", lambda e: e.tensor_tensor(out=Otok[:, 4 * qc:4 * qc + 4, h * 64:(h + 1) * 64], in0=accv[:, :, 0:64],
                                               in1=rc.unsqueeze(2).broadcast_to([128, 4, 64]), op=ALU.mult),
              r=["ps%d" % accb, rcn], w=["Otok"])

    def out_proj(self, Otok, wo):
        P = self.P
        for i in range(NT):
            kk = i % 2
            b = 6 + kk
            pb = self.bank_bf(b)
            oT, oTn = self.hb[kk], "hb%d" % kk

            def tr(e, i=i, pb=pb):
                for c in range(8):
                    ins = e.transpose(pb[:, c * 128:(c + 1) * 128], Otok[:, i, c * 128:(c + 1) * 128], self.ident)
                return ins
            P.add("pe", tr, r=["Otok", "ident"], w=["ps%d" % b])
            P.add("dve", lambda e, oT=oT, pb=pb: e.tensor_copy(out=oT, in_=pb), r=["ps%d" % b], w=[oTn])
            q = kk

            def mm(e, oT=oT, q=q):
                for hf in range(2):
                    po = self.bank(2 * q + hf)
                    for c in range(8):
                        ins = e.matmul(po, oT[:, c * 128:(c + 1) * 128], wo[:, c, hf * 512:(hf + 1) * 512], start=(c == 0), stop=(c == 7))
                return ins
            P.add("pe", mm, r=[oTn, "wo"], w=["ps%d" % (2 * q), "ps%d" % (2 * q + 1)])
            self.postnorm_res(q, i)

    def mla(self, l, j):
        P, dr = self.P, self.dr
        RX, RY, RW = self.RX, self.RY, self.RW
        RX.reset(); RY.reset(); RW.reset()
        hnT = RX.take(["hnT%d" % i for i in range(NT)], BF16, [8, T])
        wa = RW.take("wa", BF16, [8, 672])
        wuq = RW.take("wuq", BF16, [3, 1536])
        wukv = RW.take("wukv", BF16, [2, 2048])
        wo = RW.take("wo", BF16, [8, D])
        cqT = RY.take("cqT", BF16, [3, T])
        ckvT = RY.take("ckvT", BF16, [2, T])
        qT = RY.take(["qT0", "qT1"], BF16, [2, T])
        kT = RY.take(["kT0", "kT1"], BF16, [2, T])
        V = RY.take("V", BF16, [NT, 2, 66])
        self.qk_scratch(RY, 96)
        self.PT = [RY.take("pt%d" % k, BF16, [512]) for k in range(2)]
        self.wload(wa, dr["mla_w_a"][j].rearrange("(c p) n -> p c n", p=128), "wa")
        self.wload(wuq, dr["mla_w_uq"][j].rearrange("(c p) n -> p c n", p=128), "wuq")
        self.wload(wukv, dr["mla_w_ukv"][j].rearrange("(c p) n -> p c n", p=128), "wukv")
        self.wload(wo, dr["mla_w_o"][j].rearrange("(c p) n -> p c n", p=128), "wo")
        self.prenorm_T(l, 0, list(range(NT)), hnT, ["hnT%d" % i for i in range(NT)])
        P.add("dve", lambda e: e.memset(V[:, :, :, 64:66], 1.0), w=["V"])
        for i in range(NT):
            q = i % 2
            pa, pbk = self.bank(2 * q), self.bank(2 * q + 1)
            pn = ["ps%d" % (2 * q), "ps%d" % (2 * q + 1)]

            def mm(e, i=i, pa=pa, pbk=pbk):
                for c in range(8):
                    e.matmul(pa[:, 0:384], hnT[:, c, i * 128:(i + 1) * 128], wa[:, c, 0:384], start=(c == 0), stop=(c == 7))
                for c in range(8):
                    ins = e.matmul(pbk[:, 0:288], hnT[:, c, i * 128:(i + 1) * 128], wa[:, c, 384:672], start=(c == 0), stop=(c == 7))
                return ins
            P.add("pe", mm, r=["hnT%d" % i, "wa"], w=pn)
            kk = i % 2
            hb, hbn = self.hb[kk], "hb%d" % kk
            ssq, sskv = self.ss[:, kk:kk + 1], self.rs[:, kk:kk + 1]
            rq, rkv = self.st[:, 12 + 2 * kk:13 + 2 * kk], self.st[:, 13 + 2 * kk:14 + 2 * kk]
            n_ssq, n_sskv, n_rq, n_rkv = "ss%d" % kk, "rs%d" % kk, "mrq%d" % kk, "mrkv%d" % kk
            P.add("act", lambda e, hb=hb, pa=pa, ssq=ssq: e.activation(out=hb[:, 0:384], in_=pa[:, 0:384], func=AF.Square, accum_out=ssq),
                  r=pn, w=[hbn, n_ssq])
            P.add("act", lambda e, hb=hb, pbk=pbk, sskv=sskv: e.activation(out=hb[:, 384:640], in_=pbk[:, 0:256], func=AF.Square, accum_out=sskv),
                  r=pn, w=[hbn, n_sskv])
            self.rstd(ssq, rq, 1.0 / 384, n_ssq, n_rq)
            self.rstd(sskv, rkv, 1.0 / 256, n_sskv, n_rkv)
            P.add("dve", lambda e, hb=hb, pa=pa, rq=rq: e.tensor_scalar(out=hb[:, 0:384], in0=pa[:, 0:384], scalar1=rq, scalar2=None, op0=ALU.mult),
                  r=pn + [n_rq], w=[hbn])
            P.add("dve", lambda e, hb=hb, pbk=pbk, rkv=rkv: e.tensor_scalar(out=hb[:, 384:640], in0=pbk[:, 0:256], scalar1=rkv, scalar2=None, op0=ALU.mult),
                  r=pn + [n_rkv], w=[hbn])
            ta = self.ropes[0][:, 0, :]
            tb = self.ropes[1][:, 0, :]
            P.add("dve", lambda e, ta=ta, pbk=pbk, i=i: e.tensor_tensor(out=ta, in0=pbk[:, 256:288], in1=self.cos2[:, i, :], op=ALU.mult),
                  r=pn + ["cos2"], w=["ropea"])
            P.add("dve", lambda e, tb=tb, pbk=pbk, i=i: e.tensor_tensor(out=tb[:, 0:16], in0=pbk[:, 272:288], in1=self.sin2[:, i, 0:16], op=ALU.mult),
                  r=pn + ["sin2"], w=["ropeb"])
            P.add("dve", lambda e, tb=tb, pbk=pbk, i=i: e.tensor_tensor(out=tb[:, 16:32], in0=pbk[:, 256:272], in1=self.sin2[:, i, 16:32], op=ALU.mult),
                  r=pn + ["sin2"], w=["ropeb"])
            P.add("dve", lambda e, ta=ta, tb=tb, i=i: e.tensor_tensor(out=self.krope[:, i, :], in0=ta, in1=tb, op=ALU.add),
                  r=["ropea", "ropeb"], w=["krope"])
            b = 6 + kk
            pt = self.bank_bf(b)

            def tr(e, hb=hb, pt=pt):
                for c in range(5):
                    ins = e.transpose(pt[:, c * 128:(c + 1) * 128], hb[:, c * 128:(c + 1) * 128], self.ident)
                return ins
            P.add("pe", tr, r=[hbn, "ident"], w=["ps%d" % b])
            ptv = pt.rearrange("p (c t) -> p c t", c=8)
            gq = self.qn[:, 3 * j:3 * j + 3].unsqueeze(2).broadcast_to([128, 3, 128])
            gkv = self.kvn[:, 2 * j:2 * j + 2].unsqueeze(2).broadcast_to([128, 2, 128])
            P.add("dve", lambda e, ptv=ptv, gq=gq, i=i: e.tensor_tensor(out=cqT[:, :, i * 128:(i + 1) * 128], in0=ptv[:, 0:3, :], in1=gq, op=ALU.mult),
                  r=["ps%d" % b, "qn"], w=["cqT"])
            P.add("dve", lambda e, ptv=ptv, gkv=gkv, i=i: e.tensor_tensor(out=ckvT[:, :, i * 128:(i + 1) * 128], in0=ptv[:, 3:5, :], in1=gkv, op=ALU.mult),
                  r=["ps%d" % b, "kvn"], w=["ckvT"])
        if self.stop == 1:
            return
        RX.reset()
        Otok = RX.take("Otok", BF16, [NT, D])
        scale = 96.0 ** -0.5
        for hg in range(8):
            for i in range(NT):
                b = 4 + (i % 2)
                ps = self.bank(b)
                psn = ["ps%d" % b]

                def mm(e, i=i, ps=ps, hg=hg):
                    for c in range(3):
                        e.matmul(ps[:, 0:192], cqT[:, c, i * 128:(i + 1) * 128], wuq[:, c, hg * 192:(hg + 1) * 192], start=(c == 0), stop=(c == 2))
                    for c in range(2):
                        ins = e.matmul(ps[:, 256:512], ckvT[:, c, i * 128:(i + 1) * 128], wukv[:, c, hg * 256:(hg + 1) * 256], start=(c == 0), stop=(c == 1))
                    return ins
                P.add("pe", mm, r=["cqT", "ckvT", "wuq", "wukv"], w=psn)
                k2 = i % 2
                Q, K = self.Qtok[k2], self.Ktok[k2]
                qn_, kn_ = "qtok%d" % k2, "ktok%d" % k2
                qv = ps[:, 0:192].rearrange("p (h c) -> p h c", h=2)
                kvv = ps[:, 256:512].rearrange("p (h c) -> p h c", h=2)
                ta, tb = self.ropes[0], self.ropes[1]
                cosb = self.cos2[:, i, :].unsqueeze(1).broadcast_to([128, 2, 32])
                sina = self.sin2[:, i, 0:16].unsqueeze(1).broadcast_to([128, 2, 16])
                sinb = self.sin2[:, i, 16:32].unsqueeze(1).broadcast_to([128, 2, 16])
                P.add("dve", lambda e, Q=Q, qv=qv: e.tensor_copy(out=Q[:, :, 0:64], in_=qv[:, :, 0:64]), r=psn, w=[qn_])
                P.add("dve", lambda e, ta=ta, qv=qv, cosb=cosb: e.tensor_tensor(out=ta, in0=qv[:, :, 64:96], in1=cosb, op=ALU.mult),
                      r=psn + ["cos2"], w=["ropea"])
                P.add("dve", lambda e, tb=tb, qv=qv, sina=sina: e.tensor_tensor(out=tb[:, :, 0:16], in0=qv[:, :, 80:96], in1=sina, op=ALU.mult),
                      r=psn + ["sin2"], w=["ropeb"])
                P.add("dve", lambda e, tb=tb, qv=qv, sinb=sinb: e.tensor_tensor(out=tb[:, :, 16:32], in0=qv[:, :, 64:80], in1=sinb, op=ALU.mult),
                      r=psn + ["sin2"], w=["ropeb"])
                P.add("dve", lambda e, Q=Q, ta=ta, tb=tb: e.tensor_tensor(out=Q[:, :, 64:96], in0=ta, in1=tb, op=ALU.add),
                      r=["ropea", "ropeb"], w=[qn_])
                P.add("dve", lambda e, K=K, kvv=kvv: e.tensor_copy(out=K[:, :, 0:64], in_=kvv[:, :, 0:64]), r=psn, w=[kn_])
                P.add("pool", lambda e, K=K, i=i: e.tensor_copy(out=K[:, :, 64:96], in_=self.krope[:, i, :].unsqueeze(1).broadcast_to([128, 2, 32])),
                      r=["krope"], w=[kn_])
                P.add("dve", lambda e, kvv=kvv, i=i: e.tensor_copy(out=V[:, i, :, 0:64], in_=kvv[:, :, 64:128]), r=psn, w=["V"])
                self.qk_transpose(i, 96, qT, kT, i == NT - 1)
            if self.stop not in (2, 4):
                self.attention(hg, qT, kT, V, 96, scale, None, [], Otok)
            if self.stop == 4 and hg == 0:
                P.add("dve", lambda e: e.memset(Otok, 0.5), w=["Otok"])
        if self.stop == 3:
            dbg = self.nc.dram_tensor("dbg", [128, NT * D], BF16, kind="ExternalOutput").ap()
            P.add("sp", lambda e: e.dma_start(out=dbg, in_=Otok.rearrange("p a b -> p (a b)")), r=["Otok"], w=["dbgout"], dma=True)
        if self.stop in (2, 3):
            return
        self.out_proj(Otok, wo)

    def cls_ap(self, ap, d, c4):
        if d == 1:
            return ap[:, c4 * 512:(c4 + 1) * 512]
        if d == 4:
            return ap.rearrange("p (u r) -> p r u", r=4)[:, c4, :]
        return ap.rearrange("p (u r) -> p r u", r=16)[:, 4 * c4:4 * c4 + 4, :]

    def dil(self, l, j):
        P, dr = self.P, self.dr
        RX, RY, RW = self.RX, self.RY, self.RW
        RX.reset(); RY.reset(); RW.reset()
        hnT = RX.take(["hnT%d" % i for i in range(NT)], BF16, [8, T])
        oT = RY.take("oTall", BF16, [8, T])
        UT = RY.take("UTacc", F32, [T])
        UZ = [RY.take("uz%d" % k, BF16, [2, 4, 128]) for k in range(2)]
        wq = [RW.take("wq%d" % k, BF16, [8, 3, 128]) for k in range(2)]
        qkv = [RW.take(nm, BF16, [T]) for nm in ("dqT", "dkT", "dvT")]
        V = RW.take("V", BF16, [NT, 2, 66])
        strip = [RW.take("strip%d" % k, F32, [2, 256]) for k in range(2)]
        PT = [RW.take("pt%d" % k, BF16, [256]) for k in range(3)]
        tmpS = [RW.take("tmpS%d" % k, F32, [256]) for k in range(2)]
        ZT = RW.take("ZTacc", F32, [T])
        w = dr["dil_w_qkv"][j]
        hn_names = ["hnT%d" % i for i in range(NT)]

        combos = [(hp, g) for hp in range(8) for g in range(3)]

        def load_w(ci):
            hp, g = combos[ci]
            k = ci % 2
            for s_ in range(3):
                c0 = g * 3072 + s_ * 1024 + hp * 128
                self.wload(wq[k][:, :, s_, :], w[:, c0:c0 + 128].rearrange("(c p) n -> p c n", p=128), "wq%d" % k)

        def load_strip(ci):
            hp, g = combos[ci]
            k = ci % 2
            self.P.add("sp", lambda e, k=k, g=g, hp=hp: e.dma_start(out=strip[k], in_=dr["dil_strip"][g * 16 + 2 * hp:g * 16 + 2 * hp + 2].rearrange("h s t -> s h t")),
                       w=["strip%d" % k], dma=True)

        load_w(0)
        load_w(1)
        load_strip(0)
        load_strip(1)
        self.prenorm_T(l, 0, list(range(NT)), hnT, hn_names)
        P.add("dve", lambda e: e.memset(V[:, :, :, 64:66], 1.0), w=["V"])
        dils = (1, 4, 16)
        pctr = 0
        only = getattr(self, "only_g", None)
        for ci, (hp, g) in enumerate(combos):
            if only is not None and g != only:
                continue
            d = dils[g]
            n_blk = NT // d
            k = ci % 2
            for s_ in range(3):
                for c4 in range(4):
                    b = pctr % 4
                    pctr += 1
                    ps = self.bank(b)

                    def mm(e, s_=s_, c4=c4, ps=ps, k=k, d=d):
                        for c in range(8):
                            ins = e.matmul(ps, wq[k][:, c, s_, :], self.cls_ap(hnT[:, c, :], d, c4), start=(c == 0), stop=(c == 7))
                        return ins
                    P.add("pe", mm, r=hn_names + ["wq%d" % k], w=["ps%d" % b])
                    dst = qkv[s_][:, c4 * 512:(c4 + 1) * 512]
                    nm = ("dqT", "dkT", "dvT")[s_]
                    if s_ == 1:
                        P.add("dve", lambda e, dst=dst, ps=ps: e.tensor_copy(out=dst, in_=ps), r=["ps%d" % b], w=[nm])
                    else:
                        P.add("act", lambda e, dst=dst, ps=ps: e.activation(out=dst, in_=ps, func=AF.Copy), r=["ps%d" % b], w=[nm])
            for t4 in range(4):
                b = 6 + (t4 % 2)
                pb = self.bank_bf(b)

                def tr(e, t4=t4, pb=pb):
                    for tt in range(4):
                        kt = 4 * t4 + tt
                        ins = e.transpose(pb[:, tt * 128:(tt + 1) * 128], qkv[2][:, kt * 128:(kt + 1) * 128], self.ident)
                    return ins
                P.add("pe", tr, r=["dvT", "ident"], w=["ps%d" % b])
                P.add("dve", lambda e, t4=t4, pb=pb: e.tensor_copy(out=V[:, 4 * t4:4 * t4 + 4, :, 0:64],
                                                                  in_=pb[:, 0:512].rearrange("p (t h c) -> p t h c", t=4, h=2)),
                      r=["ps%d" % b], w=["V"])
            if ci + 2 < len(combos):
                load_w(ci + 2)
            if self.stop == 6:
                for nm_, ap_, dt_, shp in (("dbgV", V.rearrange("p a b c -> p (a b c)"), BF16, [128, NT * 2 * 66]), ("dbgq", qkv[0], BF16, [128, T]),
                                           ("dbgk", qkv[1], BF16, [128, T]), ("dbgh", hnT.rearrange("p a b -> p (a b)"), BF16, [128, 8 * T])):
                    dd = self.nc.dram_tensor(nm_, shp, dt_, kind="ExternalOutput").ap()
                    P.add("sp", lambda e, dd=dd, ap_=ap_: e.dma_start(out=dd, in_=ap_), r=["V", "dqT", "dkT"] + hn_names, w=[nm_], dma=True)
                return
            self.dil_attention(g, d, n_blk, k, hp, qkv, V, strip, PT, tmpS, UZ, UT, ZT)
            if ci + 2 < len(combos):
                load_strip(ci + 2)
            if (g == 2 or only is not None) and self.stop == 5:
                d1 = self.nc.dram_tensor("dbgU", [128, T], F32, kind="ExternalOutput").ap()
                d2 = self.nc.dram_tensor("dbgZ", [128, T], F32, kind="ExternalOutput").ap()
                P.add("sp", lambda e: e.dma_start(out=d1, in_=UT), r=["UTacc"], w=["dbg1"], dma=True)
                P.add("sp", lambda e: e.dma_start(out=d2, in_=ZT), r=["ZTacc"], w=["dbg2"], dma=True)
                return
            if g == 2:
                P.add("dve", lambda e: e.reciprocal(out=ZT, in_=ZT), r=["ZTacc"], w=["ZTacc"])
                P.add("dve", lambda e, hp=hp: e.tensor_tensor(out=oT[:, hp, :], in0=UT, in1=ZT, op=ALU.mult),
                      r=["UTacc", "ZTacc"], w=["oTall"])
        RW.reset()
        wo = RW.take("wo", BF16, [8, D])
        self.wload(wo, dr["dil_w_o"][j].rearrange("(c p) n -> p c n", p=128), "wo")
        for i in range(NT):
            q = i % 2

            def mm(e, i=i, q=q):
                for hf in range(2):
                    po = self.bank(2 * q + hf)
                    for c in range(8):
                        ins = e.matmul(po, oT[:, c, i * 128:(i + 1) * 128], wo[:, c, hf * 512:(hf + 1) * 512], start=(c == 0), stop=(c == 7))
                return ins
            P.add("pe", mm, r=["oTall", "wo"], w=["ps%d" % (2 * q), "ps%d" % (2 * q + 1)])
            self.postnorm_res(q, i)

    def dil_attention(self, g, d, n_blk, k, hp, qkv, V, strip, PT, tmpS, UZ, UT, ZT):
        qT, kT = qkv[0], qkv[1]
        gg = 0 if getattr(self, "only_g", None) is not None else g
        pending = []
        SKEW = 2

        def flush_one():
            st = pending.pop(0)
            self.dil_pv(st, V, PT)
            if st["kt"] % 4 == 3 and st["hh"] == 1:
                self.dil_chunk_out(gg, d, st["kt"] // 4, UZ, UT, ZT)

        for kt in range(NT):
            jb = kt % n_blk
            has_next = (jb + 1 < n_blk)
            ncols = 256 if has_next else 128
            for hh in range(2):
                st = self.dil_scores(kt, jb, has_next, ncols, hh, k, qT, kT, strip, PT, tmpS)
                pending.append(st)
                if len(pending) > SKEW:
                    flush_one()
        while pending:
            flush_one()

    def dil_scores(self, kt, jb, has_next, ncols, hh, k, qT, kT, strip, PT, tmpS):
        P = self.P
        sb = self.sctr % 2
        pt = self.sctr % 3
        ts = self.sctr % 2
        self.sctr += 1
        S = self.bank(sb)
        rows = slice(hh * 64, hh * 64 + 64)
        P.add("pe", lambda e: e.matmul(S[:, 0:ncols], kT[rows, kt * 128:(kt + 1) * 128], qT[rows, kt * 128:kt * 128 + ncols], start=True, stop=True),
              r=["dkT", "dqT"], w=["ps%d" % sb])
        tm = tmpS[ts]
        P.add("dve", lambda e: e.scalar_tensor_tensor(out=tm[:, 0:ncols], in0=S[:, 0:ncols], scalar=0.125, in1=strip[k][:, hh, 0:ncols],
                                                      op0=ALU.mult, op1=ALU.add),
              r=["ps%d" % sb, "strip%d" % k], w=["tmpS%d" % ts])
        PTb = PT[pt]
        P.add("act", lambda e: e.activation(out=PTb[:, 0:ncols], in_=tm[:, 0:ncols], func=AF.Exp), r=["tmpS%d" % ts], w=["pt%d" % pt])
        return dict(kt=kt, jb=jb, has_next=has_next, hh=hh, pt=pt)

    def dil_pv(self, st, V, PT):
        P = self.P
        kt, jb, hh, pt = st["kt"], st["jb"], st["hh"], st["pt"]
        PTb = PT[pt]

        def acc_ap(qt):
            bank = (2 if (qt // 4) % 2 == 0 else 4) + hh
            return self.bank(bank)[:, (qt % 4) * 128:(qt % 4) * 128 + 65], "ps%d" % bank

        a0, n0 = acc_ap(kt)
        P.add("pe", lambda e: e.matmul(a0, PTb[:, 0:128], V[:, kt, hh, 0:65], start=(jb == 0), stop=True), r=["pt%d" % pt, "V"], w=[n0])
        if st["has_next"]:
            a1, n1 = acc_ap(kt + 1)
            P.add("pe", lambda e: e.matmul(a1, PTb[:, 128:256], V[:, kt, hh, 0:65], start=True, stop=False), r=["pt%d" % pt, "V"], w=[n1])

    def dil_chunk_out(self, g, d, ch, UZ, UT, ZT):
        P = self.P
        uz = UZ[ch % 2]
        uzn = "uz%d" % (ch % 2)
        for hh in range(2):
            bank = (2 if ch % 2 == 0 else 4) + hh
            accv = self.bank(bank).rearrange("p (q c) -> p q c", q=4)
            P.add("dve", lambda e, accv=accv, hh=hh: e.tensor_copy(out=uz[:, 0, :, hh * 64:(hh + 1) * 64], in_=accv[:, :, 0:64]),
                  r=["ps%d" % bank], w=[uzn])
            P.add("dve", lambda e, accv=accv, hh=hh: e.tensor_copy(out=uz[:, 1, :, hh * 64:(hh + 1) * 64], in_=accv[:, :, 64:65].broadcast_to([128, 4, 64])),
                  r=["ps%d" % bank], w=[uzn])
        b = 6 + (ch % 2)
        pb = self.bank_bf(b)

        def tr(e):
            for a in range(2):
                for tt in range(4):
                    ins = e.transpose(pb[:, (a * 4 + tt) * 128:(a * 4 + tt + 1) * 128], uz[:, a, tt, :], self.ident)
            return ins
        P.add("pe", tr, r=[uzn, "ident"], w=["ps%d" % b])
        for a, acc_t, nm in ((0, UT, "UTacc"), (1, ZT, "ZTacc")):
            dst = self.cls_ap(acc_t, d, ch)
            src = pb[:, a * 512:(a + 1) * 512]
            if d == 16:
                src = src.rearrange("p (r u) -> p r u", r=4)
            if g == 0:
                P.add("dve", lambda e, dst=dst, src=src: e.tensor_copy(out=dst, in_=src), r=["ps%d" % b], w=[nm])
            else:
                P.add("dve", lambda e, dst=dst, src=src: e.tensor_tensor(out=dst, in0=dst, in1=src, op=ALU.add), r=["ps%d" % b, nm], w=[nm])

    def fox(self, l, j):
        P, dr = self.P, self.dr
        RX, RY, RW = self.RX, self.RY, self.RW
        RX.reset(); RY.reset(); RW.reset()
        hnT = RX.take(["hnT%d" % i for i in range(NT)], BF16, [8, T])
        Otok = RY.take("Otok", BF16, [NT, D])
        qT = RY.take(["qT0", "qT1"], BF16, [2, T])
        self.qk_scratch(RY, 68)
        wf = RW.take("wf", BF16, [8, 16])
        wq = [RW.take("wq%d" % k, BF16, [8, 3, 128]) for k in range(2)]
        wo = RW.take("wo", BF16, [8, D])
        kT = RW.take(["kT0", "kT1"], BF16, [2, T])
        V = RW.take("V", BF16, [NT, 2, 66])
        self.PT = [RW.take("pt%d" % k, BF16, [512]) for k in range(3)]
        w = dr["fox_w_qkvf"][j]

        def load_wq(hg):
            k = hg % 2
            for s in range(3):
                src = w[:, s * 1024 + hg * 128: s * 1024 + hg * 128 + 128].rearrange("(c p) n -> p c n", p=128)
                self.wload(wq[k][:, :, s, :], src, "wq%d" % k)

        self.wload(wf, w[:, 3072:3088].rearrange("(c p) n -> p c n", p=128), "wf")
        load_wq(0)
        load_wq(1)
        self.wload(wo, dr["fox_w_o"][j].rearrange("(c p) n -> p c n", p=128), "wo")
        self.prenorm_T(l, 0, list(range(NT)), hnT, ["hnT%d" % i for i in range(NT)])
        P.add("dve", lambda e: e.memset(V[:, :, :, 64:66], 1.0), w=["V"])
        for k in range(2):
            Kt = self.Ktok[k]
            P.add("dve", lambda e, Kt=Kt: e.memset(Kt[:, :, 64:68], 1.0), w=["ktok%d" % k])
        t0, t1 = self.tmpf[0], self.tmpf[1]
        LF = t0[:, 0:256]
        CS = t0[:, 256:512]
        C8 = t0[:, 512:768]
        R1 = t0[:, 768:1024]
        AUG = t1[:, 0:384].bitcast(BF16)[:, 0:768].rearrange("p (i h a) -> p i h a", i=NT, h=16)
        HB = t1[:, 384:512].bitcast(BF16)
        H32 = t1[:, 512:768]
        PRE = t1[:, 768:1024]
        LFv = LF.rearrange("p (i h) -> p i h", i=NT)
        for i in range(NT):
            b = 4 + (i % 2)
            ps = self.bank(b)

            def mm(e, i=i, ps=ps):
                for c in range(8):
                    ins = e.matmul(ps[:, 0:16], hnT[:, c, i * 128:(i + 1) * 128], wf[:, c, :], start=(c == 0), stop=(c == 7))
                return ins
            P.add("pe", mm, r=["hnT%d" % i, "wf"], w=["ps%d" % b])
            P.add("dve", lambda e, i=i, ps=ps: e.tensor_tensor(out=LFv[:, i, :], in0=ps[:, 0:16], in1=self.bf, op=ALU.add),
                  r=["ps%d" % b, "bf"], w=["tmpf0"])
        P.add("act", lambda e: e.activation(out=LF, in_=LF, func=AF.Exp, scale=-1.0), r=["tmpf0"], w=["tmpf0"])
        P.add("act", lambda e: e.activation(out=LF, in_=LF, func=AF.Ln, bias=self.onec), r=["tmpf0", "onec"], w=["tmpf0"])
        pw, pt_ = self.bank(4), self.bank(5)
        P.add("pe", lambda e: e.matmul(pw[:, 0:256], self.trif, LF, start=True, stop=True), r=["trif", "tmpf0"], w=["ps4"])
        P.add("pe", lambda e: e.matmul(pt_[:, 0:256], self.onesf, LF, start=True, stop=True), r=["onesf", "tmpf0"], w=["ps5"])
        PREv = PRE.rearrange("p (i h) -> p i h", i=NT)
        ptv = pt_[:, 0:256].rearrange("p (i h) -> p i h", i=NT)
        P.add("dve", lambda e: e.memset(PREv[:, 0, :], 0.0), w=["tmpf1"])
        for i in range(1, NT):
            P.add("dve", lambda e, i=i: e.tensor_tensor(out=PREv[:, i, :], in0=PREv[:, i - 1, :], in1=ptv[:, i - 1, :], op=ALU.add),
                  r=["tmpf1", "ps5"], w=["tmpf1"])
        P.add("dve", lambda e: e.tensor_tensor(out=CS, in0=pw[:, 0:256], in1=PRE, op=ALU.add), r=["ps4", "tmpf1"], w=["tmpf0"])
        P.add("dve", lambda e: e.tensor_scalar(out=C8, in0=CS, scalar1=-8.0, scalar2=None, op0=ALU.mult), r=["tmpf0"], w=["tmpf0"])
        AUGf = AUG.rearrange("p i h a -> p (i h) a")
        for a in range(3):
            src = C8 if a == 0 else R1
            P.add("dve", lambda e, a=a, src=src: e.tensor_copy(out=AUGf[:, :, a], in_=src), r=["tmpf0"], w=["tmpf1"])
            if a < 2:
                P.add("dve", lambda e, a=a: e.tensor_copy(out=H32, in_=AUGf[:, :, a]), r=["tmpf1"], w=["tmpf1"])
                P.add("dve", lambda e, src=src: e.tensor_tensor(out=R1, in0=src, in1=H32, op=ALU.subtract), r=["tmpf0", "tmpf1"], w=["tmpf0"])
        CSv = CS.rearrange("p (i h) -> p i h", i=NT)
        for hg in range(8):
            k = hg % 2
            for i in range(NT):
                b = 4 + (i % 2)
                ps = self.bank(b)
                psn = ["ps%d" % b]

                def mm(e, i=i, ps=ps, k=k):
                    for c in range(8):
                        ins = e.matmul(ps[:, 0:384], hnT[:, c, i * 128:(i + 1) * 128], wq[k][:, c, :, :], start=(c == 0), stop=(c == 7))
                    return ins
                P.add("pe", mm, r=["hnT%d" % i, "wq%d" % k], w=psn)
                k2 = i % 2
                Q, K = self.Qtok[k2], self.Ktok[k2]
                qn_, kn_ = "qtok%d" % k2, "ktok%d" % k2
                pv = ps[:, 0:384].rearrange("p (s h c) -> p s h c", s=3, h=2)
                P.add("dve", lambda e, Q=Q, pv=pv: e.tensor_copy(out=Q[:, :, 0:64], in_=pv[:, 0, :, :]), r=psn, w=[qn_])
                P.add("dve", lambda e, Q=Q, i=i, hg=hg: e.tensor_copy(out=Q[:, :, 64:67], in_=AUG[:, i, 2 * hg:2 * hg + 2, :]), r=["tmpf1"], w=[qn_])
                P.add("dve", lambda e, K=K, pv=pv: e.tensor_copy(out=K[:, :, 0:64], in_=pv[:, 1, :, :]), r=psn, w=[kn_])
                P.add("dve", lambda e, pv=pv, i=i: e.tensor_copy(out=V[:, i, :, 0:64], in_=pv[:, 2, :, :]), r=psn, w=["V"])
                self.qk_transpose(i, 67, qT, kT, i == NT - 1)
            if hg + 2 < 8:
                load_wq(hg + 2)
            self.attention(hg, qT, kT, V, 67, 0.125, lambda jj, h: CSv[:, jj, h:h + 1], ["tmpf0"], Otok)
        self.out_proj(Otok, wo)

def _t5_bucket_const(dist):
    max_exact = 16
    n = np.maximum(dist.astype(np.float32), np.float32(1.0))
    large = max_exact + (np.log(n / np.float32(max_exact)) / np.float32(math.log(2048 / max_exact))
                         * np.float32(32 - max_exact)).astype(np.int32)
    large = np.minimum(large, 31)
    return np.where(dist < max_exact, dist, large)


def host_inputs(inputs, b):
    f32 = np.float32
    m = {}
    m["x"] = np.ascontiguousarray(inputs["x"][b])
    m["pT"] = np.ascontiguousarray(np.transpose(inputs["p"][:, b], (0, 2, 1)))
    m["pos"] = np.ascontiguousarray(inputs["positions"][b].reshape(NT, 128).T.astype(np.int32))
    g = inputs["norm_g"]
    gpre = np.stack([g[:, 0], g[:, 2]], axis=1)
    m["gpre"] = np.ascontiguousarray(gpre.reshape(DEPTH * 2, 8, 128).transpose(2, 0, 1).reshape(128, DEPTH * 2 * 8))
    m["gpost"] = np.ascontiguousarray(np.stack([g[:, 1], g[:, 3]], axis=1).reshape(DEPTH * 2, D))
    for k in ("ffn_w_in", "ffn_w_out", "ple_w_proj", "ple_w_gate", "mla_w_a", "mla_w_uq", "mla_w_ukv", "mla_w_o",
              "dil_w_qkv", "dil_w_o", "fox_w_qkvf", "fox_b_f", "fox_w_o"):
        m[k] = inputs[k]
    m["mla_qn"] = np.ascontiguousarray(inputs["mla_q_norm"].reshape(2, 3, 128).transpose(2, 0, 1).reshape(128, 6))
    m["mla_kvn"] = np.ascontiguousarray(inputs["mla_kv_norm"].reshape(2, 2, 128).transpose(2, 0, 1).reshape(128, 4))
    sidx = np.arange(128)[:, None]
    tidx = np.arange(256)[None, :]
    rel = tidx - sidx
    valid = (rel >= 0) & (rel <= 128)
    strips = np.full((48, 128, 256), -30000.0, f32)
    rb = inputs["rel_bias"]
    for g_, d_ in enumerate((1, 4, 16)):
        bucket = _t5_bucket_const(np.clip(rel, 0, None) * d_)
        for h_ in range(16):
            col = g_ * 16 + h_
            strips[col] = np.where(valid, rb[bucket, col], f32(-30000.0))
    m["dil_strip"] = strips
    m["ident"] = np.eye(128).astype(ml_dtypes.bfloat16)
    tri = (np.arange(128)[:, None] <= np.arange(128)[None, :])
    m["trif"] = tri.astype(f32)
    inv = (np.float32(10000.0) ** (-np.arange(16, dtype=f32) / np.float32(16))).astype(f32)
    m["invf"] = np.ascontiguousarray(np.broadcast_to(inv[None, :], (128, 16))).astype(f32)
    return m


_NC_CACHE = {}


def get_nc(n_layers=DEPTH, mixers=(0, 1, 2)):
    key = (n_layers, tuple(mixers))
    if key not in _NC_CACHE:
        _NC_CACHE[key] = Builder(n_layers, mixers).build()
    return _NC_CACHE[key]


def kernel(**inputs):
    inputs = {k: np.asarray(v) for k, v in inputs.items()}
    nc = get_nc()
    in_maps = [host_inputs(inputs, b) for b in range(8)]
    res = run_bass_kernel_spmd(nc, in_maps, core_ids=list(range(8)))
    out = np.stack([np.asarray(r["out"]) for r in res.results], axis=0)
    return out.astype(np.float32)
```

```python
import contextlib
import math
import numpy as np
import ml_dtypes
import concourse.bass as bass
import concourse.mybir as mybir
from concourse.bass_utils import run_bass_kernel_spmd

F32 = mybir.dt.float32
BF16 = mybir.dt.bfloat16
I32 = mybir.dt.int32
AF = mybir.ActivationFunctionType
ALU = mybir.AluOpType

D = 1024
T = 2048
NT = 16
DFF = 2816
NF = 22
DEPTH = 4
EPS = 1e-6
TWO_PI = 2.0 * math.pi


def _size(dt):
    return mybir.dt.size(dt)


class Op:
    __slots__ = ("eng", "fn", "deps", "dma", "signal", "count", "dsem", "dval", "idx")


class Prog:
    def __init__(self):
        self.q = {"pe": [], "act": [], "dve": [], "pool": [], "sp": []}
        self.state = {}
        self.ndma = {"pool": 0, "sp": 0}
        self.NDSEM = 8

    def _st(self, name):
        st = self.state.get(name)
        if st is None:
            st = self.state[name] = [None, {}, []]
        return st

    def inherit(self, new, olds):
        ops = {}
        for n in olds:
            st = self.state.get(n)
            if st is None:
                continue
            for o in [st[0]] + list(st[1].values()) + list(st[2]):
                if o is not None:
                    ops[id(o)] = o
        st = self._st(new)
        for o in ops.values():
            st[2].append(o)

    def add(self, eng, fn, r=(), w=(), dma=False):
        op = Op()
        op.eng, op.fn, op.dma, op.signal = eng, fn, dma, False
        deps = {}

        def dep(o):
            if o is None or o is op:
                return
            deps[id(o)] = o

        for name in r:
            st = self._st(name)
            dep(st[0])
            if name.startswith("ps"):
                for e2, o2 in st[1].items():
                    if e2 != eng:
                        dep(o2)
        for name in w:
            st = self._st(name)
            dep(st[0])
            for o2 in st[1].values():
                dep(o2)
            for o2 in st[2]:
                dep(o2)
        for name in r:
            st = self.state[name]
            if dma:
                st[2].append(op)
            else:
                st[1][eng] = op
        for name in w:
            self.state[name] = [op, {}, []]
        dl = []
        best = {}
        for o in deps.values():
            if o.dma:
                dl.append(o)
                continue
            if o.eng == eng and not dma and eng == "pe":
                continue
            b = best.get(o.eng)
            if b is None or o.idx > b.idx:
                best[o.eng] = o
        dl.extend(best.values())
        op.deps = dl
        op.idx = len(self.q[eng])
        if dma:
            k = self.ndma[eng]
            self.ndma[eng] += 1
            op.dsem = k % self.NDSEM
            op.dval = 16 * (k // self.NDSEM + 1)
        self.q[eng].append(op)
        return op

    def finalize(self):
        for e in self.q:
            for op in self.q[e]:
                for d in op.deps:
                    if not d.dma:
                        d.signal = True
        for e in self.q:
            c = 0
            for op in self.q[e]:
                if op.dma:
                    continue
                if op.signal:
                    c += 1
                    op.count = c

    def emit(self, e, engine, sems, dsems):
        known = {}

        def wait(key, sem, val):
            if known.get(key, 0) >= val:
                return
            known[key] = val
            engine.wait_ge(sem, val)

        for op in self.q[e]:
            for d in op.deps:
                if d.dma:
                    wait(("d", d.eng, d.dsem), dsems[d.eng][d.dsem], d.dval)
                else:
                    wait(("c", d.eng), sems[d.eng], d.count)
            if op.dma:
                if op.dval > 16:
                    wait(("d", e, op.dsem), dsems[e][op.dsem], op.dval - 16)
                ins = op.fn(engine)
                ins.then_inc(dsems[e][op.dsem], 16)
            else:
                ins = op.fn(engine)
                if op.signal:
                    ins.then_inc(sems[e], 1)
        if e in self.ndma:
            for k in range(self.NDSEM):
                n = (self.ndma[e] - k + self.NDSEM - 1) // self.NDSEM
                if n > 0:
                    engine.wait_ge(dsems[e][k], 16 * n)


def sub(handle, dt, off_bytes, shape):
    es = _size(dt)
    h = handle.bitcast(dt)
    n = int(np.prod(shape))
    assert off_bytes % es == 0
    o = off_bytes // es
    ap = h[:, o:o + n]
    if len(shape) == 2:
        ap = ap.rearrange("p (a b) -> p a b", a=shape[0], b=shape[1])
    elif len(shape) == 3:
        ap = ap.rearrange("p (a b c) -> p a b c", a=shape[0], b=shape[1], c=shape[2])
    return ap


class Region:
    def __init__(self, P, handle, nbytes):
        self.P, self.h, self.size = P, handle, nbytes
        self.cur, self.old, self.off = [], [], 0

    def reset(self):
        self.old += self.cur
        self.cur = []
        self.off = 0

    def take(self, names, dt, shape):
        if isinstance(names, str):
            names = [names]
        n = int(np.prod(shape)) * _size(dt)
        n = (n + 3) // 4 * 4
        off = self.off
        self.off += n
        assert self.off <= self.size, (names, self.off, self.size)
        for nm in names:
            self.cur.append(nm)
            self.P.inherit(nm, self.old)
        return sub(self.h, dt, off, shape)


class Builder:
    def __init__(self, n_layers=DEPTH, mixers=(0, 1, 2), types=(0, 1, 2, 0)):
        self.n_layers = n_layers
        self.mixers = mixers
        self.types = list(types)
        self.cctr = 0
        self.sctr = 0
        self.stop = 0

    def build(self):
        nc = bass.Bass("TRN2", target_bir_lowering=False)
        self.nc = nc
        P = self.P = Prog()
        dr = {}

        def din(name, shape, dt=F32):
            dr[name] = nc.dram_tensor(name, list(shape), dt, kind="ExternalInput").ap()

        din("x", [T, D])
        din("pT", [DEPTH, 256, T])
        din("pos", [128, NT], I32)
        din("gpre", [128, DEPTH * 2 * 8])
        din("gpost", [DEPTH * 2, D])
        din("ffn_w_in", [DEPTH, D, 2 * DFF])
        din("ffn_w_out", [DEPTH, DFF, D])
        din("ple_w_proj", [DEPTH, 256, D])
        din("ple_w_gate", [DEPTH, D, D])
        din("mla_w_a", [2, D, 672])
        din("mla_qn", [128, 6])
        din("mla_kvn", [128, 4])
        din("mla_w_uq", [2, 384, 1536])
        din("mla_w_ukv", [2, 256, 2048])
        din("mla_w_o", [2, D, D])
        din("dil_w_qkv", [1, D, 9216])
        din("dil_w_o", [1, D, D])
        din("dil_strip", [48, 128, 256])
        din("fox_w_qkvf", [1, D, 3088])
        din("fox_b_f", [1, 16])
        din("fox_w_o", [1, D, D])
        din("ident", [128, 128], BF16)
        din("trif", [128, 128])
        din("invf", [128, 16])
        dr["out"] = nc.dram_tensor("out", [T, D], F32, kind="ExternalOutput").ap()
        self.dr = dr

        with (
            nc.sbuf_tensor("H", [128, NT, D], F32) as H,
            nc.sbuf_tensor("X", [128, 16384], BF16) as X,
            nc.sbuf_tensor("Y", [128, 22528], BF16) as Y,
            nc.sbuf_tensor("W", [128, 22528], BF16) as W,
            nc.sbuf_tensor("C", [128, 6100], F32) as C,
            nc.psum_tensor("PA", [128, 1024], F32) as PA,
            nc.psum_tensor("PB", [128, 1024], F32) as PB,
            nc.psum_tensor("PC", [128, 1024], F32) as PC,
            nc.psum_tensor("PD", [128, 1024], F32) as PD,
        ):
            self.H = H
            self.RX = Region(P, X, 32768)
            self.RY = Region(P, Y, 45056)
            self.RW = Region(P, W, 45056)
            self.RC = Region(P, C, 24400)
            self.PS = [PA, PB, PC, PD]
            self.setup_consts()
            self.load_x()
            for l in range(self.n_layers):
                self.layer(l)
            self.store_out()
            P.finalize()

            with contextlib.ExitStack() as es:
                sems = {e: es.enter_context(nc.semaphore("s_" + e)) for e in ("pe", "act", "dve", "pool", "sp")}
                dsems = {e: [es.enter_context(nc.semaphore("d_%s%d" % (e, k))) for k in range(P.NDSEM)]
                         for e in ("pool", "sp")}
                block = es.enter_context(nc.Block())

                @block.tensor
                def _(e):
                    P.emit("pe", e, sems, dsems)

                @block.scalar
                def _(e):
                    P.emit("act", e, sems, dsems)

                @block.vector
                def _(e):
                    P.emit("dve", e, sems, dsems)

                @block.gpsimd
                def _(e):
                    P.emit("pool", e, sems, dsems)

                @block.sync
                def _(e):
                    P.emit("sp", e, sems, dsems)
        return nc

    def bank(self, b):
        return self.PS[b // 2][:, (b % 2) * 512:(b % 2) * 512 + 512]

    def bank_bf(self, b):
        h = self.PS[b // 2].bitcast(BF16)
        return h[:, (b % 2) * 1024:(b % 2) * 1024 + 1024]

    def pair(self, q):
        return self.PS[q][:, :]

    def setup_consts(self):
        P, dr, RC = self.P, self.dr, self.RC
        self.ident = RC.take("ident", BF16, [128])
        self.negm = RC.take("negm", F32, [128])
        self.trif = RC.take("trif", F32, [128])
        self.onesf = RC.take("onesf", F32, [128])
        self.gpre = RC.take("gpre", F32, [DEPTH * 2 * 8])
        self.invf = RC.take("invf", F32, [16])
        self.posi = RC.take("posi", I32, [NT])
        self.qn = RC.take("qn", F32, [6])
        self.kvn = RC.take("kvn", F32, [4])
        self.bf = RC.take("bf", F32, [16])
        self.ss = RC.take(["ss0", "ss1", "pss0", "pss1"], F32, [4])
        self.rs = RC.take(["rs0", "rs1", "prs0", "prs1"], F32, [4])
        self.st = RC.take("st", F32, [16])
        self.cos2 = RC.take("cos2", F32, [NT, 32])
        self.sin2 = RC.take("sin2", F32, [NT, 32])
        self.gpost = RC.take("gpost", F32, [D])
        self.hb = [RC.take("hb%d" % k, BF16, [D]) for k in range(2)]
        self.tmpf = [RC.take("tmpf%d" % k, F32, [D]) for k in range(2)]
        self.krope = RC.take("krope", BF16, [NT, 32])
        self.ropes = [RC.take(nm, F32, [2, 32]) for nm in ("ropea", "ropeb")]
        for nm, ap in (("ident", self.ident), ("trif", self.trif), ("gpre", self.gpre),
                       ("invf", self.invf), ("qn", self.qn), ("kvn", self.kvn)):
            src = {"qn": "mla_qn", "kvn": "mla_kvn"}.get(nm, nm)
            P.add("sp", lambda e, ap=ap, src=src: e.dma_start(out=ap, in_=dr[src]), w=[nm], dma=True)
        P.add("sp", lambda e: e.dma_start(out=self.posi, in_=dr["pos"]), w=["posi"], dma=True)
        P.add("sp", lambda e: e.dma_start(out=self.bf.unsqueeze(1), in_=dr["fox_b_f"].partition_broadcast(128)),
              w=["bf"], dma=True)
        P.add("dve", lambda e: e.memset(self.onesf, 1.0), w=["onesf"])
        P.add("dve", lambda e: e.tensor_scalar(out=self.negm, in0=self.trif, scalar1=240000.0, scalar2=-240000.0, op0=ALU.mult, op1=ALU.add),
              r=["trif"], w=["negm"])
        self.rope_tables()

    def rope_tables(self):
        P = self.P
        ang = self.tmpf[0][:, 0:256].rearrange("p (i f) -> p i f", i=NT)
        kf = self.tmpf[0][:, 256:512].rearrange("p (i f) -> p i f", i=NT)
        ki = self.tmpf[0][:, 512:768].bitcast(I32).rearrange("p (i f) -> p i f", i=NT)
        m = self.tmpf[0][:, 768:1024].rearrange("p (i f) -> p i f", i=NT)
        posf = self.tmpf[1][:, 0:NT]
        ab = self.tmpf[1][:, 256:512].rearrange("p (i f) -> p i f", i=NT)
        r = ["tmpf0"]
        P.add("dve", lambda e: e.tensor_copy(out=posf, in_=self.posi), r=["posi"], w=["tmpf1"])
        P.add("dve", lambda e: e.tensor_tensor(out=ang, in0=posf.unsqueeze(2).broadcast_to([128, NT, 16]),
                                               in1=self.invf.unsqueeze(1).broadcast_to([128, NT, 16]), op=ALU.mult),
              r=["tmpf1", "invf"], w=r)
        P.add("dve", lambda e: e.tensor_scalar(out=kf, in0=ang, scalar1=1.0 / TWO_PI, scalar2=None, op0=ALU.mult), r=r, w=r)
        P.add("dve", lambda e: e.tensor_copy(out=ki, in_=kf), r=r, w=r)
        P.add("dve", lambda e: e.tensor_copy(out=kf, in_=ki), r=r, w=r)
        C1 = 6.28125
        C2 = TWO_PI - C1
        P.add("dve", lambda e: e.scalar_tensor_tensor(out=ang, in0=kf, scalar=-C1, in1=ang, op0=ALU.mult, op1=ALU.add), r=r, w=r)
        P.add("dve", lambda e: e.scalar_tensor_tensor(out=ang, in0=kf, scalar=-C2, in1=ang, op0=ALU.mult, op1=ALU.add), r=r, w=r)
        P.add("dve", lambda e: e.tensor_scalar(out=m, in0=ang, scalar1=math.pi, scalar2=-TWO_PI, op0=ALU.is_gt, op1=ALU.mult), r=r, w=r)
        P.add("dve", lambda e: e.tensor_tensor(out=ang, in0=ang, in1=m, op=ALU.add), r=r, w=r)
        P.add("dve", lambda e: e.tensor_scalar(out=m, in0=ang, scalar1=-math.pi, scalar2=TWO_PI, op0=ALU.is_lt, op1=ALU.mult), r=r, w=r)
        P.add("dve", lambda e: e.tensor_tensor(out=ang, in0=ang, in1=m, op=ALU.add), r=r, w=r)
        P.add("dve", lambda e: e.tensor_scalar(out=ang, in0=ang, scalar1=math.pi, scalar2=-math.pi, op0=ALU.min, op1=ALU.max), r=r, w=r)
        P.add("dve", lambda e: e.tensor_scalar(out=ab, in0=ang, scalar1=-1.0, scalar2=None, op0=ALU.mult), r=r, w=["tmpf1"])
        P.add("dve", lambda e: e.tensor_tensor(out=ab, in0=ab, in1=ang, op=ALU.max), r=r + ["tmpf1"], w=["tmpf1"])
        P.add("act", lambda e: e.activation(out=self.sin2[:, :, 16:32], in_=ang, func=AF.Sin), r=r, w=["sin2"])
        hp = self.st[:, 0:1]
        P.add("dve", lambda e: e.memset(hp, math.pi / 2), w=["st"])
        P.add("act", lambda e: e.activation(out=self.cos2[:, :, 0:16], in_=ab, func=AF.Sin, scale=-1.0, bias=hp),
              r=["tmpf1", "st"], w=["cos2"])
        P.add("dve", lambda e: e.tensor_copy(out=self.cos2[:, :, 16:32], in_=self.cos2[:, :, 0:16]), r=["cos2"], w=["cos2"])
        P.add("dve", lambda e: e.tensor_scalar(out=self.sin2[:, :, 0:16], in0=self.sin2[:, :, 16:32], scalar1=-1.0, scalar2=None, op0=ALU.mult),
              r=["sin2"], w=["sin2"])

    def load_x(self):
        P, dr = self.P, self.dr
        xv = dr["x"].rearrange("(i p) d -> p i d", p=128)
        for i0 in range(0, NT, 4):
            P.add("sp", lambda e, i0=i0: e.dma_start(out=self.H[:, i0:i0 + 4, :], in_=xv[:, i0:i0 + 4, :]),
                  w=["H%d" % i for i in range(i0, i0 + 4)], dma=True)

    def store_out(self):
        P, dr = self.P, self.dr
        ov = dr["out"].rearrange("(i p) d -> p i d", p=128)
        for i0 in range(0, NT, 4):
            P.add("sp", lambda e, i0=i0: e.dma_start(out=ov[:, i0:i0 + 4, :], in_=self.H[:, i0:i0 + 4, :]),
                  r=["H%d" % i for i in range(i0, i0 + 4)], w=["out%d" % i0], dma=True)

    def wload(self, dst, src, wname):
        self.P.add("pool", lambda e: e.dma_start(out=dst, in_=src), w=[wname], dma=True)

    def rstd(self, ss_ap, out_ap, inv_n, rn, wn):
        P = self.P
        P.add("act", lambda e: e.activation(out=out_ap, in_=ss_ap, func=AF.Sqrt, scale=inv_n, bias=self.epsc), r=[rn, "epsc"], w=[wn])
        P.add("dve", lambda e: e.reciprocal(out=out_ap, in_=out_ap), r=[wn], w=[wn])

    def prenorm_T(self, l, which, tiles, dst, dst_names, normalize=True):
        P, H = self.P, self.H
        for k, i in enumerate(tiles):
            kk = k % 2
            hb, hbn = self.hb[kk], "hb%d" % kk
            ssn, rsn = "ss%d" % kk, "rs%d" % kk
            ss = self.ss[:, kk:kk + 1]
            rs = self.rs[:, kk:kk + 1]
            hn = "H%d" % i
            if normalize:
                P.add("act", lambda e, i=i, ss=ss, hb=hb: e.activation(out=hb, in_=H[:, i, :], func=AF.Square, accum_out=ss),
                      r=[hn], w=[hbn, ssn])
                self.rstd(ss, rs, 1.0 / D, ssn, rsn)
                P.add("dve", lambda e, i=i, rs=rs, hb=hb: e.tensor_scalar(out=hb, in0=H[:, i, :], scalar1=rs, scalar2=None, op0=ALU.mult),
                      r=[hn, rsn], w=[hbn])
            else:
                P.add("act", lambda e, i=i, hb=hb: e.activation(out=hb, in_=H[:, i, :], func=AF.Copy), r=[hn], w=[hbn])
            b = 6 + kk
            pb = self.bank_bf(b)

            def tr(e, hb=hb, pb=pb):
                for c in range(8):
                    ins = e.transpose(pb[:, c * 128:(c + 1) * 128], hb[:, c * 128:(c + 1) * 128], self.ident)
                return ins
            P.add("pe", tr, r=[hbn, "ident"], w=["ps%d" % b])
            d = dst[:, :, k * 128:(k + 1) * 128]
            src = pb.rearrange("p (c t) -> p c t", c=8)
            dn = [dst_names[k]] if isinstance(dst_names, list) else [dst_names]
            if normalize:
                g = self.gpre[:, (l * 2 + which) * 8:(l * 2 + which) * 8 + 8]
                gb = g.unsqueeze(2).broadcast_to([128, 8, 128])
                P.add("dve", lambda e, d=d, src=src, gb=gb: e.tensor_tensor(out=d, in0=src, in1=gb, op=ALU.mult),
                      r=["ps%d" % b, "gpre"], w=dn)
            else:
                P.add("dve", lambda e, d=d, src=src: e.tensor_copy(out=d, in_=src), r=["ps%d" % b], w=dn)

    def load_gpost(self, l, which):
        P, dr = self.P, self.dr
        row = 2 * l + which
        P.add("sp", lambda e: e.dma_start(out=self.gpost.unsqueeze(1), in_=dr["gpost"][row:row + 1, :].partition_broadcast(128)),
              w=["gpost"], dma=True)

    def postnorm_res(self, q, i):
        P = self.P
        y = self.pair(q)
        pn = ["ps%d" % (2 * q), "ps%d" % (2 * q + 1)]
        k = i % 2
        ss = self.ss[:, 2 + k:3 + k]
        rs = self.rs[:, 2 + k:3 + k]
        ssn, rsn = "pss%d" % k, "prs%d" % k
        tmp, tn = self.tmpf[k], "tmpf%d" % k
        hn = "H%d" % i
        P.add("act", lambda e: e.activation(out=tmp, in_=y, func=AF.Square, accum_out=ss), r=pn, w=[tn, ssn])
        self.rstd(ss, rs, 1.0 / D, ssn, rsn)
        P.add("dve", lambda e: e.scalar_tensor_tensor(out=tmp, in0=y, scalar=rs, in1=self.gpost, op0=ALU.mult, op1=ALU.mult),
              r=pn + [rsn, "gpost"], w=[tn])
        P.add("pool", lambda e: e.tensor_tensor(out=self.H[:, i, :], in0=self.H[:, i, :], in1=tmp, op=ALU.add),
              r=[tn, hn], w=[hn])

    def layer(self, l):
        P = self.P
        if l == 0:
            self.epsc = self.st[:, 1:2]
            P.add("dve", lambda e: e.memset(self.epsc, EPS), w=["epsc"])
            self.onec = self.st[:, 2:3]
            P.add("dve", lambda e: e.memset(self.onec, 1.0), w=["onec"])
        m = self.types[l]
        j = sum(1 for t in self.types[:l] if t == m)
        if m in self.mixers:
            self.load_gpost(l, 0)
            if m == 0:
                self.mla(l, j)
            elif m == 1:
                self.dil(l, j)
            else:
                self.fox(l, j)
        self.load_gpost(l, 1)
        for blk in range(2):
            self.ffn_ple(l, blk)

    def ffn_ple(self, l, blk):
        P, dr = self.P, self.dr
        RX, RY, RW = self.RX, self.RY, self.RW
        TB = 1024
        tiles = list(range(blk * 8, blk * 8 + 8))
        RX.reset()
        RY.reset()
        hnT = RX.take("hnT", BF16, [8, TB])
        win = [RX.take("win%d" % k, BF16, [8, 2, 256]) for k in range(2)]
        actT = RY.take(["act%d" % f for f in range(NF)], BF16, [NF, TB])
        if blk == 0:
            RW.reset()
            self.wout = RW.take("wout", BF16, [NF, D])
        wout = self.wout
        w_in = dr["ffn_w_in"][l]
        NG = NF // 2

        def load_win(g):
            k = g % 2
            for s in range(2):
                src = w_in[:, s * DFF + g * 256: s * DFF + g * 256 + 256].rearrange("(c p) n -> p c n", p=128)
                self.wload(win[k][:, :, s, :], src, "win%d" % k)

        load_win(0)
        load_win(1)
        wo = dr["ffn_w_out"][l].rearrange("(f p) n -> p f n", p=128)
        if blk == 0:
            for f0 in range(0, NF, 11):
                self.wload(wout[:, f0:f0 + 11, :], wo[:, f0:f0 + 11, :], "wout")
        self.prenorm_T(l, 1, tiles, hnT, "hnT")
        step = 0
        for g in range(NG):
            k = g % 2
            for ff in range(2):
                f = 2 * g + ff
                for tc in range(2):
                    bg, bu = (0, 1) if step % 2 == 0 else (2, 3)
                    step += 1
                    pg, pu = self.bank(bg), self.bank(bu)

                    def mm(e, k=k, ff=ff, tc=tc, pg=pg, pu=pu):
                        for s, po in ((0, pg), (1, pu)):
                            for c in range(8):
                                ins = e.matmul(po, win[k][:, c, s, ff * 128:(ff + 1) * 128], hnT[:, c, tc * 512:(tc + 1) * 512],
                                               start=(c == 0), stop=(c == 7))
                        return ins
                    P.add("pe", mm, r=["win%d" % k, "hnT"], w=["ps%d" % bg, "ps%d" % bu])
                    sg = self.hb[step % 2][:, 0:512]
                    sgn = "hb%d" % (step % 2)
                    P.add("act", lambda e, sg=sg, pg=pg: e.activation(out=sg, in_=pg, func=AF.Silu), r=["ps%d" % bg], w=[sgn])
                    P.add("dve", lambda e, sg=sg, pu=pu, f=f, tc=tc: e.tensor_tensor(out=actT[:, f, tc * 512:(tc + 1) * 512], in0=sg, in1=pu, op=ALU.mult),
                          r=[sgn, "ps%d" % bu], w=["act%d" % f])
            if g + 2 < NG:
                load_win(g + 2)
        for k, i in enumerate(tiles):
            q = k % 2
            y0, y1 = self.bank(2 * q), self.bank(2 * q + 1)

            def mm2(e, k=k, y0=y0, y1=y1):
                for hf, po in ((0, y0), (1, y1)):
                    for f in range(NF):
                        ins = e.matmul(po, actT[:, f, k * 128:(k + 1) * 128], wout[:, f, hf * 512:(hf + 1) * 512],
                                       start=(f == 0), stop=(f == NF - 1))
                return ins
            P.add("pe", mm2, r=["act%d" % f for f in range(NF)] + ["wout"], w=["ps%d" % (2 * q), "ps%d" % (2 * q + 1)])
            self.postnorm_res(q, i)
        RX.reset()
        RY.reset()
        hT = RX.take("hT", BF16, [8, TB])
        pT = RX.take("pTb", BF16, [2, TB])
        wp = RX.take("wproj", BF16, [2, D])
        wg = RY.take("wgate", BF16, [8, D])
        self.wload(pT, dr["pT"][l][:, blk * TB:(blk + 1) * TB].rearrange("(c p) t -> p c t", p=128), "pTb")
        self.wload(wp, dr["ple_w_proj"][l].rearrange("(c p) n -> p c n", p=128), "wproj")
        self.wload(wg, dr["ple_w_gate"][l].rearrange("(c p) n -> p c n", p=128), "wgate")
        self.prenorm_T(l, 0, tiles, hT, "hT", normalize=False)
        for k, i in enumerate(tiles):
            qg, qp = (0, 1) if k % 2 == 0 else (2, 1)
            qg = 0 if k % 2 == 0 else 2
            qp = 1
            gps, pps = self.pair(qg), self.pair(qp)

            def mmg(e, k=k, qg=qg):
                for hf in range(2):
                    po = self.bank(2 * qg + hf)
                    for c in range(8):
                        ins = e.matmul(po, hT[:, c, k * 128:(k + 1) * 128], wg[:, c, hf * 512:(hf + 1) * 512], start=(c == 0), stop=(c == 7))
                return ins

            def mmp(e, k=k, qp=qp):
                for hf in range(2):
                    po = self.bank(2 * qp + hf)
                    for c in range(2):
                        ins = e.matmul(po, pT[:, c, k * 128:(k + 1) * 128], wp[:, c, hf * 512:(hf + 1) * 512], start=(c == 0), stop=(c == 1))
                return ins
            P.add("pe", mmg, r=["hT", "wgate"], w=["ps%d" % (2 * qg), "ps%d" % (2 * qg + 1)])
            P.add("pe", mmp, r=["pTb", "wproj"], w=["ps%d" % (2 * qp), "ps%d" % (2 * qp + 1)])
            tmp, tn = self.tmpf[k % 2], "tmpf%d" % (k % 2)
            hn = "H%d" % i
            P.add("act", lambda e, tmp=tmp, gps=gps: e.activation(out=tmp, in_=gps, func=AF.Sigmoid),
                  r=["ps%d" % (2 * qg), "ps%d" % (2 * qg + 1)], w=[tn])
            P.add("dve", lambda e, tmp=tmp, pps=pps: e.tensor_tensor(out=tmp, in0=tmp, in1=pps, op=ALU.mult),
                  r=[tn, "ps%d" % (2 * qp), "ps%d" % (2 * qp + 1)], w=[tn])
            P.add("pool", lambda e, tmp=tmp, i=i: e.tensor_tensor(out=self.H[:, i, :], in0=self.H[:, i, :], in1=tmp, op=ALU.add),
                  r=[tn, hn], w=[hn])

    def qk_scratch(self, R, dkp):
        self.Qtok = [R.take("qtok%d" % k, BF16, [2, dkp]) for k in range(2)]
        self.Ktok = [R.take("ktok%d" % k, BF16, [2, dkp]) for k in range(2)]

    def qk_transpose(self, i, dk, qT, kT, last):
        P = self.P
        k2 = i % 2
        b = 6 + ((i // 2) % 2)
        pb = self.bank_bf(b).rearrange("p (s t) -> p s t", s=4)
        Q, K = self.Qtok[k2], self.Ktok[k2]

        def tr(e):
            for s, src in enumerate((Q[:, 0, 0:dk], Q[:, 1, 0:dk], K[:, 0, 0:dk], K[:, 1, 0:dk])):
                ins = e.transpose(pb[0:dk, s, (i % 2) * 128:(i % 2) * 128 + 128], src, self.ident)
            return ins
        P.add("pe", tr, r=["qtok%d" % k2, "ktok%d" % k2, "ident"], w=["ps%d" % b])
        if i % 2 == 1:
            i0 = i - 1
            for s, dst, nm in ((0, qT[0:dk, 0, i0 * 128:i0 * 128 + 256], "qT0"), (1, qT[0:dk, 1, i0 * 128:i0 * 128 + 256], "qT1"),
                               (2, kT[0:dk, 0, i0 * 128:i0 * 128 + 256], "kT0"), (3, kT[0:dk, 1, i0 * 128:i0 * 128 + 256], "kT1")):
                P.add("dve", lambda e, dst=dst, s=s, pb=pb: e.tensor_copy(out=dst, in_=pb[0:dk, s, :]), r=["ps%d" % b], w=[nm])

    def attention(self, hg, qT, kT, V, dk, scale, bias_fn, bias_names, Otok):
        for hh in range(2):
            for qc in range(4):
                self.attn_chunk(hg * 2 + hh, hh, qc, qT, kT, V, dk, scale, bias_fn, bias_names, Otok)

    def attn_chunk(self, h, hh, qc, qT, kT, V, dk, scale, bias_fn, bias_names, Otok):
        P = self.P
        accb = 2 + (self.cctr % 2)
        rcs = self.cctr % 2
        self.cctr += 1
        acc = self.bank(accb)
        first = [True]

        def do_S(j, off, ncols):
            sb = self.sctr % 2
            pt = self.sctr % len(self.PT)
            self.sctr += 1
            S = self.bank(sb)
            PTb = self.PT[pt]
            P.add("pe", lambda e: e.matmul(S[:, 0:ncols], kT[0:dk, hh, j * 128:(j + 1) * 128],
                                           qT[0:dk, hh, qc * 512 + off:(qc + 1) * 512], start=True, stop=True),
                  r=["kT%d" % hh, "qT%d" % hh], w=["ps%d" % sb])
            if j >= 4 * qc:
                P.add("dve", lambda e: e.tensor_tensor(out=S[:, 0:128], in0=S[:, 0:128], in1=self.negm, op=ALU.add),
                      r=["negm"], w=["ps%d" % sb])
            bias = bias_fn(j, h) if bias_fn is not None else None
            if bias is None:
                P.add("act", lambda e: e.activation(out=PTb[:, 0:ncols], in_=S[:, 0:ncols], func=AF.Exp, scale=scale),
                      r=["ps%d" % sb], w=["pt%d" % pt])
            else:
                P.add("act", lambda e: e.activation(out=PTb[:, 0:ncols], in_=S[:, 0:ncols], func=AF.Exp, scale=scale, bias=bias),
                      r=["ps%d" % sb] + bias_names, w=["pt%d" % pt])
            return pt

        def do_PV(j, off, ncols, pt):
            PTb = self.PT[pt]

            def mm(e):
                for ql in range(off // 128, 4):
                    st = first[0]
                    first[0] = False
                    ins = e.matmul(acc[:, ql * 128:ql * 128 + 65], PTb[:, ql * 128 - off:ql * 128 - off + 128],
                                   V[:, j, hh, 0:65], start=st, stop=(j == 4 * qc + ql), skip_group_check=True)
                return ins
            P.add("pe", mm, r=["pt%d" % pt, "V"], w=["ps%d" % accb])

        prev = None
        for j in range(4 * qc + 4):
            off = max(0, j - 4 * qc) * 128
            pt = do_S(j, off, 512 - off)
            if prev is not None:
                do_PV(*prev)
            prev = (j, off, 512 - off, pt)
        do_PV(*prev)
        rc = self.st[:, 4 + 4 * rcs:8 + 4 * rcs]
        rcn = "rc%d" % rcs
        accv = acc.rearrange("p (q c) -> p q c", q=4)
        P.add("dve", lambda e: e.reciprocal(out=rc, in_=accv[:, :, 64]), r=["ps%d" % accb], w=[rcn])
        P.add("dve", lambda e: e.tensor_tensor(out=Otok[:, 4 * qc:4 * qc + 4, h * 64:(h + 1) * 64], in0=accv[:, :, 0:64],
                                               in1=rc.unsqueeze(2).broadcast_to([128, 4, 64]), op=ALU.mult),
              r=["ps%d" % accb, rcn], w=["Otok"])

    def out_proj(self, Otok, wo):
        P = self.P
        for i in range(NT):
            kk = i % 2
            b = 6 + kk
            pb = self.bank_bf(b)
            oT, oTn = self.hb[kk], "hb%d" % kk

            def tr(e, i=i, pb=pb):
                for c in range(8):
                    ins = e.transpose(pb[:, c * 128:(c + 1) * 128], Otok[:, i, c * 128:(c + 1) * 128], self.ident)
                return ins
            P.add("pe", tr, r=["Otok", "ident"], w=["ps%d" % b])
            P.add("dve", lambda e, oT=oT, pb=pb: e.tensor_copy(out=oT, in_=pb), r=["ps%d" % b], w=[oTn])
            q = kk

            def mm(e, oT=oT, q=q):
                for hf in range(2):
                    po = self.bank(2 * q + hf)
                    for c in range(8):
                        ins = e.matmul(po, oT[:, c * 128:(c + 1) * 128], wo[:, c, hf * 512:(hf + 1) * 512], start=(c == 0), stop=(c == 7))
                return ins
            P.add("pe", mm, r=[oTn, "wo"], w=["ps%d" % (2 * q), "ps%d" % (2 * q + 1)])
            self.postnorm_res(q, i)

    def mla(self, l, j):
        P, dr = self.P, self.dr
        RX, RY, RW = self.RX, self.RY, self.RW
        RX.reset(); RY.reset(); RW.reset()
        hnT = RX.take(["hnT%d" % i for i in range(NT)], BF16, [8, T])
        wa = RW.take("wa", BF16, [8, 672])
        wuq = RW.take("wuq", BF16, [3, 1536])
        wukv = RW.take("wukv", BF16, [2, 2048])
        wo = RW.take("wo", BF16, [8, D])
        cqT = RY.take("cqT", BF16, [3, T])
        ckvT = RY.take("ckvT", BF16, [2, T])
        qT = RY.take(["qT0", "qT1"], BF16, [2, T])
        kT = RY.take(["kT0", "kT1"], BF16, [2, T])
        V = RY.take("V", BF16, [NT, 2, 66])
        self.qk_scratch(RY, 96)
        self.PT = [RY.take("pt%d" % k, BF16, [512]) for k in range(2)]
        self.wload(wa, dr["mla_w_a"][j].rearrange("(c p) n -> p c n", p=128), "wa")
        self.wload(wuq, dr["mla_w_uq"][j].rearrange("(c p) n -> p c n", p=128), "wuq")
        self.wload(wukv, dr["mla_w_ukv"][j].rearrange("(c p) n -> p c n", p=128), "wukv")
        self.wload(wo, dr["mla_w_o"][j].rearrange("(c p) n -> p c n", p=128), "wo")
        self.prenorm_T(l, 0, list(range(NT)), hnT, ["hnT%d" % i for i in range(NT)])
        P.add("dve", lambda e: e.memset(V[:, :, :, 64:66], 1.0), w=["V"])
        for i in range(NT):
            q = i % 2
            pa, pbk = self.bank(2 * q), self.bank(2 * q + 1)
            pn = ["ps%d" % (2 * q), "ps%d" % (2 * q + 1)]

            def mm(e, i=i, pa=pa, pbk=pbk):
                for c in range(8):
                    e.matmul(pa[:, 0:384], hnT[:, c, i * 128:(i + 1) * 128], wa[:, c, 0:384], start=(c == 0), stop=(c == 7))
                for c in range(8):
                    ins = e.matmul(pbk[:, 0:288], hnT[:, c, i * 128:(i + 1) * 128], wa[:, c, 384:672], start=(c == 0), stop=(c == 7))
                return ins
            P.add("pe", mm, r=["hnT%d" % i, "wa"], w=pn)
            kk = i % 2
            hb, hbn = self.hb[kk], "hb%d" % kk
            ssq, sskv = self.ss[:, kk:kk + 1], self.rs[:, kk:kk + 1]
            rq, rkv = self.st[:, 12 + 2 * kk:13 + 2 * kk], self.st[:, 13 + 2 * kk:14 + 2 * kk]
            n_ssq, n_sskv, n_rq, n_rkv = "ss%d" % kk, "rs%d" % kk, "mrq%d" % kk, "mrkv%d" % kk
            P.add("act", lambda e, hb=hb, pa=pa, ssq=ssq: e.activation(out=hb[:, 0:384], in_=pa[:, 0:384], func=AF.Square, accum_out=ssq),
                  r=pn, w=[hbn, n_ssq])
            P.add("act", lambda e, hb=hb, pbk=pbk, sskv=sskv: e.activation(out=hb[:, 384:640], in_=pbk[:, 0:256], func=AF.Square, accum_out=sskv),
                  r=pn, w=[hbn, n_sskv])
            self.rstd(ssq, rq, 1.0 / 384, n_ssq, n_rq)
            self.rstd(sskv, rkv, 1.0 / 256, n_sskv, n_rkv)
            P.add("dve", lambda e, hb=hb, pa=pa, rq=rq: e.tensor_scalar(out=hb[:, 0:384], in0=pa[:, 0:384], scalar1=rq, scalar2=None, op0=ALU.mult),
                  r=pn + [n_rq], w=[hbn])
            P.add("dve", lambda e, hb=hb, pbk=pbk, rkv=rkv: e.tensor_scalar(out=hb[:, 384:640], in0=pbk[:, 0:256], scalar1=rkv, scalar2=None, op0=ALU.mult),
                  r=pn + [n_rkv], w=[hbn])
            ta = self.ropes[0][:, 0, :]
            tb = self.ropes[1][:, 0, :]
            P.add("dve", lambda e, ta=ta, pbk=pbk, i=i: e.tensor_tensor(out=ta, in0=pbk[:, 256:288], in1=self.cos2[:, i, :], op=ALU.mult),
                  r=pn + ["cos2"], w=["ropea"])
            P.add("dve", lambda e, tb=tb, pbk=pbk, i=i: e.tensor_tensor(out=tb[:, 0:16], in0=pbk[:, 272:288], in1=self.sin2[:, i, 0:16], op=ALU.mult),
                  r=pn + ["sin2"], w=["ropeb"])
            P.add("dve", lambda e, tb=tb, pbk=pbk, i=i: e.tensor_tensor(out=tb[:, 16:32], in0=pbk[:, 256:272], in1=self.sin2[:, i, 16:32], op=ALU.mult),
                  r=pn + ["sin2"], w=["ropeb"])
            P.add("dve", lambda e, ta=ta, tb=tb, i=i: e.tensor_tensor(out=self.krope[:, i, :], in0=ta, in1=tb, op=ALU.add),
                  r=["ropea", "ropeb"], w=["krope"])
            b = 6 + kk
            pt = self.bank_bf(b)

            def tr(e, hb=hb, pt=pt):
                for c in range(5):
                    ins = e.transpose(pt[:, c * 128:(c + 1) * 128], hb[:, c * 128:(c + 1) * 128], self.ident)
                return ins
            P.add("pe", tr, r=[hbn, "ident"], w=["ps%d" % b])
            ptv = pt.rearrange("p (c t) -> p c t", c=8)
            gq = self.qn[:, 3 * j:3 * j + 3].unsqueeze(2).broadcast_to([128, 3, 128])
            gkv = self.kvn[:, 2 * j:2 * j + 2].unsqueeze(2).broadcast_to([128, 2, 128])
            P.add("dve", lambda e, ptv=ptv, gq=gq, i=i: e.tensor_tensor(out=cqT[:, :, i * 128:(i + 1) * 128], in0=ptv[:, 0:3, :], in1=gq, op=ALU.mult),
                  r=["ps%d" % b, "qn"], w=["cqT"])
            P.add("dve", lambda e, ptv=ptv, gkv=gkv, i=i: e.tensor_tensor(out=ckvT[:, :, i * 128:(i + 1) * 128], in0=ptv[:, 3:5, :], in1=gkv, op=ALU.mult),
                  r=["ps%d" % b, "kvn"], w=["ckvT"])
        if self.stop == 1:
            return
        RX.reset()
        Otok = RX.take("Otok", BF16, [NT, D])
        scale = 96.0 ** -0.5
        for hg in range(8):
            for i in range(NT):
                b = 4 + (i % 2)
                b2 = i % 2
                ps, ps2 = self.bank(b), self.bank(b2)
                psn, psn2 = ["ps%d" % b], ["ps%d" % b2]

                def mm(e, i=i, ps=ps, ps2=ps2, hg=hg):
                    for c in range(3):
                        e.matmul(ps[:, 0:192], cqT[:, c, i * 128:(i + 1) * 128], wuq[:, c, hg * 192:(hg + 1) * 192], start=(c == 0), stop=(c == 2))
                    for c in range(2):
                        ins = e.matmul(ps2[:, 0:256], ckvT[:, c, i * 128:(i + 1) * 128], wukv[:, c, hg * 256:(hg + 1) * 256], start=(c == 0), stop=(c == 1))
                    return ins
                P.add("pe", mm, r=["cqT", "ckvT", "wuq", "wukv"], w=psn + psn2)
                k2 = i % 2
                Q, K = self.Qtok[k2], self.Ktok[k2]
                qn_, kn_ = "qtok%d" % k2, "ktok%d" % k2
                qv = ps[:, 0:192].rearrange("p (h c) -> p h c", h=2)
                kvv = ps2[:, 0:256].rearrange("p (h c) -> p h c", h=2)
                ta, tb = self.ropes[0], self.ropes[1]
                cosb = self.cos2[:, i, :].unsqueeze(1).broadcast_to([128, 2, 32])
                sina = self.sin2[:, i, 0:16].unsqueeze(1).broadcast_to([128, 2, 16])
                sinb = self.sin2[:, i, 16:32].unsqueeze(1).broadcast_to([128, 2, 16])
                P.add("dve", lambda e, Q=Q, qv=qv: e.tensor_copy(out=Q[:, :, 0:64], in_=qv[:, :, 0:64]), r=psn, w=[qn_])
                P.add("dve", lambda e, ta=ta, qv=qv, cosb=cosb: e.tensor_tensor(out=ta, in0=qv[:, :, 64:96], in1=cosb, op=ALU.mult),
                      r=psn + ["cos2"], w=["ropea"])
                P.add("dve", lambda e, tb=tb, qv=qv, sina=sina: e.tensor_tensor(out=tb[:, :, 0:16], in0=qv[:, :, 80:96], in1=sina, op=ALU.mult),
                      r=psn + ["sin2"], w=["ropeb"])
                P.add("dve", lambda e, tb=tb, qv=qv, sinb=sinb: e.tensor_tensor(out=tb[:, :, 16:32], in0=qv[:, :, 64:80], in1=sinb, op=ALU.mult),
                      r=psn + ["sin2"], w=["ropeb"])
                P.add("dve", lambda e, Q=Q, ta=ta, tb=tb: e.tensor_tensor(out=Q[:, :, 64:96], in0=ta, in1=tb, op=ALU.add),
                      r=["ropea", "ropeb"], w=[qn_])
                P.add("act", lambda e, K=K, kvv=kvv: e.activation(out=K[:, :, 0:64], in_=kvv[:, :, 0:64], func=AF.Copy), r=psn2, w=[kn_])
                P.add("pool", lambda e, K=K, i=i: e.tensor_copy(out=K[:, :, 64:96], in_=self.krope[:, i, :].unsqueeze(1).broadcast_to([128, 2, 32])),
                      r=["krope"], w=[kn_])
                P.add("act", lambda e, kvv=kvv, i=i: e.activation(out=V[:, i, :, 0:64], in_=kvv[:, :, 64:128], func=AF.Copy), r=psn2, w=["V"])
                self.qk_transpose(i, 96, qT, kT, i == NT - 1)
            if self.stop not in (2, 4):
                self.attention(hg, qT, kT, V, 96, scale, None, [], Otok)
            if self.stop == 4 and hg == 0:
                P.add("dve", lambda e: e.memset(Otok, 0.5), w=["Otok"])
        if self.stop == 3:
            dbg = self.nc.dram_tensor("dbg", [128, NT * D], BF16, kind="ExternalOutput").ap()
            P.add("sp", lambda e: e.dma_start(out=dbg, in_=Otok.rearrange("p a b -> p (a b)")), r=["Otok"], w=["dbgout"], dma=True)
        if self.stop in (2, 3):
            return
        self.out_proj(Otok, wo)

    def cls_ap(self, ap, d, c4):
        if d == 1:
            return ap[:, c4 * 512:(c4 + 1) * 512]
        if d == 4:
            return ap.rearrange("p (u r) -> p r u", r=4)[:, c4, :]
        return ap.rearrange("p (u r) -> p r u", r=16)[:, 4 * c4:4 * c4 + 4, :]

    def dil(self, l, j):
        P, dr = self.P, self.dr
        RX, RY, RW = self.RX, self.RY, self.RW
        RX.reset(); RY.reset(); RW.reset()
        hnT = RX.take(["hnT%d" % i for i in range(NT)], BF16, [8, T])
        oT = RY.take("oTall", BF16, [8, T])
        UT = RY.take("UTacc", F32, [T])
        UZ = [RY.take("uz%d" % k, BF16, [2, 4, 128]) for k in range(2)]
        wq = [RW.take("wq%d" % k, BF16, [8, 3, 128]) for k in range(2)]
        qkv = [RW.take(nm, BF16, [T]) for nm in ("dqT", "dkT", "dvT")]
        V = RW.take("V", BF16, [NT, 2, 66])
        strip = [RW.take("strip%d" % k, F32, [2, 256]) for k in range(2)]
        PT = [RW.take("pt%d" % k, BF16, [256]) for k in range(3)]
        tmpS = [RW.take("tmpS%d" % k, F32, [256]) for k in range(2)]
        ZT = RW.take("ZTacc", F32, [T])
        w = dr["dil_w_qkv"][j]
        hn_names = ["hnT%d" % i for i in range(NT)]

        combos = [(hp, g) for hp in range(8) for g in range(3)]

        def load_w(ci):
            hp, g = combos[ci]
            k = ci % 2
            for s_ in range(3):
                c0 = g * 3072 + s_ * 1024 + hp * 128
                self.wload(wq[k][:, :, s_, :], w[:, c0:c0 + 128].rearrange("(c p) n -> p c n", p=128), "wq%d" % k)

        def load_strip(ci):
            hp, g = combos[ci]
            k = ci % 2
            self.P.add("sp", lambda e, k=k, g=g, hp=hp: e.dma_start(out=strip[k], in_=dr["dil_strip"][g * 16 + 2 * hp:g * 16 + 2 * hp + 2].rearrange("h s t -> s h t")),
                       w=["strip%d" % k], dma=True)

        load_w(0)
        load_w(1)
        load_strip(0)
        load_strip(1)
        self.prenorm_T(l, 0, list(range(NT)), hnT, hn_names)
        P.add("dve", lambda e: e.memset(V[:, :, :, 64:66], 1.0), w=["V"])
        dils = (1, 4, 16)
        pctr = 0
        only = getattr(self, "only_g", None)
        for ci, (hp, g) in enumerate(combos):
            if only is not None and g != only:
                continue
            d = dils[g]
            n_blk = NT // d
            k = ci % 2
            for s_ in range(3):
                for c4 in range(4):
                    b = pctr % 4
                    pctr += 1
                    ps = self.bank(b)

                    def mm(e, s_=s_, c4=c4, ps=ps, k=k, d=d):
                        for c in range(8):
                            ins = e.matmul(ps, wq[k][:, c, s_, :], self.cls_ap(hnT[:, c, :], d, c4), start=(c == 0), stop=(c == 7))
                        return ins
                    P.add("pe", mm, r=hn_names + ["wq%d" % k], w=["ps%d" % b])
                    dst = qkv[s_][:, c4 * 512:(c4 + 1) * 512]
                    nm = ("dqT", "dkT", "dvT")[s_]
                    if s_ == 1:
                        P.add("dve", lambda e, dst=dst, ps=ps: e.tensor_copy(out=dst, in_=ps), r=["ps%d" % b], w=[nm])
                    else:
                        P.add("act", lambda e, dst=dst, ps=ps: e.activation(out=dst, in_=ps, func=AF.Copy), r=["ps%d" % b], w=[nm])
            for t4 in range(4):
                b = 6 + (t4 % 2)
                pb = self.bank_bf(b)

                def tr(e, t4=t4, pb=pb):
                    for tt in range(4):
                        kt = 4 * t4 + tt
                        ins = e.transpose(pb[:, tt * 128:(tt + 1) * 128], qkv[2][:, kt * 128:(kt + 1) * 128], self.ident)
                    return ins
                P.add("pe", tr, r=["dvT", "ident"], w=["ps%d" % b])
                P.add("dve", lambda e, t4=t4, pb=pb: e.tensor_copy(out=V[:, 4 * t4:4 * t4 + 4, :, 0:64],
                                                                  in_=pb[:, 0:512].rearrange("p (t h c) -> p t h c", t=4, h=2)),
                      r=["ps%d" % b], w=["V"])
            if ci + 2 < len(combos):
                load_w(ci + 2)
            if self.stop == 6:
                for nm_, ap_, dt_, shp in (("dbgV", V.rearrange("p a b c -> p (a b c)"), BF16, [128, NT * 2 * 66]), ("dbgq", qkv[0], BF16, [128, T]),
                                           ("dbgk", qkv[1], BF16, [128, T]), ("dbgh", hnT.rearrange("p a b -> p (a b)"), BF16, [128, 8 * T])):
                    dd = self.nc.dram_tensor(nm_, shp, dt_, kind="ExternalOutput").ap()
                    P.add("sp", lambda e, dd=dd, ap_=ap_: e.dma_start(out=dd, in_=ap_), r=["V", "dqT", "dkT"] + hn_names, w=[nm_], dma=True)
                return
            self.dil_attention(g, d, n_blk, k, hp, qkv, V, strip, PT, tmpS, UZ, UT, ZT)
            if ci + 2 < len(combos):
                load_strip(ci + 2)
            if (g == 2 or only is not None) and self.stop == 5:
                d1 = self.nc.dram_tensor("dbgU", [128, T], F32, kind="ExternalOutput").ap()
                d2 = self.nc.dram_tensor("dbgZ", [128, T], F32, kind="ExternalOutput").ap()
                P.add("sp", lambda e: e.dma_start(out=d1, in_=UT), r=["UTacc"], w=["dbg1"], dma=True)
                P.add("sp", lambda e: e.dma_start(out=d2, in_=ZT), r=["ZTacc"], w=["dbg2"], dma=True)
                return
            if g == 2:
                P.add("dve", lambda e: e.reciprocal(out=ZT, in_=ZT), r=["ZTacc"], w=["ZTacc"])
                P.add("dve", lambda e, hp=hp: e.tensor_tensor(out=oT[:, hp, :], in0=UT, in1=ZT, op=ALU.mult),
                      r=["UTacc", "ZTacc"], w=["oTall"])
        RW.reset()
        wo = RW.take("wo", BF16, [8, D])
        self.wload(wo, dr["dil_w_o"][j].rearrange("(c p) n -> p c n", p=128), "wo")
        for i in range(NT):
            q = i % 2

            def mm(e, i=i, q=q):
                for hf in range(2):
                    po = self.bank(2 * q + hf)
                    for c in range(8):
                        ins = e.matmul(po, oT[:, c, i * 128:(i + 1) * 128], wo[:, c, hf * 512:(hf + 1) * 512], start=(c == 0), stop=(c == 7))
                return ins
            P.add("pe", mm, r=["oTall", "wo"], w=["ps%d" % (2 * q), "ps%d" % (2 * q + 1)])
            self.postnorm_res(q, i)

    def dil_attention(self, g, d, n_blk, k, hp, qkv, V, strip, PT, tmpS, UZ, UT, ZT):
        qT, kT = qkv[0], qkv[1]
        gg = 0 if getattr(self, "only_g", None) is not None else g
        pending = []
        SKEW = 2

        def flush_one():
            st = pending.pop(0)
            self.dil_pv(st, V, PT)
            if st["kt"] % 4 == 3 and st["hh"] == 1:
                self.dil_chunk_out(gg, d, st["kt"] // 4, UZ, UT, ZT)

        for kt in range(NT):
            jb = kt % n_blk
            has_next = (jb + 1 < n_blk)
            ncols = 256 if has_next else 128
            for hh in range(2):
                st = self.dil_scores(kt, jb, has_next, ncols, hh, k, qT, kT, strip, PT, tmpS)
                pending.append(st)
                if len(pending) > SKEW:
                    flush_one()
        while pending:
            flush_one()

    def dil_scores(self, kt, jb, has_next, ncols, hh, k, qT, kT, strip, PT, tmpS):
        P = self.P
        sb = self.sctr % 2
        pt = self.sctr % 3
        ts = self.sctr % 2
        self.sctr += 1
        S = self.bank(sb)
        rows = slice(hh * 64, hh * 64 + 64)
        P.add("pe", lambda e: e.matmul(S[:, 0:ncols], kT[rows, kt * 128:(kt + 1) * 128], qT[rows, kt * 128:kt * 128 + ncols], start=True, stop=True),
              r=["dkT", "dqT"], w=["ps%d" % sb])
        tm = tmpS[ts]
        P.add("dve", lambda e: e.scalar_tensor_tensor(out=tm[:, 0:ncols], in0=S[:, 0:ncols], scalar=0.125, in1=strip[k][:, hh, 0:ncols],
                                                      op0=ALU.mult, op1=ALU.add),
              r=["ps%d" % sb, "strip%d" % k], w=["tmpS%d" % ts])
        PTb = PT[pt]
        P.add("act", lambda e: e.activation(out=PTb[:, 0:ncols], in_=tm[:, 0:ncols], func=AF.Exp), r=["tmpS%d" % ts], w=["pt%d" % pt])
        return dict(kt=kt, jb=jb, has_next=has_next, hh=hh, pt=pt)

    def dil_pv(self, st, V, PT):
        P = self.P
        kt, jb, hh, pt = st["kt"], st["jb"], st["hh"], st["pt"]
        PTb = PT[pt]

        def acc_ap(qt):
            bank = (2 if (qt // 4) % 2 == 0 else 4) + hh
            return self.bank(bank)[:, (qt % 4) * 128:(qt % 4) * 128 + 65], "ps%d" % bank

        a0, n0 = acc_ap(kt)
        P.add("pe", lambda e: e.matmul(a0, PTb[:, 0:128], V[:, kt, hh, 0:65], start=(jb == 0), stop=True), r=["pt%d" % pt, "V"], w=[n0])
        if st["has_next"]:
            a1, n1 = acc_ap(kt + 1)
            P.add("pe", lambda e: e.matmul(a1, PTb[:, 128:256], V[:, kt, hh, 0:65], start=True, stop=False), r=["pt%d" % pt, "V"], w=[n1])

    def dil_chunk_out(self, g, d, ch, UZ, UT, ZT):
        P = self.P
        uz = UZ[ch % 2]
        uzn = "uz%d" % (ch % 2)
        for hh in range(2):
            bank = (2 if ch % 2 == 0 else 4) + hh
            accv = self.bank(bank).rearrange("p (q c) -> p q c", q=4)
            P.add("dve", lambda e, accv=accv, hh=hh: e.tensor_copy(out=uz[:, 0, :, hh * 64:(hh + 1) * 64], in_=accv[:, :, 0:64]),
                  r=["ps%d" % bank], w=[uzn])
            P.add("dve", lambda e, accv=accv, hh=hh: e.tensor_copy(out=uz[:, 1, :, hh * 64:(hh + 1) * 64], in_=accv[:, :, 64:65].broadcast_to([128, 4, 64])),
                  r=["ps%d" % bank], w=[uzn])
        b = 6 + (ch % 2)
        pb = self.bank_bf(b)

        def tr(e):
            for a in range(2):
                for tt in range(4):
                    ins = e.transpose(pb[:, (a * 4 + tt) * 128:(a * 4 + tt + 1) * 128], uz[:, a, tt, :], self.ident)
            return ins
        P.add("pe", tr, r=[uzn, "ident"], w=["ps%d" % b])
        for a, acc_t, nm in ((0, UT, "UTacc"), (1, ZT, "ZTacc")):
            dst = self.cls_ap(acc_t, d, ch)
            src = pb[:, a * 512:(a + 1) * 512]
            if d == 16:
                src = src.rearrange("p (r u) -> p r u", r=4)
            if g == 0:
                P.add("dve", lambda e, dst=dst, src=src: e.tensor_copy(out=dst, in_=src), r=["ps%d" % b], w=[nm])
            else:
                P.add("dve", lambda e, dst=dst, src=src: e.tensor_tensor(out=dst, in0=dst, in1=src, op=ALU.add), r=["ps%d" % b, nm], w=[nm])

    def fox(self, l, j):
        P, dr = self.P, self.dr
        RX, RY, RW = self.RX, self.RY, self.RW
        RX.reset(); RY.reset(); RW.reset()
        hnT = RX.take(["hnT%d" % i for i in range(NT)], BF16, [8, T])
        Otok = RY.take("Otok", BF16, [NT, D])
        qT = RY.take(["qT0", "qT1"], BF16, [2, T])
        self.qk_scratch(RY, 68)
        wf = RW.take("wf", BF16, [8, 16])
        wq = [RW.take("wq%d" % k, BF16, [8, 3, 128]) for k in range(2)]
        wo = RW.take("wo", BF16, [8, D])
        kT = RW.take(["kT0", "kT1"], BF16, [2, T])
        V = RW.take("V", BF16, [NT, 2, 66])
        self.PT = [RW.take("pt%d" % k, BF16, [512]) for k in range(3)]
        w = dr["fox_w_qkvf"][j]

        def load_wq(hg):
            k = hg % 2
            for s in range(3):
                src = w[:, s * 1024 + hg * 128: s * 1024 + hg * 128 + 128].rearrange("(c p) n -> p c n", p=128)
                self.wload(wq[k][:, :, s, :], src, "wq%d" % k)

        self.wload(wf, w[:, 3072:3088].rearrange("(c p) n -> p c n", p=128), "wf")
        load_wq(0)
        load_wq(1)
        self.wload(wo, dr["fox_w_o"][j].rearrange("(c p) n -> p c n", p=128), "wo")
        self.prenorm_T(l, 0, list(range(NT)), hnT, ["hnT%d" % i for i in range(NT)])
        P.add("dve", lambda e: e.memset(V[:, :, :, 64:66], 1.0), w=["V"])
        for k in range(2):
            Kt = self.Ktok[k]
            P.add("dve", lambda e, Kt=Kt: e.memset(Kt[:, :, 64:68], 1.0), w=["ktok%d" % k])
        t0, t1 = self.tmpf[0], self.tmpf[1]
        LF = t0[:, 0:256]
        CS = t0[:, 256:512]
        C8 = t0[:, 512:768]
        R1 = t0[:, 768:1024]
        AUG = t1[:, 0:384].bitcast(BF16)[:, 0:768].rearrange("p (i h a) -> p i h a", i=NT, h=16)
        HB = t1[:, 384:512].bitcast(BF16)
        H32 = t1[:, 512:768]
        PRE = t1[:, 768:1024]
        LFv = LF.rearrange("p (i h) -> p i h", i=NT)
        for i in range(NT):
            b = 4 + (i % 2)
            ps = self.bank(b)

            def mm(e, i=i, ps=ps):
                for c in range(8):
                    ins = e.matmul(ps[:, 0:16], hnT[:, c, i * 128:(i + 1) * 128], wf[:, c, :], start=(c == 0), stop=(c == 7))
                return ins
            P.add("pe", mm, r=["hnT%d" % i, "wf"], w=["ps%d" % b])
            P.add("dve", lambda e, i=i, ps=ps: e.tensor_tensor(out=LFv[:, i, :], in0=ps[:, 0:16], in1=self.bf, op=ALU.add),
                  r=["ps%d" % b, "bf"], w=["tmpf0"])
        P.add("act", lambda e: e.activation(out=LF, in_=LF, func=AF.Exp, scale=-1.0), r=["tmpf0"], w=["tmpf0"])
        P.add("act", lambda e: e.activation(out=LF, in_=LF, func=AF.Ln, bias=self.onec), r=["tmpf0", "onec"], w=["tmpf0"])
        pw, pt_ = self.bank(4), self.bank(5)
        P.add("pe", lambda e: e.matmul(pw[:, 0:256], self.trif, LF, start=True, stop=True), r=["trif", "tmpf0"], w=["ps4"])
        P.add("pe", lambda e: e.matmul(pt_[:, 0:256], self.onesf, LF, start=True, stop=True), r=["onesf", "tmpf0"], w=["ps5"])
        PREv = PRE.rearrange("p (i h) -> p i h", i=NT)
        ptv = pt_[:, 0:256].rearrange("p (i h) -> p i h", i=NT)
        P.add("dve", lambda e: e.memset(PREv[:, 0, :], 0.0), w=["tmpf1"])
        for i in range(1, NT):
            P.add("dve", lambda e, i=i: e.tensor_tensor(out=PREv[:, i, :], in0=PREv[:, i - 1, :], in1=ptv[:, i - 1, :], op=ALU.add),
                  r=["tmpf1", "ps5"], w=["tmpf1"])
        P.add("dve", lambda e: e.tensor_tensor(out=CS, in0=pw[:, 0:256], in1=PRE, op=ALU.add), r=["ps4", "tmpf1"], w=["tmpf0"])
        P.add("dve", lambda e: e.tensor_scalar(out=C8, in0=CS, scalar1=-8.0, scalar2=None, op0=ALU.mult), r=["tmpf0"], w=["tmpf0"])
        AUGf = AUG.rearrange("p i h a -> p (i h) a")
        for a in range(3):
            src = C8 if a == 0 else R1
            P.add("dve", lambda e, a=a, src=src: e.tensor_copy(out=AUGf[:, :, a], in_=src), r=["tmpf0"], w=["tmpf1"])
            if a < 2:
                P.add("dve", lambda e, a=a: e.tensor_copy(out=H32, in_=AUGf[:, :, a]), r=["tmpf1"], w=["tmpf1"])
                P.add("dve", lambda e, src=src: e.tensor_tensor(out=R1, in0=src, in1=H32, op=ALU.subtract), r=["tmpf0", "tmpf1"], w=["tmpf0"])
        CSv = CS.rearrange("p (i h) -> p i h", i=NT)
        for hg in range(8):
            k = hg % 2
            for i in range(NT):
                b = 4 + (i % 2)
                b2 = i % 2
                ps, ps2 = self.bank(b), self.bank(b2)
                psn, psn2 = ["ps%d" % b], ["ps%d" % b2]

                def mm(e, i=i, ps=ps, ps2=ps2, k=k):
                    for c in range(8):
                        e.matmul(ps[:, 0:128], hnT[:, c, i * 128:(i + 1) * 128], wq[k][:, c, 0, :], start=(c == 0), stop=(c == 7))
                    for c in range(8):
                        ins = e.matmul(ps2[:, 0:256], hnT[:, c, i * 128:(i + 1) * 128], wq[k][:, c, 1:3, :], start=(c == 0), stop=(c == 7))
                    return ins
                P.add("pe", mm, r=["hnT%d" % i, "wq%d" % k], w=psn + psn2)
                k2 = i % 2
                Q, K = self.Qtok[k2], self.Ktok[k2]
                qn_, kn_ = "qtok%d" % k2, "ktok%d" % k2
                pq = ps[:, 0:128].rearrange("p (h c) -> p h c", h=2)
                pv = ps2[:, 0:256].rearrange("p (s h c) -> p s h c", s=2, h=2)
                P.add("dve", lambda e, Q=Q, pq=pq: e.tensor_copy(out=Q[:, :, 0:64], in_=pq), r=psn, w=[qn_])
                P.add("dve", lambda e, Q=Q, i=i, hg=hg: e.tensor_copy(out=Q[:, :, 64:67], in_=AUG[:, i, 2 * hg:2 * hg + 2, :]), r=["tmpf1"], w=[qn_])
                P.add("act", lambda e, K=K, pv=pv: e.activation(out=K[:, :, 0:64], in_=pv[:, 0, :, :], func=AF.Copy), r=psn2, w=[kn_])
                P.add("act", lambda e, pv=pv, i=i: e.activation(out=V[:, i, :, 0:64], in_=pv[:, 1, :, :], func=AF.Copy), r=psn2, w=["V"])
                self.qk_transpose(i, 67, qT, kT, i == NT - 1)
            if hg + 2 < 8:
                load_wq(hg + 2)
            self.attention(hg, qT, kT, V, 67, 0.125, lambda jj, h: CSv[:, jj, h:h + 1], ["tmpf0"], Otok)
        self.out_proj(Otok, wo)

def _t5_bucket_const(dist):
    max_exact = 16
    n = np.maximum(dist.astype(np.float32), np.float32(1.0))
    large = max_exact + (np.log(n / np.float32(max_exact)) / np.float32(math.log(2048 / max_exact))
                         * np.float32(32 - max_exact)).astype(np.int32)
    large = np.minimum(large, 31)
    return np.where(dist < max_exact, dist, large)


def host_inputs(inputs, b):
    f32 = np.float32
    m = {}
    m["x"] = np.ascontiguousarray(inputs["x"][b])
    m["pT"] = np.ascontiguousarray(np.transpose(inputs["p"][:, b], (0, 2, 1)))
    m["pos"] = np.ascontiguousarray(inputs["positions"][b].reshape(NT, 128).T.astype(np.int32))
    g = inputs["norm_g"]
    gpre = np.stack([g[:, 0], g[:, 2]], axis=1)
    m["gpre"] = np.ascontiguousarray(gpre.reshape(DEPTH * 2, 8, 128).transpose(2, 0, 1).reshape(128, DEPTH * 2 * 8))
    m["gpost"] = np.ascontiguousarray(np.stack([g[:, 1], g[:, 3]], axis=1).reshape(DEPTH * 2, D))
    for k in ("ffn_w_in", "ffn_w_out", "ple_w_proj", "ple_w_gate", "mla_w_a", "mla_w_uq", "mla_w_ukv", "mla_w_o",
              "dil_w_qkv", "dil_w_o", "fox_w_qkvf", "fox_b_f", "fox_w_o"):
        m[k] = inputs[k]
    m["mla_qn"] = np.ascontiguousarray(inputs["mla_q_norm"].reshape(2, 3, 128).transpose(2, 0, 1).reshape(128, 6))
    m["mla_kvn"] = np.ascontiguousarray(inputs["mla_kv_norm"].reshape(2, 2, 128).transpose(2, 0, 1).reshape(128, 4))
    sidx = np.arange(128)[:, None]
    tidx = np.arange(256)[None, :]
    rel = tidx - sidx
    valid = (rel >= 0) & (rel <= 128)
    strips = np.full((48, 128, 256), -30000.0, f32)
    rb = inputs["rel_bias"]
    for g_, d_ in enumerate((1, 4, 16)):
        bucket = _t5_bucket_const(np.clip(rel, 0, None) * d_)
        for h_ in range(16):
            col = g_ * 16 + h_
            strips[col] = np.where(valid, rb[bucket, col], f32(-30000.0))
    m["dil_strip"] = strips
    m["ident"] = np.eye(128).astype(ml_dtypes.bfloat16)
    tri = (np.arange(128)[:, None] <= np.arange(128)[None, :])
    m["trif"] = tri.astype(f32)
    inv = (np.float32(10000.0) ** (-np.arange(16, dtype=f32) / np.float32(16))).astype(f32)
    m["invf"] = np.ascontiguousarray(np.broadcast_to(inv[None, :], (128, 16))).astype(f32)
    return m


_NC_CACHE = {}


def get_nc(n_layers=DEPTH, mixers=(0, 1, 2)):
    key = (n_layers, tuple(mixers))
    if key not in _NC_CACHE:
        _NC_CACHE[key] = Builder(n_layers, mixers).build()
    return _NC_CACHE[key]


def kernel(**inputs):
    inputs = {k: np.asarray(v) for k, v in inputs.items()}
    nc = get_nc()
    in_maps = [host_inputs(inputs, b) for b in range(8)]
    res = run_bass_kernel_spmd(nc, in_maps, core_ids=list(range(8)))
    out = np.stack([np.asarray(r["out"]) for r in res.results], axis=0)
    return out.astype(np.float32)
```

```python
import contextlib
import math
import numpy as np
import ml_dtypes
import concourse.bass as bass
import concourse.mybir as mybir
from concourse.bass_utils import run_bass_kernel_spmd

F32 = mybir.dt.float32
BF16 = mybir.dt.bfloat16
I32 = mybir.dt.int32
AF = mybir.ActivationFunctionType
ALU = mybir.AluOpType

D = 1024
T = 2048
NT = 16
DFF = 2816
NF = 22
DEPTH = 4
EPS = 1e-6
TWO_PI = 2.0 * math.pi


def _size(dt):
    return mybir.dt.size(dt)


class Op:
    __slots__ = ("eng", "fn", "deps", "dma", "signal", "count", "dsem", "dval", "idx")


class Prog:
    def __init__(self):
        self.q = {"pe": [], "act": [], "dve": [], "pool": [], "sp": []}
        self.state = {}
        self.ndma = {"pool": 0, "sp": 0}
        self.NDSEM = 8

    def _st(self, name):
        st = self.state.get(name)
        if st is None:
            st = self.state[name] = [None, {}, []]
        return st

    def inherit(self, new, olds):
        ops = {}
        for n in olds:
            st = self.state.get(n)
            if st is None:
                continue
            for o in [st[0]] + list(st[1].values()) + list(st[2]):
                if o is not None:
                    ops[id(o)] = o
        st = self._st(new)
        for o in ops.values():
            st[2].append(o)

    def add(self, eng, fn, r=(), w=(), dma=False):
        op = Op()
        op.eng, op.fn, op.dma, op.signal = eng, fn, dma, False
        deps = {}

        def dep(o):
            if o is None or o is op:
                return
            deps[id(o)] = o

        for name in r:
            st = self._st(name)
            dep(st[0])
            if name.startswith("ps"):
                for e2, o2 in st[1].items():
                    if e2 != eng:
                        dep(o2)
        for name in w:
            st = self._st(name)
            dep(st[0])
            for o2 in st[1].values():
                dep(o2)
            for o2 in st[2]:
                dep(o2)
        for name in r:
            st = self.state[name]
            if dma:
                st[2].append(op)
            else:
                st[1][eng] = op
        for name in w:
            self.state[name] = [op, {}, []]
        dl = []
        best = {}
        for o in deps.values():
            if o.dma:
                dl.append(o)
                continue
            if o.eng == eng and not dma and eng == "pe":
                continue
            b = best.get(o.eng)
            if b is None or o.idx > b.idx:
                best[o.eng] = o
        dl.extend(best.values())
        op.deps = dl
        op.idx = len(self.q[eng])
        if dma:
            k = self.ndma[eng]
            self.ndma[eng] += 1
            op.dsem = k % self.NDSEM
            op.dval = 16 * (k // self.NDSEM + 1)
        self.q[eng].append(op)
        return op

    def finalize(self):
        for e in self.q:
            for op in self.q[e]:
                for d in op.deps:
                    if not d.dma:
                        d.signal = True
        for e in self.q:
            c = 0
            for op in self.q[e]:
                if op.dma:
                    continue
                if op.signal:
                    c += 1
                    op.count = c

    def emit(self, e, engine, sems, dsems):
        known = {}

        def wait(key, sem, val):
            if known.get(key, 0) >= val:
                return
            known[key] = val
            engine.wait_ge(sem, val)

        for op in self.q[e]:
            for d in op.deps:
                if d.dma:
                    wait(("d", d.eng, d.dsem), dsems[d.eng][d.dsem], d.dval)
                else:
                    wait(("c", d.eng), sems[d.eng], d.count)
            if op.dma:
                if op.dval > 16:
                    wait(("d", e, op.dsem), dsems[e][op.dsem], op.dval - 16)
                ins = op.fn(engine)
                ins.then_inc(dsems[e][op.dsem], 16)
            else:
                ins = op.fn(engine)
                if op.signal:
                    ins.then_inc(sems[e], 1)
        if e in self.ndma:
            for k in range(self.NDSEM):
                n = (self.ndma[e] - k + self.NDSEM - 1) // self.NDSEM
                if n > 0:
                    engine.wait_ge(dsems[e][k], 16 * n)


def sub(handle, dt, off_bytes, shape):
    es = _size(dt)
    h = handle.bitcast(dt)
    n = int(np.prod(shape))
    assert off_bytes % es == 0
    o = off_bytes // es
    ap = h[:, o:o + n]
    if len(shape) == 2:
        ap = ap.rearrange("p (a b) -> p a b", a=shape[0], b=shape[1])
    elif len(shape) == 3:
        ap = ap.rearrange("p (a b c) -> p a b c", a=shape[0], b=shape[1], c=shape[2])
    return ap


class Region:
    def __init__(self, P, handle, nbytes):
        self.P, self.h, self.size = P, handle, nbytes
        self.cur, self.old, self.off = [], [], 0

    def reset(self):
        self.old += self.cur
        self.cur = []
        self.off = 0

    def take(self, names, dt, shape):
        if isinstance(names, str):
            names = [names]
        n = int(np.prod(shape)) * _size(dt)
        n = (n + 3) // 4 * 4
        off = self.off
        self.off += n
        assert self.off <= self.size, (names, self.off, self.size)
        for nm in names:
            self.cur.append(nm)
            self.P.inherit(nm, self.old)
        return sub(self.h, dt, off, shape)


class Builder:
    def __init__(self, n_layers=DEPTH, mixers=(0, 1, 2), types=(0, 1, 2, 0)):
        self.n_layers = n_layers
        self.mixers = mixers
        self.types = list(types)
        self.cctr = 0
        self.sctr = 0
        self.stop = 0

    def build(self):
        nc = bass.Bass("TRN2", target_bir_lowering=False)
        self.nc = nc
        P = self.P = Prog()
        dr = {}

        def din(name, shape, dt=F32):
            dr[name] = nc.dram_tensor(name, list(shape), dt, kind="ExternalInput").ap()

        din("x", [T, D])
        din("pT", [DEPTH, 256, T])
        din("pos", [128, NT], I32)
        din("gpre", [128, DEPTH * 2 * 8])
        din("gpost", [DEPTH * 2, D])
        din("ffn_w_in", [DEPTH, D, 2 * DFF])
        din("ffn_w_out", [DEPTH, DFF, D])
        din("ple_w_proj", [DEPTH, 256, D])
        din("ple_w_gate", [DEPTH, D, D])
        din("mla_w_a", [2, D, 672])
        din("mla_qn", [128, 6])
        din("mla_kvn", [128, 4])
        din("mla_w_uq", [2, 384, 1536])
        din("mla_w_ukv", [2, 256, 2048])
        din("mla_w_o", [2, D, D])
        din("dil_w_qkv", [1, D, 9216])
        din("dil_w_o", [1, D, D])
        din("dil_strip", [48, 128, 256])
        din("fox_w_qkvf", [1, D, 3088])
        din("fox_b_f", [1, 16])
        din("fox_w_o", [1, D, D])
        din("ident", [128, 128], BF16)
        din("trif", [128, 128])
        din("invf", [128, 16])
        dr["out"] = nc.dram_tensor("out", [T, D], F32, kind="ExternalOutput").ap()
        self.dr = dr

        with (
            nc.sbuf_tensor("H", [128, NT, D], F32) as H,
            nc.sbuf_tensor("X", [128, 16384], BF16) as X,
            nc.sbuf_tensor("Y", [128, 22528], BF16) as Y,
            nc.sbuf_tensor("W", [128, 22528], BF16) as W,
            nc.sbuf_tensor("C", [128, 6100], F32) as C,
            nc.psum_tensor("PA", [128, 1024], F32) as PA,
            nc.psum_tensor("PB", [128, 1024], F32) as PB,
            nc.psum_tensor("PC", [128, 1024], F32) as PC,
            nc.psum_tensor("PD", [128, 1024], F32) as PD,
        ):
            self.H = H
            self.RX = Region(P, X, 32768)
            self.RY = Region(P, Y, 45056)
            self.RW = Region(P, W, 45056)
            self.RC = Region(P, C, 24400)
            self.PS = [PA, PB, PC, PD]
            self.setup_consts()
            self.load_x()
            for l in range(self.n_layers):
                self.layer(l)
            self.store_out()
            P.finalize()

            with contextlib.ExitStack() as es:
                sems = {e: es.enter_context(nc.semaphore("s_" + e)) for e in ("pe", "act", "dve", "pool", "sp")}
                dsems = {e: [es.enter_context(nc.semaphore("d_%s%d" % (e, k))) for k in range(P.NDSEM)]
                         for e in ("pool", "sp")}
                block = es.enter_context(nc.Block())

                @block.tensor
                def _(e):
                    P.emit("pe", e, sems, dsems)

                @block.scalar
                def _(e):
                    P.emit("act", e, sems, dsems)

                @block.vector
                def _(e):
                    P.emit("dve", e, sems, dsems)

                @block.gpsimd
                def _(e):
                    P.emit("pool", e, sems, dsems)

                @block.sync
                def _(e):
                    P.emit("sp", e, sems, dsems)
        return nc

    def bank(self, b):
        return self.PS[b // 2][:, (b % 2) * 512:(b % 2) * 512 + 512]

    def bank_bf(self, b):
        h = self.PS[b // 2].bitcast(BF16)
        return h[:, (b % 2) * 1024:(b % 2) * 1024 + 1024]

    def pair(self, q):
        return self.PS[q][:, :]

    def setup_consts(self):
        P, dr, RC = self.P, self.dr, self.RC
        self.ident = RC.take("ident", BF16, [128])
        self.negm = RC.take("negm", F32, [128])
        self.trif = RC.take("trif", F32, [128])
        self.onesf = RC.take("onesf", F32, [128])
        self.gpre = RC.take("gpre", F32, [DEPTH * 2 * 8])
        self.invf = RC.take("invf", F32, [16])
        self.posi = RC.take("posi", I32, [NT])
        self.qn = RC.take("qn", F32, [6])
        self.kvn = RC.take("kvn", F32, [4])
        self.bf = RC.take("bf", F32, [16])
        self.ss = RC.take(["ss0", "ss1", "pss0", "pss1"], F32, [4])
        self.rs = RC.take(["rs0", "rs1", "prs0", "prs1"], F32, [4])
        self.st = RC.take("st", F32, [16])
        self.cos2 = RC.take("cos2", F32, [NT, 32])
        self.sin2 = RC.take("sin2", F32, [NT, 32])
        self.gpost = RC.take("gpost", F32, [D])
        self.hb = [RC.take("hb%d" % k, BF16, [D]) for k in range(2)]
        self.tmpf = [RC.take("tmpf%d" % k, F32, [D]) for k in range(2)]
        self.krope = RC.take("krope", BF16, [NT, 32])
        self.ropes = [RC.take(nm, F32, [2, 32]) for nm in ("ropea", "ropeb")]
        for nm, ap in (("ident", self.ident), ("trif", self.trif), ("gpre", self.gpre),
                       ("invf", self.invf), ("qn", self.qn), ("kvn", self.kvn)):
            src = {"qn": "mla_qn", "kvn": "mla_kvn"}.get(nm, nm)
            P.add("sp", lambda e, ap=ap, src=src: e.dma_start(out=ap, in_=dr[src]), w=[nm], dma=True)
        P.add("sp", lambda e: e.dma_start(out=self.posi, in_=dr["pos"]), w=["posi"], dma=True)
        P.add("sp", lambda e: e.dma_start(out=self.bf.unsqueeze(1), in_=dr["fox_b_f"].partition_broadcast(128)),
              w=["bf"], dma=True)
        P.add("dve", lambda e: e.memset(self.onesf, 1.0), w=["onesf"])
        P.add("dve", lambda e: e.tensor_scalar(out=self.negm, in0=self.trif, scalar1=240000.0, scalar2=-240000.0, op0=ALU.mult, op1=ALU.add),
              r=["trif"], w=["negm"])
        self.rope_tables()

    def rope_tables(self):
        P = self.P
        ang = self.tmpf[0][:, 0:256].rearrange("p (i f) -> p i f", i=NT)
        kf = self.tmpf[0][:, 256:512].rearrange("p (i f) -> p i f", i=NT)
        ki = self.tmpf[0][:, 512:768].bitcast(I32).rearrange("p (i f) -> p i f", i=NT)
        m = self.tmpf[0][:, 768:1024].rearrange("p (i f) -> p i f", i=NT)
        posf = self.tmpf[1][:, 0:NT]
        ab = self.tmpf[1][:, 256:512].rearrange("p (i f) -> p i f", i=NT)
        r = ["tmpf0"]
        P.add("dve", lambda e: e.tensor_copy(out=posf, in_=self.posi), r=["posi"], w=["tmpf1"])
        P.add("dve", lambda e: e.tensor_tensor(out=ang, in0=posf.unsqueeze(2).broadcast_to([128, NT, 16]),
                                               in1=self.invf.unsqueeze(1).broadcast_to([128, NT, 16]), op=ALU.mult),
              r=["tmpf1", "invf"], w=r)
        P.add("dve", lambda e: e.tensor_scalar(out=kf, in0=ang, scalar1=1.0 / TWO_PI, scalar2=None, op0=ALU.mult), r=r, w=r)
        P.add("dve", lambda e: e.tensor_copy(out=ki, in_=kf), r=r, w=r)
        P.add("dve", lambda e: e.tensor_copy(out=kf, in_=ki), r=r, w=r)
        C1 = 6.28125
        C2 = TWO_PI - C1
        P.add("dve", lambda e: e.scalar_tensor_tensor(out=ang, in0=kf, scalar=-C1, in1=ang, op0=ALU.mult, op1=ALU.add), r=r, w=r)
        P.add("dve", lambda e: e.scalar_tensor_tensor(out=ang, in0=kf, scalar=-C2, in1=ang, op0=ALU.mult, op1=ALU.add), r=r, w=r)
        P.add("dve", lambda e: e.tensor_scalar(out=m, in0=ang, scalar1=math.pi, scalar2=-TWO_PI, op0=ALU.is_gt, op1=ALU.mult), r=r, w=r)
        P.add("dve", lambda e: e.tensor_tensor(out=ang, in0=ang, in1=m, op=ALU.add), r=r, w=r)
        P.add("dve", lambda e: e.tensor_scalar(out=m, in0=ang, scalar1=-math.pi, scalar2=TWO_PI, op0=ALU.is_lt, op1=ALU.mult), r=r, w=r)
        P.add("dve", lambda e: e.tensor_tensor(out=ang, in0=ang, in1=m, op=ALU.add), r=r, w=r)
        P.add("dve", lambda e: e.tensor_scalar(out=ang, in0=ang, scalar1=math.pi, scalar2=-math.pi, op0=ALU.min, op1=ALU.max), r=r, w=r)
        P.add("dve", lambda e: e.tensor_scalar(out=ab, in0=ang, scalar1=-1.0, scalar2=None, op0=ALU.mult), r=r, w=["tmpf1"])
        P.add("dve", lambda e: e.tensor_tensor(out=ab, in0=ab, in1=ang, op=ALU.max), r=r + ["tmpf1"], w=["tmpf1"])
        P.add("act", lambda e: e.activation(out=self.sin2[:, :, 16:32], in_=ang, func=AF.Sin), r=r, w=["sin2"])
        hp = self.st[:, 0:1]
        P.add("dve", lambda e: e.memset(hp, math.pi / 2), w=["st"])
        P.add("act", lambda e: e.activation(out=self.cos2[:, :, 0:16], in_=ab, func=AF.Sin, scale=-1.0, bias=hp),
              r=["tmpf1", "st"], w=["cos2"])
        P.add("dve", lambda e: e.tensor_copy(out=self.cos2[:, :, 16:32], in_=self.cos2[:, :, 0:16]), r=["cos2"], w=["cos2"])
        P.add("dve", lambda e: e.tensor_scalar(out=self.sin2[:, :, 0:16], in0=self.sin2[:, :, 16:32], scalar1=-1.0, scalar2=None, op0=ALU.mult),
              r=["sin2"], w=["sin2"])

    def load_x(self):
        P, dr = self.P, self.dr
        xv = dr["x"].rearrange("(i p) d -> p i d", p=128)
        for i0 in range(0, NT, 4):
            P.add("sp", lambda e, i0=i0: e.dma_start(out=self.H[:, i0:i0 + 4, :], in_=xv[:, i0:i0 + 4, :]),
                  w=["H%d" % i for i in range(i0, i0 + 4)], dma=True)

    def store_out(self):
        P, dr = self.P, self.dr
        ov = dr["out"].rearrange("(i p) d -> p i d", p=128)
        for i0 in range(0, NT, 4):
            P.add("sp", lambda e, i0=i0: e.dma_start(out=ov[:, i0:i0 + 4, :], in_=self.H[:, i0:i0 + 4, :]),
                  r=["H%d" % i for i in range(i0, i0 + 4)], w=["out%d" % i0], dma=True)

    def wload(self, dst, src, wname):
        self.P.add("pool", lambda e: e.dma_start(out=dst, in_=src), w=[wname], dma=True)

    def rstd(self, ss_ap, out_ap, inv_n, rn, wn):
        P = self.P
        P.add("act", lambda e: e.activation(out=out_ap, in_=ss_ap, func=AF.Sqrt, scale=inv_n, bias=self.epsc), r=[rn, "epsc"], w=[wn])
        P.add("dve", lambda e: e.reciprocal(out=out_ap, in_=out_ap), r=[wn], w=[wn])

    def prenorm_T(self, l, which, tiles, dst, dst_names, normalize=True):
        P, H = self.P, self.H
        for k, i in enumerate(tiles):
            kk = k % 2
            hb, hbn = self.hb[kk], "hb%d" % kk
            ssn, rsn = "ss%d" % kk, "rs%d" % kk
            ss = self.ss[:, kk:kk + 1]
            rs = self.rs[:, kk:kk + 1]
            hn = "H%d" % i
            if normalize:
                P.add("act", lambda e, i=i, ss=ss, hb=hb: e.activation(out=hb, in_=H[:, i, :], func=AF.Square, accum_out=ss),
                      r=[hn], w=[hbn, ssn])
                self.rstd(ss, rs, 1.0 / D, ssn, rsn)
                P.add("dve", lambda e, i=i, rs=rs, hb=hb: e.tensor_scalar(out=hb, in0=H[:, i, :], scalar1=rs, scalar2=None, op0=ALU.mult),
                      r=[hn, rsn], w=[hbn])
            else:
                P.add("act", lambda e, i=i, hb=hb: e.activation(out=hb, in_=H[:, i, :], func=AF.Copy), r=[hn], w=[hbn])
            b = 6 + kk
            pb = self.bank_bf(b)

            def tr(e, hb=hb, pb=pb):
                for c in range(8):
                    ins = e.transpose(pb[:, c * 128:(c + 1) * 128], hb[:, c * 128:(c + 1) * 128], self.ident)
                return ins
            P.add("pe", tr, r=[hbn, "ident"], w=["ps%d" % b])
            d = dst[:, :, k * 128:(k + 1) * 128]
            src = pb.rearrange("p (c t) -> p c t", c=8)
            dn = [dst_names[k]] if isinstance(dst_names, list) else [dst_names]
            if normalize:
                g = self.gpre[:, (l * 2 + which) * 8:(l * 2 + which) * 8 + 8]
                gb = g.unsqueeze(2).broadcast_to([128, 8, 128])
                P.add("dve", lambda e, d=d, src=src, gb=gb: e.tensor_tensor(out=d, in0=src, in1=gb, op=ALU.mult),
                      r=["ps%d" % b, "gpre"], w=dn)
            else:
                P.add("dve", lambda e, d=d, src=src: e.tensor_copy(out=d, in_=src), r=["ps%d" % b], w=dn)

    def load_gpost(self, l, which):
        P, dr = self.P, self.dr
        row = 2 * l + which
        P.add("sp", lambda e: e.dma_start(out=self.gpost.unsqueeze(1), in_=dr["gpost"][row:row + 1, :].partition_broadcast(128)),
              w=["gpost"], dma=True)

    def postnorm_res(self, q, i):
        P = self.P
        y = self.pair(q)
        pn = ["ps%d" % (2 * q), "ps%d" % (2 * q + 1)]
        k = i % 2
        ss = self.ss[:, 2 + k:3 + k]
        rs = self.rs[:, 2 + k:3 + k]
        ssn, rsn = "pss%d" % k, "prs%d" % k
        tmp, tn = self.tmpf[k], "tmpf%d" % k
        hn = "H%d" % i
        P.add("act", lambda e: e.activation(out=tmp, in_=y, func=AF.Square, accum_out=ss), r=pn, w=[tn, ssn])
        self.rstd(ss, rs, 1.0 / D, ssn, rsn)
        P.add("dve", lambda e: e.scalar_tensor_tensor(out=tmp, in0=y, scalar=rs, in1=self.gpost, op0=ALU.mult, op1=ALU.mult),
              r=pn + [rsn, "gpost"], w=[tn])
        P.add("pool", lambda e: e.tensor_tensor(out=self.H[:, i, :], in0=self.H[:, i, :], in1=tmp, op=ALU.add),
              r=[tn, hn], w=[hn])

    def layer(self, l):
        P = self.P
        if l == 0:
            self.epsc = self.st[:, 1:2]
            P.add("dve", lambda e: e.memset(self.epsc, EPS), w=["epsc"])
            self.onec = self.st[:, 2:3]
            P.add("dve", lambda e: e.memset(self.onec, 1.0), w=["onec"])
        m = self.types[l]
        j = sum(1 for t in self.types[:l] if t == m)
        if m in self.mixers:
            self.load_gpost(l, 0)
            if m == 0:
                self.mla(l, j)
            elif m == 1:
                self.dil(l, j)
            else:
                self.fox(l, j)
        self.load_gpost(l, 1)
        for blk in range(2):
            self.ffn_ple(l, blk)

    def ffn_ple(self, l, blk):
        P, dr = self.P, self.dr
        RX, RY, RW = self.RX, self.RY, self.RW
        TB = 1024
        tiles = list(range(blk * 8, blk * 8 + 8))
        RX.reset()
        RY.reset()
        hnT = RX.take("hnT", BF16, [8, TB])
        win = [RX.take("win%d" % k, BF16, [8, 2, 256]) for k in range(2)]
        actT = RY.take(["act%d" % f for f in range(NF)], BF16, [NF, TB])
        if blk == 0:
            RW.reset()
            self.wout = RW.take("wout", BF16, [NF, D])
        wout = self.wout
        w_in = dr["ffn_w_in"][l]
        NG = NF // 2

        def load_win(g):
            k = g % 2
            for s in range(2):
                src = w_in[:, s * DFF + g * 256: s * DFF + g * 256 + 256].rearrange("(c p) n -> p c n", p=128)
                self.wload(win[k][:, :, s, :], src, "win%d" % k)

        load_win(0)
        load_win(1)
        wo = dr["ffn_w_out"][l].rearrange("(f p) n -> p f n", p=128)
        if blk == 0:
            for f0 in range(0, NF, 11):
                self.wload(wout[:, f0:f0 + 11, :], wo[:, f0:f0 + 11, :], "wout")
        self.prenorm_T(l, 1, tiles, hnT, "hnT")
        step = 0
        for g in range(NG):
            k = g % 2
            for ff in range(2):
                f = 2 * g + ff
                for tc in range(2):
                    bg, bu = (0, 1) if step % 2 == 0 else (2, 3)
                    step += 1
                    pg, pu = self.bank(bg), self.bank(bu)

                    def mm(e, k=k, ff=ff, tc=tc, pg=pg, pu=pu):
                        for s, po in ((0, pg), (1, pu)):
                            for c in range(8):
                                ins = e.matmul(po, win[k][:, c, s, ff * 128:(ff + 1) * 128], hnT[:, c, tc * 512:(tc + 1) * 512],
                                               start=(c == 0), stop=(c == 7))
                        return ins
                    P.add("pe", mm, r=["win%d" % k, "hnT"], w=["ps%d" % bg, "ps%d" % bu])
                    sg = self.hb[step % 2][:, 0:512]
                    sgn = "hb%d" % (step % 2)
                    P.add("act", lambda e, sg=sg, pg=pg: e.activation(out=sg, in_=pg, func=AF.Silu), r=["ps%d" % bg], w=[sgn])
                    P.add("dve", lambda e, sg=sg, pu=pu, f=f, tc=tc: e.tensor_tensor(out=actT[:, f, tc * 512:(tc + 1) * 512], in0=sg, in1=pu, op=ALU.mult),
                          r=[sgn, "ps%d" % bu], w=["act%d" % f])
            if g + 2 < NG:
                load_win(g + 2)
        for k, i in enumerate(tiles):
            q = k % 2
            y0, y1 = self.bank(2 * q), self.bank(2 * q + 1)

            def mm2(e, k=k, y0=y0, y1=y1):
                for hf, po in ((0, y0), (1, y1)):
                    for f in range(NF):
                        ins = e.matmul(po, actT[:, f, k * 128:(k + 1) * 128], wout[:, f, hf * 512:(hf + 1) * 512],
                                       start=(f == 0), stop=(f == NF - 1))
                return ins
            P.add("pe", mm2, r=["act%d" % f for f in range(NF)] + ["wout"], w=["ps%d" % (2 * q), "ps%d" % (2 * q + 1)])
            self.postnorm_res(q, i)
        RX.reset()
        RY.reset()
        hT = RX.take("hT", BF16, [8, TB])
        pT = RX.take("pTb", BF16, [2, TB])
        wp = RX.take("wproj", BF16, [2, D])
        wg = RY.take("wgate", BF16, [8, D])
        self.wload(pT, dr["pT"][l][:, blk * TB:(blk + 1) * TB].rearrange("(c p) t -> p c t", p=128), "pTb")
        self.wload(wp, dr["ple_w_proj"][l].rearrange("(c p) n -> p c n", p=128), "wproj")
        self.wload(wg, dr["ple_w_gate"][l].rearrange("(c p) n -> p c n", p=128), "wgate")
        self.prenorm_T(l, 0, tiles, hT, "hT", normalize=False)
        for k, i in enumerate(tiles):
            qg, qp = (0, 1) if k % 2 == 0 else (2, 1)
            qg = 0 if k % 2 == 0 else 2
            qp = 1
            gps, pps = self.pair(qg), self.pair(qp)

            def mmg(e, k=k, qg=qg):
                for hf in range(2):
                    po = self.bank(2 * qg + hf)
                    for c in range(8):
                        ins = e.matmul(po, hT[:, c, k * 128:(k + 1) * 128], wg[:, c, hf * 512:(hf + 1) * 512], start=(c == 0), stop=(c == 7))
                return ins

            def mmp(e, k=k, qp=qp):
                for hf in range(2):
                    po = self.bank(2 * qp + hf)
                    for c in range(2):
                        ins = e.matmul(po, pT[:, c, k * 128:(k + 1) * 128], wp[:, c, hf * 512:(hf + 1) * 512], start=(c == 0), stop=(c == 1))
                return ins
            P.add("pe", mmg, r=["hT", "wgate"], w=["ps%d" % (2 * qg), "ps%d" % (2 * qg + 1)])
            P.add("pe", mmp, r=["pTb", "wproj"], w=["ps%d" % (2 * qp), "ps%d" % (2 * qp + 1)])
            tmp, tn = self.tmpf[k % 2], "tmpf%d" % (k % 2)
            hn = "H%d" % i
            P.add("act", lambda e, tmp=tmp, gps=gps: e.activation(out=tmp, in_=gps, func=AF.Sigmoid),
                  r=["ps%d" % (2 * qg), "ps%d" % (2 * qg + 1)], w=[tn])
            P.add("dve", lambda e, tmp=tmp, pps=pps: e.tensor_tensor(out=tmp, in0=tmp, in1=pps, op=ALU.mult),
                  r=[tn, "ps%d" % (2 * qp), "ps%d" % (2 * qp + 1)], w=[tn])
            P.add("pool", lambda e, tmp=tmp, i=i: e.tensor_tensor(out=self.H[:, i, :], in0=self.H[:, i, :], in1=tmp, op=ALU.add),
                  r=[tn, hn], w=[hn])

    def qk_scratch(self, R, dkp):
        self.Qtok = [R.take("qtok%d" % k, BF16, [2, dkp]) for k in range(2)]
        self.Ktok = [R.take("ktok%d" % k, BF16, [2, dkp]) for k in range(2)]

    def qk_transpose(self, i, dk, qT, kT, last):
        P = self.P
        k2 = i % 2
        b = 6 + ((i // 2) % 2)
        pb = self.bank_bf(b).rearrange("p (s t) -> p s t", s=4)
        Q, K = self.Qtok[k2], self.Ktok[k2]

        def tr(e):
            for s, src in enumerate((Q[:, 0, 0:dk], Q[:, 1, 0:dk], K[:, 0, 0:dk], K[:, 1, 0:dk])):
                ins = e.transpose(pb[0:dk, s, (i % 2) * 128:(i % 2) * 128 + 128], src, self.ident)
            return ins
        P.add("pe", tr, r=["qtok%d" % k2, "ktok%d" % k2, "ident"], w=["ps%d" % b])
        if i % 2 == 1:
            i0 = i - 1
            for s, dst, nm in ((0, qT[0:dk, 0, i0 * 128:i0 * 128 + 256], "qT0"), (1, qT[0:dk, 1, i0 * 128:i0 * 128 + 256], "qT1"),
                               (2, kT[0:dk, 0, i0 * 128:i0 * 128 + 256], "kT0"), (3, kT[0:dk, 1, i0 * 128:i0 * 128 + 256], "kT1")):
                P.add("dve", lambda e, dst=dst, s=s, pb=pb: e.tensor_copy(out=dst, in_=pb[0:dk, s, :]), r=["ps%d" % b], w=[nm])

    def attention(self, hg, qT, kT, V, dk, scale, bias_fn, bias_names, Otok):
        for hh in range(2):
            for qc in range(4):
                self.attn_chunk(hg * 2 + hh, hh, qc, qT, kT, V, dk, scale, bias_fn, bias_names, Otok)

    def attn_chunk(self, h, hh, qc, qT, kT, V, dk, scale, bias_fn, bias_names, Otok):
        P = self.P
        accb = 2 + (self.cctr % 2)
        rcs = self.cctr % 2
        self.cctr += 1
        acc = self.bank(accb)
        first = [True]

        PTs = [(ap, "pt%d" % k) for k, ap in enumerate(self.PT[:2])] + [(self.hb[0][:, 0:512], "hb0"), (self.hb[1][:, 0:512], "hb1")]
        SB = (0, 1, 4)

        def do_S(j, off, ncols):
            sb = SB[self.sctr % 3]
            pt = self.sctr % 4
            self.sctr += 1
            S = self.bank(sb)
            PTb, ptn = PTs[pt]
            P.add("pe", lambda e: e.matmul(S[:, 0:ncols], kT[0:dk, hh, j * 128:(j + 1) * 128],
                                           qT[0:dk, hh, qc * 512 + off:(qc + 1) * 512], start=True, stop=True),
                  r=["kT%d" % hh, "qT%d" % hh], w=["ps%d" % sb])
            if j >= 4 * qc:
                P.add("dve", lambda e: e.tensor_tensor(out=S[:, 0:128], in0=S[:, 0:128], in1=self.negm, op=ALU.add),
                      r=["negm"], w=["ps%d" % sb])
            bias = bias_fn(j, h) if bias_fn is not None else None
            if bias is None:
                P.add("act", lambda e: e.activation(out=PTb[:, 0:ncols], in_=S[:, 0:ncols], func=AF.Exp, scale=scale),
                      r=["ps%d" % sb], w=[ptn])
            else:
                P.add("act", lambda e: e.activation(out=PTb[:, 0:ncols], in_=S[:, 0:ncols], func=AF.Exp, scale=scale, bias=bias),
                      r=["ps%d" % sb] + bias_names, w=[ptn])
            return pt

        def do_PV(j, off, ncols, pt):
            PTb, ptn = PTs[pt]

            def mm(e):
                for ql in range(off // 128, 4):
                    st = first[0]
                    first[0] = False
                    ins = e.matmul(acc[:, ql * 128:ql * 128 + 65], PTb[:, ql * 128 - off:ql * 128 - off + 128],
                                   V[:, j, hh, 0:65], start=st, stop=(j == 4 * qc + ql), skip_group_check=True)
                return ins
            P.add("pe", mm, r=[ptn, "V"], w=["ps%d" % accb])

        pending = []
        for j in range(4 * qc + 4):
            off = max(0, j - 4 * qc) * 128
            pt = do_S(j, off, 512 - off)
            pending.append((j, off, 512 - off, pt))
            if len(pending) > 2:
                do_PV(*pending.pop(0))
        while pending:
            do_PV(*pending.pop(0))
        rc = self.st[:, 4 + 4 * rcs:8 + 4 * rcs]
        rcn = "rc%d" % rcs
        accv = acc.rearrange("p (q c) -> p q c", q=4)
        P.add("dve", lambda e: e.reciprocal(out=rc, in_=accv[:, :, 64]), r=["ps%d" % accb], w=[rcn])
        P.add("dve", lambda e: e.tensor_tensor(out=Otok[:, 4 * qc:4 * qc + 4, h * 64:(h + 1) * 64], in0=accv[:, :, 0:64],
                                               in1=rc.unsqueeze(2).broadcast_to([128, 4, 64]), op=ALU.mult),
              r=["ps%d" % accb, rcn], w=["Otok"])

    def out_proj(self, Otok, wo):
        P = self.P
        for i in range(NT):
            kk = i % 2
            b = 6 + kk
            pb = self.bank_bf(b)
            oT, oTn = self.hb[kk], "hb%d" % kk

            def tr(e, i=i, pb=pb):
                for c in range(8):
                    ins = e.transpose(pb[:, c * 128:(c + 1) * 128], Otok[:, i, c * 128:(c + 1) * 128], self.ident)
                return ins
            P.add("pe", tr, r=["Otok", "ident"], w=["ps%d" % b])
            P.add("dve", lambda e, oT=oT, pb=pb: e.tensor_copy(out=oT, in_=pb), r=["ps%d" % b], w=[oTn])
            q = kk

            def mm(e, oT=oT, q=q):
                for hf in range(2):
                    po = self.bank(2 * q + hf)
                    for c in range(8):
                        ins = e.matmul(po, oT[:, c * 128:(c + 1) * 128], wo[:, c, hf * 512:(hf + 1) * 512], start=(c == 0), stop=(c == 7))
                return ins
            P.add("pe", mm, r=[oTn, "wo"], w=["ps%d" % (2 * q), "ps%d" % (2 * q + 1)])
            self.postnorm_res(q, i)

    def mla(self, l, j):
        P, dr = self.P, self.dr
        RX, RY, RW = self.RX, self.RY, self.RW
        RX.reset(); RY.reset(); RW.reset()
        hnT = RX.take(["hnT%d" % i for i in range(NT)], BF16, [8, T])
        wa = RW.take("wa", BF16, [8, 672])
        wuq = RW.take("wuq", BF16, [3, 1536])
        wukv = RW.take("wukv", BF16, [2, 2048])
        wo = RW.take("wo", BF16, [8, D])
        cqT = RY.take("cqT", BF16, [3, T])
        ckvT = RY.take("ckvT", BF16, [2, T])
        qT = RY.take(["qT0", "qT1"], BF16, [2, T])
        kT = RY.take(["kT0", "kT1"], BF16, [2, T])
        V = RY.take("V", BF16, [NT, 2, 66])
        self.qk_scratch(RY, 96)
        self.PT = [RY.take("pt%d" % k, BF16, [512]) for k in range(2)]
        self.wload(wa, dr["mla_w_a"][j].rearrange("(c p) n -> p c n", p=128), "wa")
        self.wload(wuq, dr["mla_w_uq"][j].rearrange("(c p) n -> p c n", p=128), "wuq")
        self.wload(wukv, dr["mla_w_ukv"][j].rearrange("(c p) n -> p c n", p=128), "wukv")
        self.wload(wo, dr["mla_w_o"][j].rearrange("(c p) n -> p c n", p=128), "wo")
        self.prenorm_T(l, 0, list(range(NT)), hnT, ["hnT%d" % i for i in range(NT)])
        P.add("dve", lambda e: e.memset(V[:, :, :, 64:66], 1.0), w=["V"])
        for i in range(NT):
            q = i % 2
            pa, pbk = self.bank(2 * q), self.bank(2 * q + 1)
            pn = ["ps%d" % (2 * q), "ps%d" % (2 * q + 1)]

            def mm(e, i=i, pa=pa, pbk=pbk):
                for c in range(8):
                    e.matmul(pa[:, 0:384], hnT[:, c, i * 128:(i + 1) * 128], wa[:, c, 0:384], start=(c == 0), stop=(c == 7))
                for c in range(8):
                    ins = e.matmul(pbk[:, 0:288], hnT[:, c, i * 128:(i + 1) * 128], wa[:, c, 384:672], start=(c == 0), stop=(c == 7))
                return ins
            P.add("pe", mm, r=["hnT%d" % i, "wa"], w=pn)
            kk = i % 2
            hb, hbn = self.hb[kk], "hb%d" % kk
            ssq, sskv = self.ss[:, kk:kk + 1], self.rs[:, kk:kk + 1]
            rq, rkv = self.st[:, 12 + 2 * kk:13 + 2 * kk], self.st[:, 13 + 2 * kk:14 + 2 * kk]
            n_ssq, n_sskv, n_rq, n_rkv = "ss%d" % kk, "rs%d" % kk, "mrq%d" % kk, "mrkv%d" % kk
            P.add("act", lambda e, hb=hb, pa=pa, ssq=ssq: e.activation(out=hb[:, 0:384], in_=pa[:, 0:384], func=AF.Square, accum_out=ssq),
                  r=pn, w=[hbn, n_ssq])
            P.add("act", lambda e, hb=hb, pbk=pbk, sskv=sskv: e.activation(out=hb[:, 384:640], in_=pbk[:, 0:256], func=AF.Square, accum_out=sskv),
                  r=pn, w=[hbn, n_sskv])
            self.rstd(ssq, rq, 1.0 / 384, n_ssq, n_rq)
            self.rstd(sskv, rkv, 1.0 / 256, n_sskv, n_rkv)
            P.add("dve", lambda e, hb=hb, pa=pa, rq=rq: e.tensor_scalar(out=hb[:, 0:384], in0=pa[:, 0:384], scalar1=rq, scalar2=None, op0=ALU.mult),
                  r=pn + [n_rq], w=[hbn])
            P.add("dve", lambda e, hb=hb, pbk=pbk, rkv=rkv: e.tensor_scalar(out=hb[:, 384:640], in0=pbk[:, 0:256], scalar1=rkv, scalar2=None, op0=ALU.mult),
                  r=pn + [n_rkv], w=[hbn])
            ta = self.ropes[0][:, 0, :]
            tb = self.ropes[1][:, 0, :]
            P.add("dve", lambda e, ta=ta, pbk=pbk, i=i: e.tensor_tensor(out=ta, in0=pbk[:, 256:288], in1=self.cos2[:, i, :], op=ALU.mult),
                  r=pn + ["cos2"], w=["ropea"])
            P.add("dve", lambda e, tb=tb, pbk=pbk, i=i: e.tensor_tensor(out=tb[:, 0:16], in0=pbk[:, 272:288], in1=self.sin2[:, i, 0:16], op=ALU.mult),
                  r=pn + ["sin2"], w=["ropeb"])
            P.add("dve", lambda e, tb=tb, pbk=pbk, i=i: e.tensor_tensor(out=tb[:, 16:32], in0=pbk[:, 256:272], in1=self.sin2[:, i, 16:32], op=ALU.mult),
                  r=pn + ["sin2"], w=["ropeb"])
            P.add("dve", lambda e, ta=ta, tb=tb, i=i: e.tensor_tensor(out=self.krope[:, i, :], in0=ta, in1=tb, op=ALU.add),
                  r=["ropea", "ropeb"], w=["krope"])
            b = 6 + kk
            pt = self.bank_bf(b)

            def tr(e, hb=hb, pt=pt):
                for c in range(5):
                    ins = e.transpose(pt[:, c * 128:(c + 1) * 128], hb[:, c * 128:(c + 1) * 128], self.ident)
                return ins
            P.add("pe", tr, r=[hbn, "ident"], w=["ps%d" % b])
            ptv = pt.rearrange("p (c t) -> p c t", c=8)
            gq = self.qn[:, 3 * j:3 * j + 3].unsqueeze(2).broadcast_to([128, 3, 128])
            gkv = self.kvn[:, 2 * j:2 * j + 2].unsqueeze(2).broadcast_to([128, 2, 128])
            P.add("dve", lambda e, ptv=ptv, gq=gq, i=i: e.tensor_tensor(out=cqT[:, :, i * 128:(i + 1) * 128], in0=ptv[:, 0:3, :], in1=gq, op=ALU.mult),
                  r=["ps%d" % b, "qn"], w=["cqT"])
            P.add("dve", lambda e, ptv=ptv, gkv=gkv, i=i: e.tensor_tensor(out=ckvT[:, :, i * 128:(i + 1) * 128], in0=ptv[:, 3:5, :], in1=gkv, op=ALU.mult),
                  r=["ps%d" % b, "kvn"], w=["ckvT"])
        if self.stop == 1:
            return
        RX.reset()
        Otok = RX.take("Otok", BF16, [NT, D])
        scale = 96.0 ** -0.5
        for hg in range(8):
            for i in range(NT):
                b = 4 + (i % 2)
                b2 = i % 2
                ps, ps2 = self.bank(b), self.bank(b2)
                psn, psn2 = ["ps%d" % b], ["ps%d" % b2]

                def mm(e, i=i, ps=ps, ps2=ps2, hg=hg):
                    for c in range(3):
                        e.matmul(ps[:, 0:192], cqT[:, c, i * 128:(i + 1) * 128], wuq[:, c, hg * 192:(hg + 1) * 192], start=(c == 0), stop=(c == 2))
                    for c in range(2):
                        ins = e.matmul(ps2[:, 0:256], ckvT[:, c, i * 128:(i + 1) * 128], wukv[:, c, hg * 256:(hg + 1) * 256], start=(c == 0), stop=(c == 1))
                    return ins
                P.add("pe", mm, r=["cqT", "ckvT", "wuq", "wukv"], w=psn + psn2)
                k2 = i % 2
                Q, K = self.Qtok[k2], self.Ktok[k2]
                qn_, kn_ = "qtok%d" % k2, "ktok%d" % k2
                qv = ps[:, 0:192].rearrange("p (h c) -> p h c", h=2)
                kvv = ps2[:, 0:256].rearrange("p (h c) -> p h c", h=2)
                ta, tb = self.ropes[0], self.ropes[1]
                cosb = self.cos2[:, i, :].unsqueeze(1).broadcast_to([128, 2, 32])
                sina = self.sin2[:, i, 0:16].unsqueeze(1).broadcast_to([128, 2, 16])
                sinb = self.sin2[:, i, 16:32].unsqueeze(1).broadcast_to([128, 2, 16])
                P.add("dve", lambda e, Q=Q, qv=qv: e.tensor_copy(out=Q[:, :, 0:64], in_=qv[:, :, 0:64]), r=psn, w=[qn_])
                P.add("dve", lambda e, ta=ta, qv=qv, cosb=cosb: e.tensor_tensor(out=ta, in0=qv[:, :, 64:96], in1=cosb, op=ALU.mult),
                      r=psn + ["cos2"], w=["ropea"])
                P.add("dve", lambda e, tb=tb, qv=qv, sina=sina: e.tensor_tensor(out=tb[:, :, 0:16], in0=qv[:, :, 80:96], in1=sina, op=ALU.mult),
                      r=psn + ["sin2"], w=["ropeb"])
                P.add("dve", lambda e, tb=tb, qv=qv, sinb=sinb: e.tensor_tensor(out=tb[:, :, 16:32], in0=qv[:, :, 64:80], in1=sinb, op=ALU.mult),
                      r=psn + ["sin2"], w=["ropeb"])
                P.add("dve", lambda e, Q=Q, ta=ta, tb=tb: e.tensor_tensor(out=Q[:, :, 64:96], in0=ta, in1=tb, op=ALU.add),
                      r=["ropea", "ropeb"], w=[qn_])
                P.add("act", lambda e, K=K, kvv=kvv: e.activation(out=K[:, :, 0:64], in_=kvv[:, :, 0:64], func=AF.Copy), r=psn2, w=[kn_])
                P.add("pool", lambda e, K=K, i=i: e.tensor_copy(out=K[:, :, 64:96], in_=self.krope[:, i, :].unsqueeze(1).broadcast_to([128, 2, 32])),
                      r=["krope"], w=[kn_])
                P.add("act", lambda e, kvv=kvv, i=i: e.activation(out=V[:, i, :, 0:64], in_=kvv[:, :, 64:128], func=AF.Copy), r=psn2, w=["V"])
                self.qk_transpose(i, 96, qT, kT, i == NT - 1)
            if self.stop not in (2, 4):
                self.attention(hg, qT, kT, V, 96, scale, None, [], Otok)
            if self.stop == 4 and hg == 0:
                P.add("dve", lambda e: e.memset(Otok, 0.5), w=["Otok"])
        if self.stop == 3:
            dbg = self.nc.dram_tensor("dbg", [128, NT * D], BF16, kind="ExternalOutput").ap()
            P.add("sp", lambda e: e.dma_start(out=dbg, in_=Otok.rearrange("p a b -> p (a b)")), r=["Otok"], w=["dbgout"], dma=True)
        if self.stop in (2, 3):
            return
        self.out_proj(Otok, wo)

    def cls_ap(self, ap, d, c4):
        if d == 1:
            return ap[:, c4 * 512:(c4 + 1) * 512]
        if d == 4:
            return ap.rearrange("p (u r) -> p r u", r=4)[:, c4, :]
        return ap.rearrange("p (u r) -> p r u", r=16)[:, 4 * c4:4 * c4 + 4, :]

    def dil(self, l, j):
        P, dr = self.P, self.dr
        RX, RY, RW = self.RX, self.RY, self.RW
        RX.reset(); RY.reset(); RW.reset()
        hnT = RX.take(["hnT%d" % i for i in range(NT)], BF16, [8, T])
        oT = RY.take("oTall", BF16, [8, T])
        UT = RY.take("UTacc", F32, [T])
        UZ = [RY.take("uz%d" % k, BF16, [2, 4, 128]) for k in range(2)]
        wq = [RW.take("wq%d" % k, BF16, [8, 3, 128]) for k in range(2)]
        qkv = [RW.take(nm, BF16, [T]) for nm in ("dqT", "dkT", "dvT")]
        V = RW.take("V", BF16, [NT, 2, 66])
        strip = [RW.take("strip%d" % k, F32, [2, 256]) for k in range(2)]
        PT = [RW.take("pt%d" % k, BF16, [256]) for k in range(3)]
        tmpS = [RW.take("tmpS%d" % k, F32, [256]) for k in range(2)]
        ZT = RW.take("ZTacc", F32, [T])
        w = dr["dil_w_qkv"][j]
        hn_names = ["hnT%d" % i for i in range(NT)]

        combos = [(hp, g) for hp in range(8) for g in range(3)]

        def load_w(ci):
            hp, g = combos[ci]
            k = ci % 2
            for s_ in range(3):
                c0 = g * 3072 + s_ * 1024 + hp * 128
                self.wload(wq[k][:, :, s_, :], w[:, c0:c0 + 128].rearrange("(c p) n -> p c n", p=128), "wq%d" % k)

        def load_strip(ci):
            hp, g = combos[ci]
            k = ci % 2
            self.P.add("sp", lambda e, k=k, g=g, hp=hp: e.dma_start(out=strip[k], in_=dr["dil_strip"][g * 16 + 2 * hp:g * 16 + 2 * hp + 2].rearrange("h s t -> s h t")),
                       w=["strip%d" % k], dma=True)

        load_w(0)
        load_w(1)
        load_strip(0)
        load_strip(1)
        self.prenorm_T(l, 0, list(range(NT)), hnT, hn_names)
        P.add("dve", lambda e: e.memset(V[:, :, :, 64:66], 1.0), w=["V"])
        dils = (1, 4, 16)
        pctr = 0
        only = getattr(self, "only_g", None)
        for ci, (hp, g) in enumerate(combos):
            if only is not None and g != only:
                continue
            d = dils[g]
            n_blk = NT // d
            k = ci % 2
            for s_ in range(3):
                for c4 in range(4):
                    b = pctr % 4
                    pctr += 1
                    ps = self.bank(b)

                    def mm(e, s_=s_, c4=c4, ps=ps, k=k, d=d):
                        for c in range(8):
                            ins = e.matmul(ps, wq[k][:, c, s_, :], self.cls_ap(hnT[:, c, :], d, c4), start=(c == 0), stop=(c == 7))
                        return ins
                    P.add("pe", mm, r=hn_names + ["wq%d" % k], w=["ps%d" % b])
                    dst = qkv[s_][:, c4 * 512:(c4 + 1) * 512]
                    nm = ("dqT", "dkT", "dvT")[s_]
                    if s_ == 1:
                        P.add("dve", lambda e, dst=dst, ps=ps: e.tensor_copy(out=dst, in_=ps), r=["ps%d" % b], w=[nm])
                    else:
                        P.add("act", lambda e, dst=dst, ps=ps: e.activation(out=dst, in_=ps, func=AF.Copy), r=["ps%d" % b], w=[nm])
            for t4 in range(4):
                b = 6 + (t4 % 2)
                pb = self.bank_bf(b)

                def tr(e, t4=t4, pb=pb):
                    for tt in range(4):
                        kt = 4 * t4 + tt
                        ins = e.transpose(pb[:, tt * 128:(tt + 1) * 128], qkv[2][:, kt * 128:(kt + 1) * 128], self.ident)
                    return ins
                P.add("pe", tr, r=["dvT", "ident"], w=["ps%d" % b])
                P.add("dve", lambda e, t4=t4, pb=pb: e.tensor_copy(out=V[:, 4 * t4:4 * t4 + 4, :, 0:64],
                                                                  in_=pb[:, 0:512].rearrange("p (t h c) -> p t h c", t=4, h=2)),
                      r=["ps%d" % b], w=["V"])
            if ci + 2 < len(combos):
                load_w(ci + 2)
            if self.stop == 6:
                for nm_, ap_, dt_, shp in (("dbgV", V.rearrange("p a b c -> p (a b c)"), BF16, [128, NT * 2 * 66]), ("dbgq", qkv[0], BF16, [128, T]),
                                           ("dbgk", qkv[1], BF16, [128, T]), ("dbgh", hnT.rearrange("p a b -> p (a b)"), BF16, [128, 8 * T])):
                    dd = self.nc.dram_tensor(nm_, shp, dt_, kind="ExternalOutput").ap()
                    P.add("sp", lambda e, dd=dd, ap_=ap_: e.dma_start(out=dd, in_=ap_), r=["V", "dqT", "dkT"] + hn_names, w=[nm_], dma=True)
                return
            self.dil_attention(g, d, n_blk, k, hp, qkv, V, strip, PT, tmpS, UZ, UT, ZT)
            if ci + 2 < len(combos):
                load_strip(ci + 2)
            if (g == 2 or only is not None) and self.stop == 5:
                d1 = self.nc.dram_tensor("dbgU", [128, T], F32, kind="ExternalOutput").ap()
                d2 = self.nc.dram_tensor("dbgZ", [128, T], F32, kind="ExternalOutput").ap()
                P.add("sp", lambda e: e.dma_start(out=d1, in_=UT), r=["UTacc"], w=["dbg1"], dma=True)
                P.add("sp", lambda e: e.dma_start(out=d2, in_=ZT), r=["ZTacc"], w=["dbg2"], dma=True)
                return
            if g == 2:
                P.add("dve", lambda e: e.reciprocal(out=ZT, in_=ZT), r=["ZTacc"], w=["ZTacc"])
                P.add("dve", lambda e, hp=hp: e.tensor_tensor(out=oT[:, hp, :], in0=UT, in1=ZT, op=ALU.mult),
                      r=["UTacc", "ZTacc"], w=["oTall"])
        RW.reset()
        wo = RW.take("wo", BF16, [8, D])
        self.wload(wo, dr["dil_w_o"][j].rearrange("(c p) n -> p c n", p=128), "wo")
        for i in range(NT):
            q = i % 2

            def mm(e, i=i, q=q):
                for hf in range(2):
                    po = self.bank(2 * q + hf)
                    for c in range(8):
                        ins = e.matmul(po, oT[:, c, i * 128:(i + 1) * 128], wo[:, c, hf * 512:(hf + 1) * 512], start=(c == 0), stop=(c == 7))
                return ins
            P.add("pe", mm, r=["oTall", "wo"], w=["ps%d" % (2 * q), "ps%d" % (2 * q + 1)])
            self.postnorm_res(q, i)

    def dil_attention(self, g, d, n_blk, k, hp, qkv, V, strip, PT, tmpS, UZ, UT, ZT):
        qT, kT = qkv[0], qkv[1]
        gg = 0 if getattr(self, "only_g", None) is not None else g
        pending = []
        SKEW = 2

        def flush_one():
            st = pending.pop(0)
            self.dil_pv(st, V, PT)
            if st["kt"] % 4 == 3 and st["hh"] == 1:
                self.dil_chunk_out(gg, d, st["kt"] // 4, UZ, UT, ZT)

        for kt in range(NT):
            jb = kt % n_blk
            has_next = (jb + 1 < n_blk)
            ncols = 256 if has_next else 128
            for hh in range(2):
                st = self.dil_scores(kt, jb, has_next, ncols, hh, k, qT, kT, strip, PT, tmpS)
                pending.append(st)
                if len(pending) > SKEW:
                    flush_one()
        while pending:
            flush_one()

    def dil_scores(self, kt, jb, has_next, ncols, hh, k, qT, kT, strip, PT, tmpS):
        P = self.P
        sb = self.sctr % 2
        pt = self.sctr % 3
        ts = self.sctr % 2
        self.sctr += 1
        S = self.bank(sb)
        rows = slice(hh * 64, hh * 64 + 64)
        P.add("pe", lambda e: e.matmul(S[:, 0:ncols], kT[rows, kt * 128:(kt + 1) * 128], qT[rows, kt * 128:kt * 128 + ncols], start=True, stop=True),
              r=["dkT", "dqT"], w=["ps%d" % sb])
        tm = tmpS[ts]
        P.add("dve", lambda e: e.scalar_tensor_tensor(out=tm[:, 0:ncols], in0=S[:, 0:ncols], scalar=0.125, in1=strip[k][:, hh, 0:ncols],
                                                      op0=ALU.mult, op1=ALU.add),
              r=["ps%d" % sb, "strip%d" % k], w=["tmpS%d" % ts])
        PTb = PT[pt]
        P.add("act", lambda e: e.activation(out=PTb[:, 0:ncols], in_=tm[:, 0:ncols], func=AF.Exp), r=["tmpS%d" % ts], w=["pt%d" % pt])
        return dict(kt=kt, jb=jb, has_next=has_next, hh=hh, pt=pt)

    def dil_pv(self, st, V, PT):
        P = self.P
        kt, jb, hh, pt = st["kt"], st["jb"], st["hh"], st["pt"]
        PTb = PT[pt]

        def acc_ap(qt):
            bank = (2 if (qt // 4) % 2 == 0 else 4) + hh
            return self.bank(bank)[:, (qt % 4) * 128:(qt % 4) * 128 + 65], "ps%d" % bank

        a0, n0 = acc_ap(kt)
        P.add("pe", lambda e: e.matmul(a0, PTb[:, 0:128], V[:, kt, hh, 0:65], start=(jb == 0), stop=True), r=["pt%d" % pt, "V"], w=[n0])
        if st["has_next"]:
            a1, n1 = acc_ap(kt + 1)
            P.add("pe", lambda e: e.matmul(a1, PTb[:, 128:256], V[:, kt, hh, 0:65], start=True, stop=False), r=["pt%d" % pt, "V"], w=[n1])

    def dil_chunk_out(self, g, d, ch, UZ, UT, ZT):
        P = self.P
        uz = UZ[ch % 2]
        uzn = "uz%d" % (ch % 2)
        for hh in range(2):
            bank = (2 if ch % 2 == 0 else 4) + hh
            accv = self.bank(bank).rearrange("p (q c) -> p q c", q=4)
            P.add("dve", lambda e, accv=accv, hh=hh: e.tensor_copy(out=uz[:, 0, :, hh * 64:(hh + 1) * 64], in_=accv[:, :, 0:64]),
                  r=["ps%d" % bank], w=[uzn])
            P.add("dve", lambda e, accv=accv, hh=hh: e.tensor_copy(out=uz[:, 1, :, hh * 64:(hh + 1) * 64], in_=accv[:, :, 64:65].broadcast_to([128, 4, 64])),
                  r=["ps%d" % bank], w=[uzn])
        b = 6 + (ch % 2)
        pb = self.bank_bf(b)

        def tr(e):
            for a in range(2):
                for tt in range(4):
                    ins = e.transpose(pb[:, (a * 4 + tt) * 128:(a * 4 + tt + 1) * 128], uz[:, a, tt, :], self.ident)
            return ins
        P.add("pe", tr, r=[uzn, "ident"], w=["ps%d" % b])
        for a, acc_t, nm in ((0, UT, "UTacc"), (1, ZT, "ZTacc")):
            dst = self.cls_ap(acc_t, d, ch)
            src = pb[:, a * 512:(a + 1) * 512]
            if d == 16:
                src = src.rearrange("p (r u) -> p r u", r=4)
            if g == 0:
                P.add("dve", lambda e, dst=dst, src=src: e.tensor_copy(out=dst, in_=src), r=["ps%d" % b], w=[nm])
            else:
                P.add("dve", lambda e, dst=dst, src=src: e.tensor_tensor(out=dst, in0=dst, in1=src, op=ALU.add), r=["ps%d" % b, nm], w=[nm])

    def fox(self, l, j):
        P, dr = self.P, self.dr
        RX, RY, RW = self.RX, self.RY, self.RW
        RX.reset(); RY.reset(); RW.reset()
        hnT = RX.take(["hnT%d" % i for i in range(NT)], BF16, [8, T])
        Otok = RY.take("Otok", BF16, [NT, D])
        qT = RY.take(["qT0", "qT1"], BF16, [2, T])
        self.qk_scratch(RY, 68)
        wf = RW.take("wf", BF16, [8, 16])
        wq = [RW.take("wq%d" % k, BF16, [8, 3, 128]) for k in range(2)]
        wo = RW.take("wo", BF16, [8, D])
        kT = RW.take(["kT0", "kT1"], BF16, [2, T])
        V = RW.take("V", BF16, [NT, 2, 66])
        self.PT = [RW.take("pt%d" % k, BF16, [512]) for k in range(3)]
        w = dr["fox_w_qkvf"][j]

        def load_wq(hg):
            k = hg % 2
            for s in range(3):
                src = w[:, s * 1024 + hg * 128: s * 1024 + hg * 128 + 128].rearrange("(c p) n -> p c n", p=128)
                self.wload(wq[k][:, :, s, :], src, "wq%d" % k)

        self.wload(wf, w[:, 3072:3088].rearrange("(c p) n -> p c n", p=128), "wf")
        load_wq(0)
        load_wq(1)
        self.wload(wo, dr["fox_w_o"][j].rearrange("(c p) n -> p c n", p=128), "wo")
        self.prenorm_T(l, 0, list(range(NT)), hnT, ["hnT%d" % i for i in range(NT)])
        P.add("dve", lambda e: e.memset(V[:, :, :, 64:66], 1.0), w=["V"])
        for k in range(2):
            Kt = self.Ktok[k]
            P.add("dve", lambda e, Kt=Kt: e.memset(Kt[:, :, 64:68], 1.0), w=["ktok%d" % k])
        t0, t1 = self.tmpf[0], self.tmpf[1]
        LF = t0[:, 0:256]
        CS = t0[:, 256:512]
        C8 = t0[:, 512:768]
        R1 = t0[:, 768:1024]
        AUG = t1[:, 0:384].bitcast(BF16)[:, 0:768].rearrange("p (i h a) -> p i h a", i=NT, h=16)
        HB = t1[:, 384:512].bitcast(BF16)
        H32 = t1[:, 512:768]
        PRE = t1[:, 768:1024]
        LFv = LF.rearrange("p (i h) -> p i h", i=NT)
        for i in range(NT):
            b = 4 + (i % 2)
            ps = self.bank(b)

            def mm(e, i=i, ps=ps):
                for c in range(8):
                    ins = e.matmul(ps[:, 0:16], hnT[:, c, i * 128:(i + 1) * 128], wf[:, c, :], start=(c == 0), stop=(c == 7))
                return ins
            P.add("pe", mm, r=["hnT%d" % i, "wf"], w=["ps%d" % b])
            P.add("dve", lambda e, i=i, ps=ps: e.tensor_tensor(out=LFv[:, i, :], in0=ps[:, 0:16], in1=self.bf, op=ALU.add),
                  r=["ps%d" % b, "bf"], w=["tmpf0"])
        P.add("act", lambda e: e.activation(out=LF, in_=LF, func=AF.Exp, scale=-1.0), r=["tmpf0"], w=["tmpf0"])
        P.add("act", lambda e: e.activation(out=LF, in_=LF, func=AF.Ln, bias=self.onec), r=["tmpf0", "onec"], w=["tmpf0"])
        pw, pt_ = self.bank(4), self.bank(5)
        P.add("pe", lambda e: e.matmul(pw[:, 0:256], self.trif, LF, start=True, stop=True), r=["trif", "tmpf0"], w=["ps4"])
        P.add("pe", lambda e: e.matmul(pt_[:, 0:256], self.onesf, LF, start=True, stop=True), r=["onesf", "tmpf0"], w=["ps5"])
        PREv = PRE.rearrange("p (i h) -> p i h", i=NT)
        ptv = pt_[:, 0:256].rearrange("p (i h) -> p i h", i=NT)
        P.add("dve", lambda e: e.memset(PREv[:, 0, :], 0.0), w=["tmpf1"])
        for i in range(1, NT):
            P.add("dve", lambda e, i=i: e.tensor_tensor(out=PREv[:, i, :], in0=PREv[:, i - 1, :], in1=ptv[:, i - 1, :], op=ALU.add),
                  r=["tmpf1", "ps5"], w=["tmpf1"])
        P.add("dve", lambda e: e.tensor_tensor(out=CS, in0=pw[:, 0:256], in1=PRE, op=ALU.add), r=["ps4", "tmpf1"], w=["tmpf0"])
        P.add("dve", lambda e: e.tensor_scalar(out=C8, in0=CS, scalar1=-8.0, scalar2=None, op0=ALU.mult), r=["tmpf0"], w=["tmpf0"])
        AUGf = AUG.rearrange("p i h a -> p (i h) a")
        for a in range(3):
            src = C8 if a == 0 else R1
            P.add("dve", lambda e, a=a, src=src: e.tensor_copy(out=AUGf[:, :, a], in_=src), r=["tmpf0"], w=["tmpf1"])
            if a < 2:
                P.add("dve", lambda e, a=a: e.tensor_copy(out=H32, in_=AUGf[:, :, a]), r=["tmpf1"], w=["tmpf1"])
                P.add("dve", lambda e, src=src: e.tensor_tensor(out=R1, in0=src, in1=H32, op=ALU.subtract), r=["tmpf0", "tmpf1"], w=["tmpf0"])
        CSv = CS.rearrange("p (i h) -> p i h", i=NT)
        for hg in range(8):
            k = hg % 2
            for i in range(NT):
                b = 4 + (i % 2)
                b2 = i % 2
                ps, ps2 = self.bank(b), self.bank(b2)
                psn, psn2 = ["ps%d" % b], ["ps%d" % b2]

                def mm(e, i=i, ps=ps, ps2=ps2, k=k):
                    for c in range(8):
                        e.matmul(ps[:, 0:128], hnT[:, c, i * 128:(i + 1) * 128], wq[k][:, c, 0, :], start=(c == 0), stop=(c == 7))
                    for c in range(8):
                        ins = e.matmul(ps2[:, 0:256], hnT[:, c, i * 128:(i + 1) * 128], wq[k][:, c, 1:3, :], start=(c == 0), stop=(c == 7))
                    return ins
                P.add("pe", mm, r=["hnT%d" % i, "wq%d" % k], w=psn + psn2)
                k2 = i % 2
                Q, K = self.Qtok[k2], self.Ktok[k2]
                qn_, kn_ = "qtok%d" % k2, "ktok%d" % k2
                pq = ps[:, 0:128].rearrange("p (h c) -> p h c", h=2)
                pv = ps2[:, 0:256].rearrange("p (s h c) -> p s h c", s=2, h=2)
                P.add("dve", lambda e, Q=Q, pq=pq: e.tensor_copy(out=Q[:, :, 0:64], in_=pq), r=psn, w=[qn_])
                P.add("dve", lambda e, Q=Q, i=i, hg=hg: e.tensor_copy(out=Q[:, :, 64:67], in_=AUG[:, i, 2 * hg:2 * hg + 2, :]), r=["tmpf1"], w=[qn_])
                P.add("act", lambda e, K=K, pv=pv: e.activation(out=K[:, :, 0:64], in_=pv[:, 0, :, :], func=AF.Copy), r=psn2, w=[kn_])
                P.add("act", lambda e, pv=pv, i=i: e.activation(out=V[:, i, :, 0:64], in_=pv[:, 1, :, :], func=AF.Copy), r=psn2, w=["V"])
                self.qk_transpose(i, 67, qT, kT, i == NT - 1)
            if hg + 2 < 8:
                load_wq(hg + 2)
            self.attention(hg, qT, kT, V, 67, 0.125, lambda jj, h: CSv[:, jj, h:h + 1], ["tmpf0"], Otok)
        self.out_proj(Otok, wo)

def _t5_bucket_const(dist):
    max_exact = 16
    n = np.maximum(dist.astype(np.float32), np.float32(1.0))
    large = max_exact + (np.log(n / np.float32(max_exact)) / np.float32(math.log(2048 / max_exact))
                         * np.float32(32 - max_exact)).astype(np.int32)
    large = np.minimum(large, 31)
    return np.where(dist < max_exact, dist, large)


def host_inputs(inputs, b):
    f32 = np.float32
    m = {}
    m["x"] = np.ascontiguousarray(inputs["x"][b])
    m["pT"] = np.ascontiguousarray(np.transpose(inputs["p"][:, b], (0, 2, 1)))
    m["pos"] = np.ascontiguousarray(inputs["positions"][b].reshape(NT, 128).T.astype(np.int32))
    g = inputs["norm_g"]
    gpre = np.stack([g[:, 0], g[:, 2]], axis=1)
    m["gpre"] = np.ascontiguousarray(gpre.reshape(DEPTH * 2, 8, 128).transpose(2, 0, 1).reshape(128, DEPTH * 2 * 8))
    m["gpost"] = np.ascontiguousarray(np.stack([g[:, 1], g[:, 3]], axis=1).reshape(DEPTH * 2, D))
    for k in ("ffn_w_in", "ffn_w_out", "ple_w_proj", "ple_w_gate", "mla_w_a", "mla_w_uq", "mla_w_ukv", "mla_w_o",
              "dil_w_qkv", "dil_w_o", "fox_w_qkvf", "fox_b_f", "fox_w_o"):
        m[k] = inputs[k]
    m["mla_qn"] = np.ascontiguousarray(inputs["mla_q_norm"].reshape(2, 3, 128).transpose(2, 0, 1).reshape(128, 6))
    m["mla_kvn"] = np.ascontiguousarray(inputs["mla_kv_norm"].reshape(2, 2, 128).transpose(2, 0, 1).reshape(128, 4))
    sidx = np.arange(128)[:, None]
    tidx = np.arange(256)[None, :]
    rel = tidx - sidx
    valid = (rel >= 0) & (rel <= 128)
    strips = np.full((48, 128, 256), -30000.0, f32)
    rb = inputs["rel_bias"]
    for g_, d_ in enumerate((1, 4, 16)):
        bucket = _t5_bucket_const(np.clip(rel, 0, None) * d_)
        for h_ in range(16):
            col = g_ * 16 + h_
            strips[col] = np.where(valid, rb[bucket, col], f32(-30000.0))
    m["dil_strip"] = strips
    m["ident"] = np.eye(128).astype(ml_dtypes.bfloat16)
    tri = (np.arange(128)[:, None] <= np.arange(128)[None, :])
    m["trif"] = tri.astype(f32)
    inv = (np.float32(10000.0) ** (-np.arange(16, dtype=f32) / np.float32(16))).astype(f32)
    m["invf"] = np.ascontiguousarray(np.broadcast_to(inv[None, :], (128, 16))).astype(f32)
    return m


_NC_CACHE = {}


def get_nc(n_layers=DEPTH, mixers=(0, 1, 2)):
    key = (n_layers, tuple(mixers))
    if key not in _NC_CACHE:
        _NC_CACHE[key] = Builder(n_layers, mixers).build()
    return _NC_CACHE[key]


def kernel(**inputs):
    inputs = {k: np.asarray(v) for k, v in inputs.items()}
    nc = get_nc()
    in_maps = [host_inputs(inputs, b) for b in range(8)]
    res = run_bass_kernel_spmd(nc, in_maps, core_ids=list(range(8)))
    out = np.stack([np.asarray(r["out"]) for r in res.results], axis=0)
    return out.astype(np.float32)
```
